# Optimizing a Trainium2 kernel written in Bass

```python
import math
import jax, jax.numpy as jnp
from jax import lax
import numpy as np

D_MODEL = 1024
BATCH = 2
SEQ = 8192
DEPTH = 2

CHUNK = 64
N_META = 16

D_MIX = D_MODEL
HEAD_DIM = 64
LRU_WIDTH = D_MIX // 4
LRU_HEADS = LRU_WIDTH // HEAD_DIM
CONV_A = 4
LRU_C = 8.0
FOX_WIDTH = D_MIX // 2
FOX_HEADS = FOX_WIDTH // HEAD_DIM
Q_BLOCK = 128
RWKV_WIDTH = D_MIX - LRU_WIDTH - FOX_WIDTH
RWKV_HEADS = RWKV_WIDTH // HEAD_DIM
W_RANK = 64
A_RANK = 64
G_RANK = 128
V_RANK = 32
GN_EPS = 64e-5
A_COLS = 2 * LRU_WIDTH
B_COLS = 3 * FOX_WIDTH + FOX_HEADS
C_COLS = 3 * RWKV_WIDTH + W_RANK + A_RANK + G_RANK
D_IN = A_COLS + B_COLS + C_COLS
C_SPLITS = (RWKV_WIDTH, 2 * RWKV_WIDTH, 3 * RWKV_WIDTH, 3 * RWKV_WIDTH + W_RANK, 3 * RWKV_WIDTH + W_RANK + A_RANK)
D_FF = 2816
CONV_F = 3
EPS = 1e-6

kernel_name = "hybrid_lru_fox_rwkv7_stream_block"


def rms_norm(x, g):
    xf = x.astype(jnp.float32)
    y = xf * lax.rsqrt(jnp.mean(xf * xf, axis=-1, keepdims=True) + EPS)
    return (y * g.astype(jnp.float32)).astype(x.dtype)


def causal_dwconv(x, w):
    k = w.shape[0]
    return lax.conv_general_dilated(
        x, w[:, None, :].astype(x.dtype), window_strides=(1,), padding=[(k - 1, 0)],
        dimension_numbers=('NWC', 'WIO', 'NWC'), feature_group_count=x.shape[-1])


def token_shift(p):
    return jnp.pad(p, ((0, 0), (1, 0), (0, 0)))[:, :-1]


def rg_lru_group(xa, ya, conv_w, conv_b, ga_w, ga_b, gx_w, gx_b, lam):
    bsz, t, _ = xa.shape
    u = causal_dwconv(xa, conv_w) + conv_b
    uh = u.reshape(bsz, t, LRU_HEADS, HEAD_DIM)
    r = jax.nn.sigmoid(jnp.einsum('bthi,hij->bthj', uh, ga_w).reshape(bsz, t, LRU_WIDTH) + ga_b)
    i = jax.nn.sigmoid(jnp.einsum('bthi,hij->bthj', uh, gx_w).reshape(bsz, t, LRU_WIDTH) + gx_b)
    log_a = -LRU_C * r.astype(jnp.float32) * jax.nn.softplus(-lam.astype(jnp.float32))
    a = jnp.exp(log_a)
    b = jnp.sqrt(-jnp.expm1(2.0 * log_a)) * (i * u).astype(jnp.float32)

    def combine(left, right):
        a1, b1 = left
        a2, b2 = right
        return a1 * a2, a2 * b1 + b2

    _, h = lax.associative_scan(combine, (a, b), axis=1)
    return h.astype(xa.dtype) * jax.nn.gelu(ya, approximate=True)


def forgetting_attention_group(q, k, v, f_logit, f_bias):
    bsz, t, _ = q.shape
    n_blk = -(-t // Q_BLOCK)
    tp = n_blk * Q_BLOCK

    def heads(z):
        z = jnp.pad(z, ((0, 0), (0, tp - t), (0, 0)))
        return z.reshape(bsz, tp, FOX_HEADS, HEAD_DIM).transpose(0, 2, 1, 3)

    qh = heads(q) * (HEAD_DIM ** -0.5)
    kh = heads(k)
    vh = heads(v)
    log_f = jax.nn.log_sigmoid((f_logit + f_bias).astype(jnp.float32))
    log_f = jnp.pad(log_f, ((0, 0), (0, tp - t), (0, 0)))
    c = jnp.cumsum(log_f, axis=1).transpose(0, 2, 1)
    kpos = jnp.arange(tp)

    def block(i):
        start = i * Q_BLOCK
        qb = lax.dynamic_slice_in_dim(qh, start, Q_BLOCK, axis=2)
        cb = lax.dynamic_slice_in_dim(c, start, Q_BLOCK, axis=2)
        qpos = start + jnp.arange(Q_BLOCK)
        s = jnp.einsum('bhqd,bhkd->bhqk', qb, kh).astype(jnp.float32) + (cb[..., :, None] - c[..., None, :])
        s = jnp.where(kpos[None, :] <= qpos[:, None], s, -jnp.inf)
        p = jax.nn.softmax(s, axis=-1)
        return jnp.einsum('bhqk,bhkd->bhqd', p.astype(vh.dtype), vh)

    o = lax.map(block, jnp.arange(n_blk))
    o = o.transpose(1, 0, 3, 2, 4).reshape(bsz, tp, FOX_WIDTH)
    return o[:, :t]


def rwkv7_group(pc, mu, w0, w_up, a0, a_up, g_up, k_k, k_a, r_k, ln_w, ln_b, v_first, v_mix):
    f32 = jnp.float32
    bsz, t, _ = pc.shape
    pc = pc + (token_shift(pc) - pc) * mu
    r, k, v, wd, ad, gd = jnp.split(pc, C_SPLITS, axis=-1)
    v_own = v
    if v_mix is not None:
        v0, v_down, v_up = v_mix
        v = v + (v_first - v) * jax.nn.sigmoid(v0 + (v @ v_down) @ v_up)
    w_log = -jax.nn.softplus(-(w0 + jnp.tanh(wd) @ w_up).astype(f32)) - 0.5
    decay = jnp.exp(-jnp.exp(w_log))
    a = jax.nn.sigmoid(a0 + ad @ a_up)
    g = jax.nn.sigmoid(gd) @ g_up

    def heads(z):
        return z.astype(f32).reshape(bsz, t, RWKV_HEADS, HEAD_DIM)

    kk = heads(k * k_k)
    kk = kk / jnp.maximum(jnp.sqrt(jnp.sum(kk * kk, axis=-1, keepdims=True)), 1e-12)
    k = k * (1.0 + (a - 1.0) * k_a)
    rh, kh, vh, ah, wh = heads(r), heads(k), heads(v), heads(a), heads(decay)
    xs = tuple(z.transpose(1, 0, 2, 3) for z in (rh, wh, kh, vh, kk, ah))

    def step(S, inp):
        r_t, w_t, k_t, v_t, kk_t, a_t = inp
        s_kk = jnp.einsum('bhij,bhj->bhi', S, kk_t)
        S = S * w_t[:, :, None, :] - s_kk[..., None] * (kk_t * a_t)[:, :, None, :] + v_t[..., None] * k_t[:, :, None, :]
        return S, jnp.einsum('bhij,bhj->bhi', S, r_t)

    s0 = jnp.zeros((bsz, RWKV_HEADS, HEAD_DIM, HEAD_DIM), f32)
    _, o = lax.scan(step, s0, xs)
    o = o.transpose(1, 0, 2, 3)
    mean = jnp.mean(o, axis=-1, keepdims=True)
    var = jnp.mean(jnp.square(o - mean), axis=-1, keepdims=True)
    o = ((o - mean) * lax.rsqrt(var + GN_EPS)).reshape(bsz, t, RWKV_WIDTH) * ln_w + ln_b
    bonus = (jnp.sum(rh * kh * r_k, axis=-1, keepdims=True) * vh).reshape(bsz, t, RWKV_WIDTH)
    out = ((o + bonus) * g.astype(f32)).astype(pc.dtype)
    return out, v_own


def setup_inputs(seed: int = 0) -> dict:
    key = jax.random.key(seed)
    ks = iter(jax.random.split(key, 40))
    f32 = jnp.float32

    def nrm(shape, scale):
        return scale * jax.random.normal(next(ks), shape, f32)

    def unif(shape, lo, hi):
        return jax.random.uniform(next(ks), shape, f32, lo, hi)

    L = DEPTH
    Lv = DEPTH - 1
    x = nrm((BATCH, SEQ, D_MODEL), 1.0)
    meta_tokens = nrm((N_META, D_MODEL), 1.0)
    norm_mix_pre = 1.0 + nrm((L, D_MODEL), 0.05)
    norm_mix_post = 1.0 + nrm((L, D_MODEL), 0.05)
    norm_ffn_pre = 1.0 + nrm((L, D_MODEL), 0.05)
    norm_ffn_post = 1.0 + nrm((L, D_MODEL), 0.05)
    w_in = nrm((L, D_MODEL, D_IN), D_MODEL ** -0.5)
    w_out = nrm((L, D_MIX, D_MODEL), D_MIX ** -0.5)
    lru_conv_w = nrm((L, CONV_A, LRU_WIDTH), CONV_A ** -0.5)
    lru_conv_b = nrm((L, LRU_WIDTH), 0.02)
    lru_gate_a_w = nrm((L, LRU_HEADS, HEAD_DIM, HEAD_DIM), HEAD_DIM ** -0.5)
    lru_gate_a_b = nrm((L, LRU_WIDTH), 0.02)
    lru_gate_x_w = nrm((L, LRU_HEADS, HEAD_DIM, HEAD_DIM), HEAD_DIM ** -0.5)
    lru_gate_x_b = nrm((L, LRU_WIDTH), 0.02)
    a_init = unif((L, LRU_WIDTH), 0.9, 0.999) ** (1.0 / LRU_C)
    lru_lambda = jnp.log(a_init) - jnp.log1p(-a_init)
    fox_f_bias = unif((L, FOX_HEADS), 1.0, 4.0)
    rwkv_mu = unif((L, C_COLS), 0.0, 1.0)
    rwkv_w0 = unif((L, RWKV_WIDTH), -6.0, -1.0)
    rwkv_w_up = nrm((L, W_RANK, RWKV_WIDTH), 0.1 * W_RANK ** -0.5)
    rwkv_a0 = nrm((L, RWKV_WIDTH), 0.1)
    rwkv_a_up = nrm((L, A_RANK, RWKV_WIDTH), 0.5 * A_RANK ** -0.5)
    rwkv_g_up = nrm((L, G_RANK, RWKV_WIDTH), G_RANK ** -0.5)
    rwkv_v0 = 1.0 + nrm((Lv, RWKV_WIDTH), 0.1)
    rwkv_v_down = nrm((Lv, RWKV_WIDTH, V_RANK), RWKV_WIDTH ** -0.5)
    rwkv_v_up = nrm((Lv, V_RANK, RWKV_WIDTH), 0.5 * V_RANK ** -0.5)
    rwkv_k_k = 0.85 + nrm((L, RWKV_WIDTH), 0.05)
    rwkv_k_a = 1.0 + nrm((L, RWKV_WIDTH), 0.05)
    rwkv_r_k = nrm((L, RWKV_HEADS, HEAD_DIM), 0.1)
    rwkv_ln_w = 1.0 + nrm((L, RWKV_WIDTH), 0.05)
    rwkv_ln_b = nrm((L, RWKV_WIDTH), 0.02)
    ffn_up = nrm((L, D_MODEL, 2 * D_FF), D_MODEL ** -0.5)
    ffn_conv = nrm((L, CONV_F, 2 * D_FF), CONV_F ** -0.5)
    ffn_down = nrm((L, D_FF, D_MODEL), D_FF ** -0.5)
    return {
        'x': x, 'meta_tokens': meta_tokens,
        'norm_mix_pre': norm_mix_pre, 'norm_mix_post': norm_mix_post,
        'norm_ffn_pre': norm_ffn_pre, 'norm_ffn_post': norm_ffn_post,
        'w_in': w_in, 'w_out': w_out,
        'lru_conv_w': lru_conv_w, 'lru_conv_b': lru_conv_b,
        'lru_gate_a_w': lru_gate_a_w, 'lru_gate_a_b': lru_gate_a_b,
        'lru_gate_x_w': lru_gate_x_w, 'lru_gate_x_b': lru_gate_x_b,
        'lru_lambda': lru_lambda, 'fox_f_bias': fox_f_bias,
        'rwkv_mu': rwkv_mu, 'rwkv_w0': rwkv_w0, 'rwkv_w_up': rwkv_w_up,
        'rwkv_a0': rwkv_a0, 'rwkv_a_up': rwkv_a_up, 'rwkv_g_up': rwkv_g_up,
        'rwkv_v0': rwkv_v0, 'rwkv_v_down': rwkv_v_down, 'rwkv_v_up': rwkv_v_up,
        'rwkv_k_k': rwkv_k_k, 'rwkv_k_a': rwkv_k_a, 'rwkv_r_k': rwkv_r_k,
        'rwkv_ln_w': rwkv_ln_w, 'rwkv_ln_b': rwkv_ln_b,
        'ffn_up': ffn_up, 'ffn_conv': ffn_conv, 'ffn_down': ffn_down,
    }


def reference(x, meta_tokens, norm_mix_pre, norm_mix_post, norm_ffn_pre, norm_ffn_post,
              w_in, w_out, lru_conv_w, lru_conv_b, lru_gate_a_w, lru_gate_a_b,
              lru_gate_x_w, lru_gate_x_b, lru_lambda, fox_f_bias,
              rwkv_mu, rwkv_w0, rwkv_w_up, rwkv_a0, rwkv_a_up, rwkv_g_up,
              rwkv_v0, rwkv_v_down, rwkv_v_up, rwkv_k_k, rwkv_k_a, rwkv_r_k,
              rwkv_ln_w, rwkv_ln_b, ffn_up, ffn_conv, ffn_down):
    bsz = x.shape[0]
    meta = jnp.broadcast_to(meta_tokens.astype(x.dtype)[None], (bsz, N_META, D_MODEL))
    h = jnp.concatenate([meta, x], axis=1)
    v_first = None
    for l in range(DEPTH):
        p = rms_norm(h, norm_mix_pre[l]) @ w_in[l]
        pa = p[..., :A_COLS]
        pb = p[..., A_COLS:A_COLS + B_COLS]
        pc = p[..., A_COLS + B_COLS:]
        xa, ya = jnp.split(pa, 2, axis=-1)
        oa = rg_lru_group(xa, ya, lru_conv_w[l], lru_conv_b[l], lru_gate_a_w[l], lru_gate_a_b[l],
                          lru_gate_x_w[l], lru_gate_x_b[l], lru_lambda[l])
        ob = forgetting_attention_group(pb[..., :FOX_WIDTH], pb[..., FOX_WIDTH:2 * FOX_WIDTH],
                                        pb[..., 2 * FOX_WIDTH:3 * FOX_WIDTH], pb[..., 3 * FOX_WIDTH:],
                                        fox_f_bias[l])
        v_mix = None if l == 0 else (rwkv_v0[l - 1], rwkv_v_down[l - 1], rwkv_v_up[l - 1])
        oc, v_own = rwkv7_group(pc, rwkv_mu[l], rwkv_w0[l], rwkv_w_up[l], rwkv_a0[l], rwkv_a_up[l],
                                rwkv_g_up[l], rwkv_k_k[l], rwkv_k_a[l], rwkv_r_k[l],
                                rwkv_ln_w[l], rwkv_ln_b[l], v_first, v_mix)
        if l == 0:
            v_first = v_own
        mixed = jnp.concatenate([oa, ob, oc], axis=-1) @ w_out[l]
        h = h + rms_norm(mixed, norm_mix_post[l])
        f = causal_dwconv(rms_norm(h, norm_ffn_pre[l]) @ ffn_up[l], ffn_conv[l])
        gate, val = jnp.split(f, 2, axis=-1)
        f = (jax.nn.gelu(gate, approximate=True) * val) @ ffn_down[l]
        h = h + rms_norm(f, norm_ffn_post[l])
    return h[:, N_META:]
```

```python
import contextlib
import numpy as np
import ml_dtypes
import concourse.bass as bass
import concourse.mybir as mybir
from concourse.bass_utils import run_bass_kernel_spmd

F32 = mybir.dt.float32
BF16 = mybir.dt.bfloat16
AF = mybir.ActivationFunctionType
ALU = mybir.AluOpType

D = 1024
T_REAL = 8208
NTT = 17
TOK = NTT * 128
OWN = 2052
TP = 8320
DFF = 2816
EPS = 1e-6


class Sched:
    EPOCH = 10 ** 9

    def __init__(self, nc, es):
        self.nc = nc
        self.es = es
        self.eng = {'pe': nc.tensor, 'act': nc.scalar, 'dve': nc.vector, 'pool': nc.gpsimd, 'sp': nc.sync}
        self.sem = {}
        self.cnt = {}
        self.nsem = 0
        for e in ['pe', 'act', 'dve', 'pool']:
            self._newsem(e)
        self.dsem = {}
        self.dcnt = {}
        self.waited = {}
        self.lastw = {}
        self.readers = {}
        self.n_inst = 0
        self.n_wait = 0

    def _newsem(self, e):
        self.nsem += 1
        self.sem[e] = self.es.enter_context(self.nc.semaphore(f"s_{e}_{self.nsem}"))
        self.cnt[e] = 0

    def _wait(self, ec, tok):
        kind, sem, val, ep = tok
        if kind == 'e' and ep == 'pe' and ec == 'pe':
            return
        key = (ec, id(sem))
        if self.waited.get(key, 0) >= val:
            return
        self.waited[key] = val
        self.eng[ec].wait_ge(sem, val)
        self.n_wait += 1

    def _deps(self, ec, reads, writes):
        for k in reads:
            t = self.lastw.get(k)
            if t is not None:
                self._wait(ec, t)
        for k in writes:
            t = self.lastw.get(k)
            if t is not None:
                self._wait(ec, t)
            for t in self.readers.get(k, {}).values():
                self._wait(ec, t)

    def _update(self, reads, writes, tok):
        for k in writes:
            self.lastw[k] = tok
            self.readers[k] = {}
        for k in reads:
            if k in writes:
                continue
            self.readers.setdefault(k, {})[(tok[0], tok[3] if tok[0] == 'e' else id(tok[1]))] = tok

    def op(self, e, reads, writes, fn, sig=True):
        self._deps(e, reads, writes)
        inst = fn(self.eng[e])
        self.n_inst += 1
        if sig:
            if self.cnt[e] >= self.EPOCH:
                self._newsem(e)
            self.cnt[e] += 1
            inst.then_inc(self.sem[e], 1)
            tok = ('e', self.sem[e], self.cnt[e], e)
        else:
            tok = ('e', self.sem[e], self.cnt[e] + 1, e)
        self._update(reads, writes, tok)
        return tok

    def dma(self, q, out, in_, reads, writes, sem, **kw):
        self._deps(q, reads, writes)
        if sem not in self.dsem:
            self.dsem[sem] = self.es.enter_context(self.nc.semaphore(f"d_{sem}"))
            self.dcnt[sem] = 0
        inst = self.eng[q].dma_start(out=out, in_=in_, **kw)
        self.n_inst += 1
        self.dcnt[sem] += 16
        inst.then_inc(self.dsem[sem], 16)
        tok = ('d', self.dsem[sem], self.dcnt[sem], q)
        self._update(reads, writes, tok)
        return tok

    def barrier(self):
        toks = []
        for e in ['pe', 'act', 'dve', 'pool']:
            if self.cnt[e] > 0:
                toks.append(('e', self.sem[e], self.cnt[e], '__'))
        for s_, h in self.dsem.items():
            toks.append(('d', h, self.dcnt[s_], '__'))
        for ec in ['pe', 'act', 'dve', 'pool', 'sp']:
            for t in toks:
                self._wait(ec, t)
        self.lastw = {}
        self.readers = {}

    def collective(self, kind, ins, outs, groups, writes):
        self.ncc = getattr(self, 'ncc', 0) + 1
        sem = self.es.enter_context(self.nc.semaphore(f"cc{self.ncc}"))
        inst = self.nc.gpsimd.collective_compute(kind, ALU.bypass, replica_groups=groups, ins=ins, outs=outs)
        inst.then_inc(sem)
        self.n_inst += 1
        tok = ('d', sem, 1, 'pool')
        self.cctoks = getattr(self, 'cctoks', []) + [tok]
        return tok

    def wait_collectives(self):
        for ec in ['pe', 'act', 'dve', 'pool', 'sp']:
            for t in getattr(self, 'cctoks', []):
                self._wait(ec, t)

    def finish(self, q='sp'):
        for s, h in self.dsem.items():
            self.eng[q].wait_ge(h, self.dcnt[s])
        for e in ['pe', 'act', 'dve', 'pool']:
            if self.cnt[e] > 0:
                self.eng[q].wait_ge(self.sem[e], self.cnt[e])


class Ctx:
    def __init__(self, name):
        self.nc = bass.Bass("TRN2", target_bir_lowering=False)
        self.es = contextlib.ExitStack()
        self.S = Sched(self.nc, self.es)
        self.rr = 0
        self.pid = 0
        self.pes = self.es

    def din(self, name, shape, dt=F32):
        return self.nc.dram_tensor(name, list(shape), dt, kind="ExternalInput").ap()

    def dout(self, name, shape, dt=F32):
        return self.nc.dram_tensor(name, list(shape), dt, kind="ExternalOutput").ap()

    def sb(self, name, shape, dt=F32):
        return self.pes.enter_context(self.nc.sbuf_tensor(f'sb{self.pid}_' + name, list(shape), dt))

    def ps(self, name, shape, dt=F32):
        return self.pes.enter_context(self.nc.psum_tensor(f'ps{self.pid}_' + name, list(shape), dt))

    def dint(self, name, shape, dt=F32):
        return self.nc.dram_tensor(name, list(shape), dt)

    @contextlib.contextmanager
    def phase(self, name):
        self.pid += 1
        self.S.wait_collectives()
        self.pes = contextlib.ExitStack()
        for a in ('wstage', 'wsi'):
            if hasattr(self, a):
                delattr(self, a)
        try:
            yield
        finally:
            self.S.barrier()
            self.pes.close()
            self.pes = self.es

    def anyeng(self, choices=('dve', 'pool')):
        self.rr += 1
        return choices[self.rr % len(choices)]


def load_cast_weight(C, w_dram, nk, ncol, dst_bf, key, gcol=None, piece=1024, sem='wld'):
    S = C.S
    if not hasattr(C, 'wstage'):
        C.wstage = [C.sb(f"wstage{i}", [128, 1408], F32) for i in range(3)]
        C.wsi = 0
    wv = w_dram.rearrange("(c p) n -> p c n", p=128)
    for c in range(nk):
        for n0 in range(0, ncol, piece):
            n1 = min(ncol, n0 + piece)
            si = C.wsi % 3
            C.wsi += 1
            st = C.wstage[si]
            S.dma('sp', st[:, 0:n1 - n0], wv[:, c, n0:n1], [], [f'wst{si}'], f'wld{si}')
            e = C.anyeng(('dve', 'act'))
            rd = [f'wst{si}'] + ([] if gcol is None else ['gcols'])
            if gcol is None:
                if e == 'act':
                    S.op(e, rd, [key], lambda en, st=st, c=c, n0=n0, n1=n1: en.copy(out=dst_bf[:, c, n0:n1], in_=st[:, 0:n1 - n0]))
                else:
                    S.op(e, rd, [key], lambda en, st=st, c=c, n0=n0, n1=n1: en.tensor_copy(out=dst_bf[:, c, n0:n1], in_=st[:, 0:n1 - n0]))
            else:
                if e == 'act':
                    S.op(e, rd, [key], lambda en, st=st, c=c, n0=n0, n1=n1: en.mul(out=dst_bf[:, c, n0:n1], in_=st[:, 0:n1 - n0], mul=gcol[:, c:c + 1]))
                else:
                    S.op(e, rd, [key], lambda en, st=st, c=c, n0=n0, n1=n1: en.tensor_scalar(out=dst_bf[:, c, n0:n1], in0=st[:, 0:n1 - n0], scalar1=gcol[:, c:c + 1], scalar2=None, op0=ALU.mult))


def rows_to_cols(C, rows_sb, nrows, length, dst, ident, key_rows, key_dst, ps, ps_key):
    S = C.S
    nch = length // 128
    for c in range(nch):
        S.op('pe', [key_rows, 'ident'], [ps_key],
             lambda en, c=c: en.matmul(ps[:, c * nrows:(c + 1) * nrows], lhsT=rows_sb[0:nrows, c * 128:(c + 1) * 128], rhs=ident[0:nrows, 0:nrows], start=True, stop=True))
    S.op('dve', [ps_key], [key_dst], lambda en: en.tensor_copy(out=dst[:, 0:nch, 0:nrows], in_=ps[:, 0:nch * nrows].rearrange("p (c r) -> p c r", r=nrows)))


def emit_rstd(C, src_ap_list, ss, rs, keys_src, slot, tag):
    S = C.S
    junk = C.junk
    for j, (ap, k) in enumerate(zip(src_ap_list, keys_src)):
        n = ap.shape[-1]
        S.op('act', [k], [f'junk', f'ss{tag}{slot}_{j}'],
             lambda en, ap=ap, j=j, n=n: en.activation(out=junk[:, 0:n], in_=ap, func=AF.Square, accum_out=ss[:, j:j + 1]))
    if len(src_ap_list) == 2:
        S.op('dve', [f'ss{tag}{slot}_0', f'ss{tag}{slot}_1'], [f'ss{tag}{slot}_0'],
             lambda en: en.tensor_tensor(out=ss[:, 0:1], in0=ss[:, 0:1], in1=ss[:, 1:2], op=ALU.add))
    S.op('dve', [f'ss{tag}{slot}_0'], [f'rs{tag}{slot}'],
         lambda en: en.tensor_scalar(out=rs[:, 0:1], in0=ss[:, 0:1], scalar1=1.0 / D, scalar2=EPS, op0=ALU.mult, op1=ALU.add))
    S.op('act', [f'rs{tag}{slot}'], [f'rs{tag}{slot}'], lambda en: en.activation(out=rs[:, 0:1], in_=rs[:, 0:1], func=AF.Sqrt))
    S.op('dve', [f'rs{tag}{slot}'], [f'rs{tag}{slot}'], lambda en: en.reciprocal(out=rs[:, 0:1], in_=rs[:, 0:1]))


def emit_norm_xnT(C, h_ap, h_key, slot, xnT_dram, col0, ncols=128, mask0=False):
    S = C.S
    ss = C.ssn[slot]
    rs = C.rsn[slot]
    xnb = C.xnb[slot]
    tp = C.tp[slot]
    xst = C.xst[slot]
    emit_rstd(C, [h_ap], ss, rs, [h_key], slot, 'n')
    if mask0:
        S.op('dve', [f'rsn{slot}', 'rowmask'], [f'rsn{slot}'], lambda en: en.tensor_tensor(out=rs[:, 0:1], in0=rs[:, 0:1], in1=C.rowmask[:, 0:1], op=ALU.mult))
    S.op('dve', [h_key, f'rsn{slot}'], [f'xnb{slot}'],
         lambda en: en.tensor_scalar(out=xnb[:], in0=h_ap, scalar1=rs[:, 0:1], scalar2=None, op0=ALU.mult))
    for c in range(8):
        S.op('pe', [f'xnb{slot}', 'identb'], [f'tp{slot}'],
             lambda en, c=c: en.transpose(out=tp[:, c, :], in_=xnb[:, c * 128:(c + 1) * 128], identity=C.identb[:]), sig=(c == 7))
    S.op('act', [f'tp{slot}'], [f'xst{slot}'], lambda en: en.copy(out=xst[:], in_=tp[:]))
    xv = xnT_dram.rearrange("(c p) t -> p c t", p=128)
    S.dma('sp', xv[:, :, col0:col0 + ncols], xst[:, :, 0:ncols], [f'xst{slot}'], ['xn2'], f'xo{slot}')


def alloc_norm_bufs(C, io):
    C.junk = C.sb("junk", [128, 1024], BF16)
    C.ssn = [C.sb(f"ssn{i}", [128, 2], F32) for i in range(2)]
    C.rsn = [C.sb(f"rsn{i}", [128, 1], F32) for i in range(2)]
    C.xnb = [C.sb(f"xnb{i}", [128, 1024], BF16) for i in range(2)]
    C.tp = [C.ps(f"tp{i}", [128, 8, 128], BF16) for i in range(2)]
    C.xst = [C.sb(f"xst{i}", [128, 8, 128], BF16) for i in range(2)]
    C.identb = C.sb("identb", [128, 128], BF16)
    C.identf = C.sb("identf", [128, 128], F32)
    ident_d = io['ident']
    C.rowmask = C.sb("rowmask", [128, 1], F32)
    C.S.dma('sp', C.rowmask[:], io['rowmask'][:, :], [], ['rowmask'], 'su_rowmask')
    C.S.dma('sp', C.identf[:], ident_d[:, :], [], ['ident'], 'su_ident')
    C.S.op('dve', ['ident'], ['identb'], lambda en: en.tensor_copy(out=C.identb[:], in_=C.identf[:]))


def emit_tok(C, mode, io):
    S = C.S
    h_in = io['h_in']
    xnT_out = io.get('xnT_out')
    h_out = io.get('h_out')
    out_final = io.get('out_final')
    gpost_d = io.get('gpost')
    alloc_norm_bufs(C, io)
    ht = [C.sb(f"ht{i}", [128, D], F32) for i in range(2)]
    if mode != 'P':
        gpost = C.sb("gpost_bc", [128, D], F32)
        S.dma('sp', gpost[:], gpost_d.partition_broadcast(128), [], ['gpost'], 'su_gpost')
        tmpn = [C.sb("tmpn0", [128, D], F32)] * 2
        ssm = [C.sb(f"ssm{i}", [128, 2], F32) for i in range(2)]
        rsm = [C.sb(f"rsm{i}", [128, 1], F32) for i in range(2)]
        mpA = [C.ps(f"mpA{i}", [128, 512], F32) for i in range(1)]
        mpB = [C.ps(f"mpB{i}", [128, 512], F32) for i in range(1)]

    def post_norm_residual(slot, i, hkey):
        emit_rstd(C, [mpA[0][:], mpB[0][:]], ssm[slot], rsm[slot], ['mpA', 'mpB'], slot, 'm')
        for half, mp, k in ((0, mpA[0], 'mpA'), (1, mpB[0], 'mpB')):
            S.op('dve', [k, f'rsm{slot}', 'gpost'], [f'tmpn_{half}'],
                 lambda en, half=half, mp=mp: en.scalar_tensor_tensor(out=tmpn[slot][:, half * 512:(half + 1) * 512], in0=mp[:], scalar=rsm[slot][:, 0:1],
                                                                      in1=gpost[:, half * 512:(half + 1) * 512], op0=ALU.mult, op1=ALU.mult))
        S.op('pool', ['tmpn_0', 'tmpn_1', hkey], [hkey],
             lambda en: en.tensor_tensor(out=ht[slot][:], in0=ht[slot][:], in1=tmpn[slot][:], op=ALU.add))
        if h_out is not None:
            S.dma('sp', h_out[i * 128:(i + 1) * 128, :], ht[slot][:], [hkey], [f'hd{i}'], f'ho{slot}')
        if out_final is not None and i >= 1:
            S.dma('sp', out_final[(i - 1) * 128:i * 128, :], ht[slot][:], [hkey], [], f'ho{slot}')

    issued = set()

    def load_h(i):
        if i >= NTT or ('h', i) in issued:
            return
        issued.add(('h', i))
        S.dma('sp', ht[i % 2][:], h_in[i * 128:(i + 1) * 128, :], [f'hd{i}'], [f'ht{i % 2}'], f'hl{i % 2}')

    if mode == 'P':
        for i in range(NTT):
            slot = i % 2
            load_h(i)
            load_h(i + 1)
            emit_norm_xnT(C, ht[slot][:], f'ht{slot}', slot, xnT_out, i * 128, mask0=(i == 0))
    elif mode == 'T1':
        catT = io['cat_all']
        w_out = io['w_out']
        qoff = C.qoff
        wo_bf = C.sb("wo_bf", [128, 8, D], BF16)
        load_cast_weight(C, w_out, 8, D, wo_bf, 'wo')
        ct = [C.sb(f"ct{i}", [128, 8, 128], BF16) for i in range(2)]
        cv = catT.rearrange("(c p) t -> p c t", p=128)
        for i in range(NTT):
            slot = i % 2
            def load_c(j):
                if j >= NTT or ('c', j) in issued:
                    return
                issued.add(('c', j))
                S.dma('sp', ct[j % 2][:], catT[:, bass.ds(qoff + j * 128, 128)].rearrange("(c p) t -> p c t", p=128), ['cat_all'], [f'ct{j % 2}'], f'cl{j % 2}')
            load_h(i)
            load_c(i)
            load_h(i + 1)
            load_c(i + 1)
            for half, mp, k in ((0, mpA[0], 'mpA'), (1, mpB[0], 'mpB')):
                for c in range(8):
                    S.op('pe', [f'ct{slot}', 'wo'], [k],
                         lambda en, c=c, half=half, mp=mp: en.matmul(mp[:], lhsT=ct[slot][:, c, :], rhs=wo_bf[:, c, half * 512:(half + 1) * 512], start=(c == 0), stop=(c == 7)),
                         sig=(c == 7))
            post_norm_residual(slot, i, f'ht{slot}')
            emit_norm_xnT(C, ht[slot][:], f'ht{slot}', slot, xnT_out, i * 128, mask0=(i == 0))
    else:
        xnT_in = io['xnT_in']
        w_up = io['w_up']
        w_dn = io['w_dn']
        conv_d = io['conv']
        gpre_d = io['gpre']
        NCH = 2 * DFF // 128
        load_cast_weight
        C.wstage = [C.sb(f"wstage{i}", [128, 1408], F32) for i in range(3)]
        C.wsi = 0
        pcol = mpA[0]
        convc = C.sb("convc", [128, NCH, 3], F32)
        gcol = C.sb("gcol", [128, 8, 1], F32)
        for pi in range(4):
            S.dma('sp', C.wstage[0][0:3, :], conv_d[:, pi * 1408:(pi + 1) * 1408], [], ['wst0'], 'wld0')
            for c in range(11):
                S.op('pe', ['wst0', 'ident'], ['mpA'],
                     lambda en, c=c: en.matmul(pcol[:, c * 3:(c + 1) * 3], lhsT=C.wstage[0][0:3, c * 128:(c + 1) * 128], rhs=C.identf[0:3, 0:3], start=True, stop=True))
            S.op('dve', ['mpA'], ['convc'], lambda en, pi=pi: en.tensor_copy(out=convc[:, pi * 11:(pi + 1) * 11, :], in_=pcol[:, 0:33].rearrange("p (c r) -> p c r", r=3)))
        S.dma('sp', C.wstage[1][0:1, 0:D], gpre_d.rearrange("(o n) -> o n", o=1), [], ['wst1'], 'wld1')
        for c in range(8):
            S.op('pe', ['wst1', 'ident'], ['mpA'],
                 lambda en, c=c: en.matmul(pcol[:, c:c + 1], lhsT=C.wstage[1][0:1, c * 128:(c + 1) * 128], rhs=C.identf[0:1, 0:1], start=True, stop=True))
        S.op('dve', ['mpA'], ['gcols'], lambda en: en.tensor_copy(out=gcol[:, :, 0], in_=pcol[:, 0:8]))
        wup_bf = C.sb("wup_bf", [128, 8, 2 * DFF], BF16)
        wdn_bf = C.sb("wdn_bf", [128, DFF // 128, D], BF16)
        load_cast_weight(C, w_up, 8, 2 * DFF, wup_bf, 'wup', gcol=gcol[:].rearrange("p c o -> p (c o)"), piece=1408)
        load_cast_weight(C, w_dn, DFF // 128, D, wdn_bf, 'wdn')
        NB = 256
        xin = [C.sb(f"xin{i}", [128, 8, NB], BF16) for i in range(2)]
        xv = xnT_in.rearrange("(c p) t -> p c t", p=128)
        fps = [C.ps(f"fps{i}", [128, 512], F32) for i in range(4)]
        fraw = [C.sb(f"fraw{i}", [128, NB + 2], F32) for i in range(4)]
        fcv = [C.sb(f"fcv{i}", [128, NB], F32) for i in range(4)]
        gel = [C.sb(f"gel{i}", [128, NB], F32) for i in range(2)]
        halo = C.sb("halo", [128, NCH, 2], F32)
        ptmp = C.sb("ptmp", [128, NB], F32)
        S.op('pool', [], [f'halo{ch}' for ch in range(NCH)], lambda en: en.memset(halo[:], 0.0))
        hid = C.sb("hid", [128, DFF // 128, NB], BF16)
        blocks = [(b0, min(NB, TOK - b0)) for b0 in range(0, TOK, NB)]
        it = 0
        for bi, (b0, nb) in enumerate(blocks):
            xs = bi % 2
            def load_x(bj):
                if bj >= len(blocks) or ('x', bj) in issued:
                    return
                issued.add(('x', bj))
                b0_, nb_ = blocks[bj]
                S.dma('sp', xin[bj % 2][:, :, 0:nb_], xv[:, :, b0_:b0_ + nb_], ['xn2'], [f'xin{bj % 2}'], f'xl{bj % 2}')
            load_x(bi)
            load_x(bi + 1)
            for cc in range(DFF // 128):
                fc = []
                for wi, ch in enumerate((cc, DFF // 128 + cc)):
                    fi = (it % 2) * 2 + wi
                    pk = f'fps{fi}'
                    for c in range(8):
                        S.op('pe', [f'xin{xs}', 'wup'], [pk],
                             lambda en, c=c, ch=ch, fi=fi: en.matmul(fps[fi][:, 0:nb], lhsT=wup_bf[:, c, ch * 128:(ch + 1) * 128], rhs=xin[xs][:, c, 0:nb], start=(c == 0), stop=(c == 7)),
                             sig=(c == 7))
                    S.op('pool', [f'halo{ch}'], [f'fraw{fi}h'], lambda en, fi=fi, ch=ch: en.tensor_copy(out=fraw[fi][:, 0:2], in_=halo[:, ch, :]))
                    S.op('act', [pk], [f'fraw{fi}'], lambda en, fi=fi: en.copy(out=fraw[fi][:, 2:2 + nb], in_=fps[fi][:, 0:nb]))
                    if wi == 0:
                        S.op('act', [pk, 'convc'], [f'fcv{fi}'], lambda en, fi=fi, ch=ch: en.mul(out=fcv[fi][:, 0:nb], in_=fps[fi][:, 0:nb], mul=convc[:, ch, 2:3]))
                        taps = (0, 1)
                    else:
                        S.op('dve', [f'fraw{fi}', 'convc'], [f'fcv{fi}'],
                             lambda en, fi=fi, ch=ch: en.tensor_scalar(out=fcv[fi][:, 0:nb], in0=fraw[fi][:, 2:2 + nb], scalar1=convc[:, ch, 2:3], scalar2=None, op0=ALU.mult))
                        taps = (0, 1)
                    for k in taps:
                        S.op('dve', [f'fraw{fi}', f'fraw{fi}h', 'convc', f'fcv{fi}'], [f'fcv{fi}'],
                             lambda en, fi=fi, ch=ch, k=k: en.scalar_tensor_tensor(out=fcv[fi][:, 0:nb], in0=fraw[fi][:, k:k + nb], scalar=convc[:, ch, k:k + 1],
                                                                                  in1=fcv[fi][:, 0:nb], op0=ALU.mult, op1=ALU.add))
                    S.op('pool', [f'fraw{fi}'], [f'halo{ch}'], lambda en, fi=fi, ch=ch: en.tensor_copy(out=halo[:, ch, :], in_=fraw[fi][:, nb:nb + 2]))
                    fc.append(fi)
                gs = it % 2
                S.op('act', [f'fcv{fc[0]}'], [f'gel{gs}'], lambda en, gs=gs, f0=fc[0]: en.activation(out=gel[gs][:, 0:nb], in_=fcv[f0][:, 0:nb], func=AF.Gelu_apprx_tanh))
                S.op('dve', [f'gel{gs}', f'fcv{fc[1]}'], [f'hid{cc}'],
                     lambda en, gs=gs, f1=fc[1], cc=cc: en.tensor_tensor(out=hid[:, cc, 0:nb], in0=gel[gs][:, 0:nb], in1=fcv[f1][:, 0:nb], op=ALU.mult))
                it += 1
            for ti in range(nb // 128):
                i = (b0 // 128) + ti
                slot = i % 2
                load_h(i)
                load_h(i + 1)
                for half, mp, k in ((0, mpA[0], 'mpA'), (1, mpB[0], 'mpB')):
                    for cc in range(DFF // 128):
                        S.op('pe', [f'hid{cc}', 'wdn'], [k],
                             lambda en, cc=cc, half=half, mp=mp, ti=ti: en.matmul(mp[:], lhsT=hid[:, cc, ti * 128:(ti + 1) * 128], rhs=wdn_bf[:, cc, half * 512:(half + 1) * 512],
                                                                           start=(cc == 0), stop=(cc == DFF // 128 - 1)),
                             sig=(cc == DFF // 128 - 1))
                post_norm_residual(slot, i, f'ht{slot}')
                if xnT_out is not None:
                    emit_norm_xnT(C, ht[slot][:], f'ht{slot}', slot, xnT_out, i * 128, mask0=(i == 0))


NCOLS = 1280
CG = {'xa': 0, 'ya': 64, 'r': 128, 'k': 192, 'v': 256, 'wd': 320, 'ad': 384, 'gd': 448, 'qA': 576, 'kA': 640, 'qB': 704, 'kB': 768,
      'vAB': 832, 'f': 960, 'vall': 1024}
GN_EPS = 64e-5
DECAY_C = -0.6065306597126334


def emit_mix(C, layer, io):
    S = C.S
    NBLK = TP // 128
    xn_all = io['xn_all']
    w_d = io['w']
    gpre_d = io['gpre']
    pcols_d = io['pcols']
    pmats_d = io['pmats']
    cmask_d = io['cmask']
    ident_d = io['ident']
    oT = io['cat_loc']
    vown = io['vf_dram']
    vfirst = io['vf_dram']
    debug = False
    identf = C.sb("identf", [128, 128], F32)
    pcols = C.sb("pcols", [128, 32], F32)
    pmats = C.sb("pmats", [128, 448], F32)
    cmask = C.sb("cmask", [128, 512], F32)
    S.dma('sp', identf[:], ident_d[:, :], [], ['ident'], 'su0')
    S.dma('sp', pcols[:], pcols_d[:, :], [], ['pcols'], 'su1')
    S.dma('sp', pmats[:], pmats_d[:, :], [], ['pmats'], 'su2')
    S.dma('sp', cmask[:], cmask_d[:, :], [], ['cmask'], 'su3')
    pb = [C.ps(f"pb{i}", [128, 512], F32) for i in range(8)]
    C.wstage = [C.sb(f"wstage{i}", [128, 1408], F32) for i in range(3)]
    C.wsi = 0
    gcol = C.sb("gcol", [128, 8], F32)
    S.dma('sp', C.wstage[1][0:1, 0:D], gpre_d.rearrange("(o n) -> o n", o=1), [], ['wst1'], 'wld1')
    for c in range(8):
        S.op('pe', ['wst1', 'ident'], ['pb0'],
             lambda en, c=c: en.matmul(pb[0][:, c:c + 1], lhsT=C.wstage[1][0:1, c * 128:(c + 1) * 128], rhs=identf[0:1, 0:1], start=True, stop=True))
    S.op('dve', ['pb0'], ['gcols'], lambda en: en.tensor_copy(out=gcol[:], in_=pb[0][:, 0:8]))
    w_bf = C.sb("w_bf", [128, 8, NCOLS], BF16)
    load_cast_weight(C, w_d, 8, NCOLS, w_bf, 'w', gcol=gcol[:], piece=1280)
    causb = C.sb("causb", [128, 128], BF16)
    S.op('dve', ['cmask'], ['causb'], lambda en: en.tensor_copy(out=causb[:], in_=cmask[:, 0:128]))
    ones = C.sb("ones", [128, 128], F32)
    S.op('pool', [], ['ones'], lambda en: en.memset(ones[:], 1.0))
    MASK2 = cmask[0:64, 128:256]
    MASKL = cmask[0:64, 256:320]
    RMASK = cmask[0:64, 320:448]
    pc = lambda j, n=64: pcols[0:n, j:j + 1]
    dcol = C.sb("dcol", [128, 12], F32)
    S.op('act', ['pcols'], ['dcol'], lambda en: en.activation(out=dcol[0:64, 0:1], in_=pc(7), func=AF.Sigmoid))
    S.op('act', ['dcol'], ['dcol'], lambda en: en.activation(out=dcol[0:64, 0:1], in_=dcol[0:64, 0:1], func=AF.Ln))
    S.op('dve', ['dcol'], ['dcol'], lambda en: en.tensor_scalar(out=dcol[0:64, 1:2], in0=dcol[0:64, 0:1], scalar1=16.0, scalar2=None, op0=ALU.mult))
    S.op('dve', ['dcol'], ['dcol'], lambda en: en.tensor_scalar(out=dcol[0:64, 0:1], in0=dcol[0:64, 0:1], scalar1=8.0, scalar2=None, op0=ALU.mult))
    S.op('dve', ['pcols', 'dcol'], ['dcol'], lambda en: en.tensor_scalar(out=dcol[0:64, 2:3], in0=pcols[0:64, 8:9], scalar1=-1.0, scalar2=None, op0=ALU.mult))
    S.op('dve', ['pcols', 'dcol'], ['dcol'], lambda en: en.tensor_scalar(out=dcol[0:64, 3:4], in0=pc(18), scalar1=-1.0, scalar2=1.0, op0=ALU.mult, op1=ALU.add))
    CL, CL2, NFB, OMKA = dcol[0:64, 0:1], dcol[0:64, 1:2], dcol[0:33, 2:3], dcol[0:64, 3:4]
    for jj, src in enumerate((5, 6, 15, 16, 24)):
        S.op('dve', ['pcols', 'dcol'], ['dcol'], lambda en, jj=jj, src=src: en.tensor_scalar(out=dcol[0:64, 4 + jj:5 + jj], in0=pcols[0:64, src:src + 1], scalar1=-1.0, scalar2=None, op0=ALU.mult))
    NB5, NB6, NB15, NB16, NB24 = (dcol[0:64, 4 + jj:5 + jj] for jj in range(5))

    def sigm(out_ap, out_key, in_ap, in_keys, nbias=None, scale=1.0):
        if nbias is None:
            S.op('act', in_keys, [out_key], lambda en: en.activation(out=out_ap, in_=in_ap, func=AF.Exp, scale=-scale))
        else:
            S.op('act', in_keys + ['dcol'], [out_key], lambda en: en.activation(out=out_ap, in_=in_ap, func=AF.Exp, scale=-scale, bias=nbias))
        S.op('dve', [out_key], [out_key], lambda en: en.tensor_scalar(out=out_ap, in0=out_ap, scalar1=1.0, scalar2=None, op0=ALU.add))
        S.op('dve', [out_key], [out_key], lambda en: en.reciprocal(out=out_ap, in_=out_ap))
    MU4 = C.sb("MU4", [64, 4, 128], F32)
    MUB = C.sb("MUB", [128, 4, 128], F32)
    S.op('pool', [], ['MUB'], lambda en: en.memset(MUB[:], 0.0))
    for g in range(4):
        S.op('dve', ['pcols'], ['MU4'], lambda en, g=g: en.tensor_copy(out=MU4[:, g, :], in_=pcols[0:64, 9 + g:10 + g].to_broadcast([64, 128])))
    S.op('dve', ['pcols', 'MUB'], ['MUB'], lambda en: en.tensor_copy(out=MUB[0:64, 0, :], in_=pcols[0:64, 13:14].to_broadcast([64, 128])))
    for g, j in ((1, 14), (2, 22), (3, 23)):
        S.op('dve', ['pcols', 'MUB'], ['MUB'], lambda en, g=g, j=j: en.tensor_copy(out=MUB[:, g, :], in_=pcols[:, j:j + 1].to_broadcast([128, 128])))
    KT = [C.sb(f"KT{h}", [64, TP], BF16) for h in range(2)]
    VA = [C.sb(f"VA{h}", [128, NBLK, 65], BF16) for h in range(2)]
    for h in range(2):
        S.op('pool', [], [f'VA{h}'], lambda en, h=h: en.memset(VA[h][:], 1.0))
    ctm = C.sb("ctm", [128, NBLK, 2], F32)
    cb = C.sb("cbrow", [33, 129], F32)
    S.op('dve', [], ['cb'], lambda en: en.memset(cb[:], 0.0))
    H = C.sb("Hst", [64, 64], F32)
    S.op('dve', [], ['H'], lambda en: en.memset(H[:], 0.0))
    hl = C.sb("hlru", [64, 129], F32)
    S.op('dve', [], ['hl'], lambda en: en.memset(hl[:], 0.0))
    xa_raw = C.sb("xa_raw", [64, 131], F32)
    S.op('dve', [], ['xa_raw'], lambda en: en.memset(xa_raw[:], 0.0))
    raw4 = C.sb("raw4", [64, 4, 129], F32)
    S.op('dve', [], ['raw4'], lambda en: en.memset(raw4[:], 0.0))
    rawB = C.sb("rawB", [128, 4, 129], F32)
    S.op('dve', [], ['rawB'], lambda en: en.memset(rawB[:], 0.0))
    xb = [C.sb(f"xb{i}", [128, 8, 128], BF16) for i in range(2)]
    ost = [C.sb(f"ost{i}", [64, 4, 128], BF16) for i in range(2)]
    xv = xn_all.rearrange("(c r p) t -> r p c t", r=4, p=128)
    ov = oT.rearrange("(g p) t -> p g t", p=64)
    cnt = [0]

    def T(shape, dt=F32, name=None):
        cnt[0] += 1
        return C.sb(name or f"t{cnt[0]}", shape, dt)

    u = T([64, 128]); rg = T([64, 128]); ig = T([64, 128]); aa = T([64, 128]); a2 = T([64, 128]); bbv = T([64, 128]); gy = T([64, 128]); ysb = T([64, 128])
    qT = [T([64, 128], BF16) for _ in range(2)]
    e1 = T([33, 128]); rj = T([128, 4]); Bn = [T([128, NBLK]) for _ in range(2)]
    Pm = [T([128, 128], BF16) for _ in range(3)]
    Osb = T([65, 2, 128]); rec = T([64, 128])
    d4 = T([64, 4, 128]); mx4 = T([64, 4, 128]); dB = T([128, 4, 128]); mxB = T([128, 4, 128])
    wdt = T([64, 128]); sg = T([128, 128]); logw = T([64, 128]); av = T([64, 128]); gg = T([64, 128])
    kkr = T([64, 128]); sq = T([64, 128]); nrm = T([64, 128]); kk = T([64, 128]); k2 = T([64, 128]); tmpk = T([64, 128]); bneg = T([64, 128])
    v2 = T([64, 128]); vd = T([32, 128]); sv = T([64, 128]); vf = T([64, 128])
    L = T([64, 128]); eL = T([64, 128]); eLn = T([64, 128]); eLx = T([64, 128]); eLC = T([64, 128])
    kr = T([64, 2, 2, 64]); kt = T([64, 128]); btn = T([64, 128]); khat = T([64, 128]); bhn = T([64, 128])
    tmC = [T([64, 3, 64]) for _ in range(2)]; ABkC = [T([64, 128]) for _ in range(2)]; ABbC = [T([64, 128]) for _ in range(2)]
    PqC = [[T([64, 2, 64]) for _ in range(2)] for _ in range(2)]; WC = [T([64, 64]) for _ in range(2)]
    Xsb = T([64, 64]); Usb = T([64, 64]); osb = T([64, 128])
    mean = T([64, 128]); cen = T([64, 128]); msq = T([64, 128]); var = T([64, 128]); rk = T([64, 128]); bon = T([64, 128])
    r_ = mx4[:, 0, :]; k_ = mx4[:, 1, :]; v_ = mx4[:, 2, :]; wd_ = mx4[:, 3, :]
    pm_ = lambda j, n=64, k=64: pmats[0:k, j * 64:j * 64 + n]

    def proj(bank, grp, col0, m, xs, last=True):
        for c in range(8):
            S.op('pe', [f'xb{xs}', 'w'], [f'pb{bank}'],
                 lambda en, c=c: en.matmul(pb[bank][0:m, grp * 128:(grp + 1) * 128], lhsT=w_bf[:, c, col0:col0 + m], rhs=xb[xs][:, c, :], start=(c == 0), stop=(c == 7)),
                 sig=(c == 7))

    def dv(reads, writes, fn, e='dve'):
        S.op(e, reads, writes, fn)

    for n in range(NBLK):
        t0 = n * 128
        xs = n % 2
        os_ = ost[xs]
        def load_xb(nn):
            rk_, sl_ = (0, 0) if nn == 0 else ((nn - 1) // 16, (nn - 1) % 16 + 1)
            S.dma('sp', xb[nn % 2][:], xv[rk_, :, :, sl_ * 128:(sl_ + 1) * 128], ['xn_all'], [f'xb{nn % 2}'], f'xl{nn % 2}')
        if n == 0:
            load_xb(0)
        for g, nm in enumerate(('r', 'k', 'v', 'wd')):
            proj(0, g, CG[nm], 64, xs)
        proj(1, 0, CG['ad'], 64, xs)
        proj(1, 1, CG['gd'], 128, xs)
        if layer == 1:
            proj(1, 2, CG['vall'], 128, xs)
            proj(1, 3, CG['vall'] + 128, 128, xs)
        proj(2, 0, CG['xa'], 64, xs)
        proj(2, 1, CG['ya'], 64, xs)
        proj(2, 2, CG['f'], 33, xs)
        for g, nm in enumerate(('qA', 'kA', 'qB', 'kB')):
            proj(3, g, CG[nm], 64, xs)
        for c in range(8):
            S.op('pe', [f'xb{xs}', 'w'], ['pb2'],
                 lambda en, c=c: en.matmul(pb[2][:, 384:512], lhsT=xb[xs][:, c, :], rhs=w_bf[:, c, CG['vAB']:CG['vAB'] + 128], start=(c == 0), stop=(c == 7)), sig=(c == 7))
        if n + 1 < NBLK:
            load_xb(n + 1)
        S.op('act', ['pb0'], ['raw4'], lambda en: en.copy(out=raw4[:, :, 1:129], in_=pb[0][0:64, :].rearrange("p (g t) -> p g t", g=4)))
        ng = 4 if layer == 1 else 2
        S.op('act', ['pb1'], ['rawB'], lambda en: en.copy(out=rawB[:, 0:ng, 1:129], in_=pb[1][:, 0:ng * 128].rearrange("p (g t) -> p g t", g=ng)))
        S.op('act', ['pb2'], ['xa_raw'], lambda en: en.copy(out=xa_raw[:, 3:131], in_=pb[2][0:64, 0:128]))
        S.op('act', ['pb2'], ['ysb'], lambda en: en.copy(out=ysb[:], in_=pb[2][0:64, 128:256]))
        dv(['ysb'], ['gy'], lambda en: en.tensor_tensor(out=gy[:], in0=ysb[:], in1=ysb[:], op=ALU.mult))
        dv(['gy'], ['gy'], lambda en: en.tensor_scalar(out=gy[:], in0=gy[:], scalar1=0.044715, scalar2=1.0, op0=ALU.mult, op1=ALU.add))
        dv(['gy', 'ysb'], ['gy'], lambda en: en.tensor_tensor(out=gy[:], in0=gy[:], in1=ysb[:], op=ALU.mult))
        sigm(gy[:], 'gy', gy[:], ['gy'], scale=1.5957691216057308)
        dv(['gy', 'ysb'], ['gy'], lambda en: en.tensor_tensor(out=gy[:], in0=gy[:], in1=ysb[:], op=ALU.mult))
        S.op('act', ['pb2', 'dcol'], ['e1'], lambda en: en.activation(out=e1[:], in_=pb[2][0:33, 256:384], func=AF.Exp, bias=NFB, scale=-1.0))
        for h in range(2):
            dv(['pb3'], [f'qT{h}'], lambda en, h=h: en.tensor_scalar(out=qT[h][:], in0=pb[3][0:64, (2 * h) * 128:(2 * h + 1) * 128], scalar1=0.125, scalar2=None, op0=ALU.mult))
            S.op('act', ['pb3'], [f'KT{h}'], lambda en, h=h: en.copy(out=KT[h][:, t0:t0 + 128], in_=pb[3][0:64, (2 * h + 1) * 128:(2 * h + 2) * 128]))
            dv(['pb2'], [f'VA{h}'], lambda en, h=h: en.tensor_copy(out=VA[h][:, n, 0:64], in_=pb[2][:, 384 + h * 64:384 + (h + 1) * 64]))
            if n == 0:
                dv([f'VA{h}'], [f'VA{h}'], lambda en, h=h: en.memset(VA[h][0:112, 0, :], 0.0))
        def lru_gen():
            dv(['xa_raw', 'pcols'], ['u'], lambda en: en.tensor_scalar(out=u[:], in0=xa_raw[:, 0:128], scalar1=pc(0), scalar2=pc(4), op0=ALU.mult, op1=ALU.add))
            for k in (1, 2, 3):
                dv(['xa_raw', 'pcols', 'u'], ['u'], lambda en, k=k: en.scalar_tensor_tensor(out=u[:], in0=xa_raw[:, k:k + 128], scalar=pc(k), in1=u[:], op0=ALU.mult, op1=ALU.add))
            dv(['xa_raw', 'u'], ['xa_raw'], lambda en: en.tensor_copy(out=xa_raw[:, 0:3], in_=xa_raw[:, 128:131]))
            yield
            S.op('pe', ['u', 'pmats'], ['pb2'], lambda en: en.matmul(pb[2][0:64, 0:128], lhsT=pm_(0), rhs=u[:], start=True, stop=True))
            S.op('pe', ['u', 'pmats'], ['pb2'], lambda en: en.matmul(pb[2][0:64, 128:256], lhsT=pm_(1), rhs=u[:], start=True, stop=True))
            sigm(rg[:], 'rg', pb[2][0:64, 0:128], ['pb2'], nbias=NB5)
            sigm(ig[:], 'ig', pb[2][0:64, 128:256], ['pb2'], nbias=NB6)
            yield
            S.op('act', ['rg', 'dcol'], ['aa'], lambda en: en.activation(out=aa[:], in_=rg[:], func=AF.Exp, scale=CL))
            S.op('act', ['rg', 'dcol'], ['a2'], lambda en: en.activation(out=a2[:], in_=rg[:], func=AF.Exp, scale=CL2))
            yield
            dv(['a2'], ['a2'], lambda en: en.tensor_scalar(out=a2[:], in0=a2[:], scalar1=-1.0, scalar2=1.0, op0=ALU.mult, op1=ALU.add))
            dv(['a2'], ['a2'], lambda en: en.tensor_scalar(out=a2[:], in0=a2[:], scalar1=1e-30, scalar2=None, op0=ALU.max))
            S.op('act', ['a2'], ['a2'], lambda en: en.activation(out=a2[:], in_=a2[:], func=AF.Ln))
            S.op('act', ['a2'], ['a2'], lambda en: en.activation(out=a2[:], in_=a2[:], func=AF.Exp, scale=0.5))
            dv(['ig', 'u'], ['bbv'], lambda en: en.tensor_tensor(out=bbv[:], in0=ig[:], in1=u[:], op=ALU.mult))
            dv(['bbv', 'a2'], ['bbv'], lambda en: en.tensor_tensor(out=bbv[:], in0=bbv[:], in1=a2[:], op=ALU.mult))
            yield
            if n == 0:
                dv(['bbv'], ['bbv'], lambda en: en.memset(bbv[:, 0:112], 0.0))
            dv(['aa', 'bbv', 'hl'], ['hl'], lambda en: en.tensor_tensor_scan(out=hl[:, 1:129], data0=aa[:], data1=bbv[:], initial=hl[:, 0:1], op0=ALU.mult, op1=ALU.add))
            dv(['hl', 'gy'], [f'ost{xs}_0'], lambda en: en.tensor_tensor(out=os_[:, 0, :], in0=hl[:, 1:129], in1=gy[:], op=ALU.mult))
            dv(['hl'], ['hl'], lambda en: en.tensor_copy(out=hl[:, 0:1], in_=hl[:, 128:129]))

            yield

        def fox_gen():
            S.op('act', ['e1'], ['e1'], lambda en: en.activation(out=e1[:], in_=e1[:], func=AF.Ln, bias=1.0))
            dv(['e1', 'cb', 'ones'], ['cb'], lambda en: en.tensor_tensor_scan(out=cb[:, 1:129], data0=ones[0:33, :], data1=e1[:], initial=cb[:, 0:1], op0=ALU.mult, op1=ALU.subtract))
            for h, hr in ((0, 0), (1, 32)):
                S.dma('sp', ctm[:, n, h:h + 1], cb[hr:hr + 1, 1:129], ['cb'], ['ctm'], f'ctmd{h}')
                S.op('pe', ['cb', 'ones'], ['pb5'], lambda en, h=h, hr=hr: en.matmul(pb[5][:, 386 + h:387 + h], lhsT=ones[hr:hr + 1, 0:128], rhs=cb[hr:hr + 1, 128:129], start=True, stop=True))
            dv(['pb5'], ['rj'], lambda en: en.tensor_copy(out=rj[:, 0:2], in_=pb[5][:, 386:388]))
            dv(['cb'], ['cb'], lambda en: en.tensor_copy(out=cb[:, 0:1], in_=cb[:, 128:129]))
            for h in range(2):
                dv(['ctm', 'rj'], [f'Bn{h}'], lambda en, h=h: en.tensor_scalar(out=Bn[h][:, 0:n + 1], in0=ctm[:, 0:n + 1, h], scalar1=-1.0, scalar2=rj[:, h:h + 1], op0=ALU.mult, op1=ALU.add))
            yield
            for h in range(2):
                SB = (5, 7, 4)

                def s_mm(i):
                    bk = SB[i % 3]
                    S.op('pe', [f'KT{h}', f'qT{h}'], [f'pb{bk}'], lambda en, i=i, bk=bk: en.matmul(pb[bk][:, 0:128], lhsT=KT[h][:, i * 128:(i + 1) * 128], rhs=qT[h][:], start=True, stop=True))
                s_mm(0)
                if n >= 1:
                    s_mm(1)
                for i in range(n + 1):
                    if i + 2 <= n:
                        s_mm(i + 2)
                    bk = SB[i % 3]
                    pm = Pm[i % 3]
                    pk = f'Pm{i % 3}'
                    S.op('act', [f'pb{bk}', f'Bn{h}'], [pk], lambda en, i=i, bk=bk, pm=pm: en.activation(out=pm[:], in_=pb[bk][:, 0:128], func=AF.Exp, bias=Bn[h][:, i:i + 1]))
                    if i == n:
                        S.op('dve', [pk, 'causb'], [pk], lambda en, pm=pm: en.tensor_tensor(out=pm[:], in0=pm[:], in1=causb[:], op=ALU.mult))
                    S.op('pe', [f'VA{h}', pk], ['pb6'], lambda en, i=i, pm=pm: en.matmul(pb[6][0:65, 0:128], lhsT=VA[h][:, i, :], rhs=pm[:], start=(i == 0), stop=(i == n)))
                    yield
                S.op('act', ['pb6'], ['Osb'], lambda en: en.copy(out=Osb[:, h, :], in_=pb[6][0:65, 0:128]))
                S.op('pe', ['Osb', 'ones'], ['pb5'], lambda en: en.matmul(pb[5][0:64, 256:384], lhsT=ones[64:65, 0:64], rhs=Osb[64:65, h, :], start=True, stop=True))
                dv(['pb5'], ['rec'], lambda en: en.tensor_scalar(out=rec[:], in0=pb[5][0:64, 256:384], scalar1=1e-30, scalar2=None, op0=ALU.add))
                dv(['rec'], ['rec'], lambda en: en.reciprocal(out=rec[:], in_=rec[:]))
                dv(['rec', 'Osb'], [f'ost{xs}_{1 + h}'], lambda en: en.tensor_tensor(out=os_[:, 1 + h, :], in0=Osb[0:64, h, :], in1=rec[:], op=ALU.mult))
                yield

        def rwkv_gen():
            dv(['raw4'], ['d4'], lambda en: en.tensor_tensor(out=d4[:], in0=raw4[:, :, 0:128], in1=raw4[:, :, 1:129], op=ALU.subtract))
            dv(['d4', 'MU4'], ['d4'], lambda en: en.tensor_tensor(out=d4[:], in0=d4[:], in1=MU4[:], op=ALU.mult))
            dv(['d4', 'raw4'], ['mx4'], lambda en: en.tensor_tensor(out=mx4[:], in0=d4[:], in1=raw4[:, :, 1:129], op=ALU.add))
            dv(['raw4', 'mx4'], ['raw4'], lambda en: en.tensor_copy(out=raw4[:, :, 0:1], in_=raw4[:, :, 128:129]))
            dv(['rawB'], ['dB'], lambda en: en.tensor_tensor(out=dB[:, 0:ng], in0=rawB[:, 0:ng, 0:128], in1=rawB[:, 0:ng, 1:129], op=ALU.subtract))
            dv(['dB', 'MUB'], ['dB'], lambda en: en.tensor_tensor(out=dB[:, 0:ng], in0=dB[:, 0:ng], in1=MUB[:, 0:ng], op=ALU.mult))
            dv(['dB', 'rawB'], ['mxB'], lambda en: en.tensor_tensor(out=mxB[:, 0:ng], in0=dB[:, 0:ng], in1=rawB[:, 0:ng, 1:129], op=ALU.add))
            dv(['rawB', 'mxB'], ['rawB'], lambda en: en.tensor_copy(out=rawB[:, 0:ng, 0:1], in_=rawB[:, 0:ng, 128:129]))
            yield
            sigm(wdt[:], 'wdt', wd_, ['mx4'], scale=2.0)
            dv(['wdt'], ['wdt'], lambda en: en.tensor_scalar(out=wdt[:], in0=wdt[:], scalar1=2.0, scalar2=-1.0, op0=ALU.mult, op1=ALU.add))
            sigm(sg[:], 'sg', mxB[:, 1, :], ['mxB'])
            S.op('pe', ['wdt', 'pmats'], ['pb0'], lambda en: en.matmul(pb[0][0:64, 0:128], lhsT=pm_(2), rhs=wdt[:], start=True, stop=True))
            S.op('pe', ['mxB', 'pmats'], ['pb0'], lambda en: en.matmul(pb[0][0:64, 128:256], lhsT=pm_(3), rhs=mxB[0:64, 0, :], start=True, stop=True))
            S.op('pe', ['sg', 'pmats'], ['pb0'], lambda en: en.matmul(pb[0][0:64, 256:384], lhsT=pm_(4, 64, 128), rhs=sg[:], start=True, stop=True))
            dv(['mx4', 'pcols'], ['kkr'], lambda en: en.tensor_scalar(out=kkr[:], in0=k_, scalar1=pc(17), scalar2=None, op0=ALU.mult))
            dv(['kkr'], ['sq'], lambda en: en.tensor_tensor(out=sq[:], in0=kkr[:], in1=kkr[:], op=ALU.mult))
            S.op('pe', ['sq', 'ones'], ['pb0'], lambda en: en.matmul(pb[0][0:64, 384:512], lhsT=ones[0:64, 0:64], rhs=sq[:], start=True, stop=True))
            yield
            sigm(logw[:], 'logw', pb[0][0:64, 0:128], ['pb0'], nbias=NB15)
            dv(['logw'], ['logw'], lambda en: en.tensor_scalar(out=logw[:], in0=logw[:], scalar1=DECAY_C, scalar2=None, op0=ALU.mult))
            sigm(av[:], 'av', pb[0][0:64, 128:256], ['pb0'], nbias=NB16)
            S.op('act', ['pb0'], ['gg'], lambda en: en.copy(out=gg[:], in_=pb[0][0:64, 256:384]))
            dv(['pb0'], ['nrm'], lambda en: en.tensor_scalar(out=nrm[:], in0=pb[0][0:64, 384:512], scalar1=1e-24, scalar2=None, op0=ALU.max))
            S.op('act', ['nrm'], ['nrm'], lambda en: en.activation(out=nrm[:], in_=nrm[:], func=AF.Ln))
            S.op('act', ['nrm'], ['nrm'], lambda en: en.activation(out=nrm[:], in_=nrm[:], func=AF.Exp, scale=-0.5))
            dv(['kkr', 'nrm'], ['kk'], lambda en: en.tensor_tensor(out=kk[:], in0=kkr[:], in1=nrm[:], op=ALU.mult))
            dv(['av', 'pcols', 'dcol'], ['tmpk'], lambda en: en.tensor_scalar(out=tmpk[:], in0=av[:], scalar1=pc(18), scalar2=OMKA, op0=ALU.mult, op1=ALU.add))
            dv(['tmpk', 'mx4'], ['k2'], lambda en: en.tensor_tensor(out=k2[:], in0=k_, in1=tmpk[:], op=ALU.mult))
            dv(['kk', 'av'], ['bneg'], lambda en: en.scalar_tensor_tensor(out=bneg[:], in0=kk[:], scalar=-1.0, in1=av[:], op0=ALU.mult, op1=ALU.mult))
            yield
            if layer == 0:
                dv(['mx4'], ['v2'], lambda en: en.tensor_copy(out=v2[:], in_=v_))
                S.dma('sp', vown[:, t0:t0 + 128], v2[:], ['v2'], ['vf_dram'], 'vo')
            else:
                S.dma('sp', vf[:], vfirst[:, t0:t0 + 128], ['vf_dram'], ['vf'], 'vfl')
                for c in range(2):
                    S.op('pe', ['mxB', 'pmats'], ['pb1'], lambda en, c=c: en.matmul(pb[1][0:32, 0:128], lhsT=pmats[:, 320 + c * 32:352 + c * 32], rhs=mxB[:, 2 + c, :], start=(c == 0), stop=(c == 1)))
                S.op('act', ['pb1'], ['vd'], lambda en: en.copy(out=vd[:], in_=pb[1][0:32, 0:128]))
                S.op('pe', ['vd', 'pmats'], ['pb1'], lambda en: en.matmul(pb[1][0:64, 128:256], lhsT=pmats[0:32, 384:448], rhs=vd[:], start=True, stop=True))
                sigm(sv[:], 'sv', pb[1][0:64, 128:256], ['pb1'], nbias=NB24)
                dv(['vf', 'mx4'], ['vf'], lambda en: en.tensor_tensor(out=vf[:], in0=vf[:], in1=v_, op=ALU.subtract))
                dv(['vf', 'sv'], ['vf'], lambda en: en.tensor_tensor(out=vf[:], in0=vf[:], in1=sv[:], op=ALU.mult))
                dv(['vf', 'mx4'], ['v2'], lambda en: en.tensor_tensor(out=v2[:], in0=vf[:], in1=v_, op=ALU.add))
            yield
            dv(['logw', 'cmask'], ['L'], lambda en: en.tensor_tensor_scan(out=L[:], data0=RMASK, data1=logw[:], initial=0.0, op0=ALU.mult, op1=ALU.add))
            S.op('act', ['L'], ['eL'], lambda en: en.activation(out=eL[:], in_=L[:], func=AF.Exp))
            S.op('act', ['L'], ['eLn'], lambda en: en.activation(out=eLn[:], in_=L[:], func=AF.Exp, scale=-1.0))
            dv(['L', 'logw'], ['eLx'], lambda en: en.tensor_tensor(out=eLx[:], in0=L[:], in1=logw[:], op=ALU.subtract))
            S.op('act', ['eLx'], ['eLx'], lambda en: en.activation(out=eLx[:], in_=eLx[:], func=AF.Exp))
            for c in range(2):
                S.op('act', ['L'], ['eLC'], lambda en, c=c: en.activation(out=eLC[:, c * 64:(c + 1) * 64], in_=L[:, c * 64:(c + 1) * 64], func=AF.Exp, scale=-1.0, bias=L[:, c * 64 + 63:c * 64 + 64]))
            v3 = lambda ap: ap.rearrange("p (c t) -> p c t", c=2)
            dv(['kk', 'eLx'], ['kr'], lambda en: en.tensor_tensor(out=kr[:, :, 0, :], in0=v3(kk[:]), in1=v3(eLx[:]), op=ALU.mult))
            dv(['mx4', 'eL'], ['kr'], lambda en: en.tensor_tensor(out=kr[:, :, 1, :], in0=v3(r_), in1=v3(eL[:]), op=ALU.mult))
            dv(['k2', 'eLn'], ['kt'], lambda en: en.tensor_tensor(out=kt[:], in0=k2[:], in1=eLn[:], op=ALU.mult), e='pool')
            dv(['bneg', 'eLn'], ['btn'], lambda en: en.tensor_tensor(out=btn[:], in0=bneg[:], in1=eLn[:], op=ALU.mult), e='pool')
            dv(['k2', 'eLC'], ['khat'], lambda en: en.tensor_tensor(out=khat[:], in0=k2[:], in1=eLC[:], op=ALU.mult), e='pool')
            dv(['bneg', 'eLC'], ['bhn'], lambda en: en.tensor_tensor(out=bhn[:], in0=bneg[:], in1=eLC[:], op=ALU.mult), e='pool')
            yield
            for c in range(2):
                cs = slice(c * 64, (c + 1) * 64)
                tm, ABk, ABb, W, Pq = tmC[c], ABkC[c], ABbC[c], WC[c], PqC[c]
                kT, kAk, kAb, kW = f'tm{c}', f'ABk{c}', f'ABb{c}', f'W{c}'
                for j, (src, key) in enumerate(((v2, 'v2'), (khat, 'khat'), (bhn, 'bhn'))):
                    S.op('pe', [key, 'ident'], ['pb1'], lambda en, j=j, src=src: en.matmul(pb[1][0:64, j * 64:(j + 1) * 64], lhsT=src[:, cs], rhs=identf[0:64, 0:64], start=True, stop=True))
                S.op('act', ['pb1'], [kT], lambda en: en.copy(out=tm[:], in_=pb[1][0:64, 0:192].rearrange("p (j t) -> p j t", j=3)))
                krc = kr[:, c, :, :].rearrange("p a t -> p (a t)")
                S.op('pe', ['kt', 'kr'], ['pb2'], lambda en: en.matmul(pb[2][0:64, 0:128], lhsT=kt[:, cs], rhs=krc, start=True, stop=True))
                S.op('pe', ['btn', 'kr'], ['pb2'], lambda en: en.matmul(pb[2][0:64, 128:256], lhsT=btn[:, cs], rhs=krc, start=True, stop=True))
                S.op('pe', ['btn', 'kr'], ['pb2'], lambda en: en.matmul(pb[2][0:64, 256:320], lhsT=kr[:, c, 0, :], rhs=btn[:, cs], start=True, stop=True))
                dv(['pb2', 'cmask'], [kAk], lambda en: en.tensor_tensor(out=ABk[:], in0=pb[2][0:64, 0:128], in1=MASK2, op=ALU.mult))
                dv(['pb2', 'cmask'], [kAb], lambda en: en.tensor_tensor(out=ABb[:], in0=pb[2][0:64, 128:256], in1=MASK2, op=ALU.mult))
                dv([kAb], [f'Pq{c}_0'], lambda en: en.tensor_copy(out=Pq[0][:, 0, :], in_=ABb[:, 0:64]))
                dv(['pb2', 'cmask', f'Pq{c}_0'], [f'Pq{c}_0'], lambda en: en.tensor_tensor(out=Pq[0][:, 1, :], in0=pb[2][0:64, 256:320], in1=MASKL, op=ALU.mult))
                dv([kAb, 'ident'], [kW], lambda en: en.tensor_tensor(out=W[:], in0=ABb[:, 0:64], in1=identf[0:64, 0:64], op=ALU.add))
                yield
                for lvl in range(5):
                    cur, nxt = Pq[lvl % 2], Pq[(lvl + 1) % 2]
                    ck, nk = f'Pq{c}_{lvl % 2}', f'Pq{c}_{(lvl + 1) % 2}'
                    S.op('pe', [ck], ['pb3'], lambda en, cur=cur: en.matmul(pb[3][0:64, 0:64], lhsT=cur[:, 1, :], rhs=cur[:, 0, :], start=True, stop=True))
                    S.op('pe', [ck], ['pb3'], lambda en, cur=cur: en.matmul(pb[3][0:64, 64:128], lhsT=cur[:, 0, :], rhs=cur[:, 1, :], start=True, stop=True))
                    S.op('act', ['pb3'], [nk], lambda en, nxt=nxt: en.copy(out=nxt[:], in_=pb[3][0:64, 0:128].rearrange("p (a t) -> p a t", a=2)))
                    S.op('pe', [nk, kW], ['pb3'], lambda en, nxt=nxt: en.matmul(pb[3][0:64, 128:192], lhsT=nxt[:, 1, :], rhs=W[:], start=True, stop=True))
                    dv(['pb3', kW], [kW], lambda en: en.tensor_tensor(out=W[:], in0=pb[3][0:64, 128:192], in1=W[:], op=ALU.add))
                    yield
                pre_done[c] = True

        def rwkv_chain():
            for c in range(2):
                while not pre_done[c]:
                    yield
                cs = slice(c * 64, (c + 1) * 64)
                tm, ABk, ABb, W = tmC[c], ABkC[c], ABbC[c], WC[c]
                kT, kAk, kAb, kW = f'tm{c}', f'ABk{c}', f'ABb{c}', f'W{c}'
                S.op('pe', [kAk, kT], ['pb0'], lambda en: en.matmul(pb[0][0:64, 0:64], lhsT=ABk[:, 0:64], rhs=tm[:, 0, :], start=True, stop=False))
                S.op('pe', ['kr', 'H'], ['pb0'], lambda en: en.matmul(pb[0][0:64, 0:64], lhsT=kr[:, c, 0, :], rhs=H[:], start=False, stop=True))
                S.op('act', ['pb0'], ['Xsb'], lambda en: en.copy(out=Xsb[:], in_=pb[0][0:64, 0:64]))
                yield
                S.op('pe', [kW, 'Xsb'], ['pb0'], lambda en: en.matmul(pb[0][0:64, 64:128], lhsT=W[:], rhs=Xsb[:], start=True, stop=True))
                S.op('act', ['pb0'], ['Usb'], lambda en: en.copy(out=Usb[:], in_=pb[0][0:64, 64:128]))
                yield
                S.op('pe', ['H', 'kr'], ['pb0'], lambda en: en.matmul(pb[0][0:64, 192:256], lhsT=H[:], rhs=kr[:, c, 1, :], start=True, stop=False))
                S.op('pe', [kT, kAk], ['pb0'], lambda en: en.matmul(pb[0][0:64, 192:256], lhsT=tm[:, 0, :], rhs=ABk[:, 64:128], start=False, stop=False))
                S.op('pe', ['Usb', kAb], ['pb0'], lambda en: en.matmul(pb[0][0:64, 192:256], lhsT=Usb[:], rhs=ABb[:, 64:128], start=False, stop=True))
                S.op('act', ['pb0'], ['osb'], lambda en: en.copy(out=osb[:, cs], in_=pb[0][0:64, 192:256]))
                yield
                S.op('pe', [kT], ['pb0'], lambda en: en.matmul(pb[0][0:64, 128:192], lhsT=tm[:, 1, :], rhs=tm[:, 0, :], start=True, stop=False))
                S.op('pe', [kT, 'Usb'], ['pb0'], lambda en: en.matmul(pb[0][0:64, 128:192], lhsT=tm[:, 2, :], rhs=Usb[:], start=False, stop=True))
                dv(['pb0', 'H', 'eL'], ['H'], lambda en: en.scalar_tensor_tensor(out=H[:], in0=H[:], scalar=eL[:, c * 64 + 63:c * 64 + 64], in1=pb[0][0:64, 128:192], op0=ALU.mult, op1=ALU.add))
                yield
            yield
            S.op('pe', ['osb', 'ones'], ['pb1'], lambda en: en.matmul(pb[1][0:64, 0:128], lhsT=ones[0:64, 0:64], rhs=osb[:], start=True, stop=True))
            dv(['osb'], ['sq'], lambda en: en.tensor_tensor(out=sq[:], in0=osb[:], in1=osb[:], op=ALU.mult))
            S.op('pe', ['sq', 'ones'], ['pb1'], lambda en: en.matmul(pb[1][0:64, 128:256], lhsT=ones[0:64, 0:64], rhs=sq[:], start=True, stop=True))
            dv(['mx4', 'pcols', 'k2'], ['rk'], lambda en: en.scalar_tensor_tensor(out=rk[:], in0=r_, scalar=pc(19), in1=k2[:], op0=ALU.mult, op1=ALU.mult))
            S.op('pe', ['rk', 'ones'], ['pb1'], lambda en: en.matmul(pb[1][0:64, 256:384], lhsT=ones[0:64, 0:64], rhs=rk[:], start=True, stop=True))
            dv(['pb1'], ['mean'], lambda en: en.tensor_scalar(out=mean[:], in0=pb[1][0:64, 0:128], scalar1=1.0 / 64, scalar2=None, op0=ALU.mult))
            dv(['osb', 'mean'], ['cen'], lambda en: en.tensor_tensor(out=cen[:], in0=osb[:], in1=mean[:], op=ALU.subtract))
            dv(['mean'], ['msq'], lambda en: en.tensor_tensor(out=msq[:], in0=mean[:], in1=mean[:], op=ALU.mult))
            dv(['pb1', 'msq'], ['var'], lambda en: en.scalar_tensor_tensor(out=var[:], in0=pb[1][0:64, 128:256], scalar=1.0 / 64, in1=msq[:], op0=ALU.mult, op1=ALU.subtract))
            dv(['var'], ['var'], lambda en: en.tensor_scalar(out=var[:], in0=var[:], scalar1=GN_EPS, scalar2=None, op0=ALU.add))
            S.op('act', ['var'], ['var'], lambda en: en.activation(out=var[:], in_=var[:], func=AF.Ln))
            S.op('act', ['var'], ['var'], lambda en: en.activation(out=var[:], in_=var[:], func=AF.Exp, scale=-0.5))
            dv(['cen', 'var'], ['cen'], lambda en: en.tensor_tensor(out=cen[:], in0=cen[:], in1=var[:], op=ALU.mult))
            dv(['cen', 'pcols'], ['cen'], lambda en: en.tensor_scalar(out=cen[:], in0=cen[:], scalar1=pc(20), scalar2=pc(21), op0=ALU.mult, op1=ALU.add))
            dv(['pb1', 'v2'], ['bon'], lambda en: en.tensor_tensor(out=bon[:], in0=pb[1][0:64, 256:384], in1=v2[:], op=ALU.mult))
            dv(['cen', 'bon'], ['cen'], lambda en: en.tensor_tensor(out=cen[:], in0=cen[:], in1=bon[:], op=ALU.add))
            dv(['cen', 'gg'], [f'ost{xs}_3'], lambda en: en.tensor_tensor(out=os_[:, 3, :], in0=cen[:], in1=gg[:], op=ALU.mult))

        pre_done = [False, False]
        gfox, grw = fox_gen(), [lru_gen(), rwkv_gen(), rwkv_chain()]
        kfox = max(1, (2 * (n + 1) + 2) // 30)
        fox_alive = True
        while fox_alive or grw:
            if fox_alive:
                for _ in range(kfox):
                    try:
                        next(gfox)
                    except StopIteration:
                        fox_alive = False
                        break
            for g_ in list(grw):
                try:
                    next(g_)
                except StopIteration:
                    grw.remove(g_)
        S.dma('sp', ov[:, :, t0:t0 + 128], os_[:], [f'ost{xs}_{j}' for j in range(4)], ['cat_loc'], f'oo{xs}')


A_COLS, B_COLS = 512, 1544
PC0 = A_COLS + B_COLS


def consts():
    ident = np.eye(128, dtype=np.float32)
    cm = np.zeros((128, 512), np.float32)
    r = np.arange(128)[:, None]
    c = np.arange(128)[None, :]
    cm[:, 0:128] = (r <= c)
    r64 = np.arange(64)[:, None]
    c64 = np.arange(64)[None, :]
    cm[0:64, 128:192] = (r64 < c64)
    cm[0:64, 192:256] = (r64 <= c64)
    cm[0:64, 256:320] = (r64 > c64)
    cm[0:64, 320:448] = 1.0
    cm[0:64, 320] = 0.0
    cm[0:64, 384] = 0.0
    return ident, cm


def mix_inputs(l, g, I):
    w_in = I['w_in'][l]
    W = np.zeros((D, NCOLS), np.float32)
    hs = slice(64 * g, 64 * g + 64)
    W[:, CG['xa']:CG['xa'] + 64] = w_in[:, 0:256][:, hs]
    W[:, CG['ya']:CG['ya'] + 64] = w_in[:, 256:512][:, hs]
    pb = w_in[:, A_COLS:A_COLS + B_COLS]
    for nm, h in (('A', 2 * g), ('B', 2 * g + 1)):
        W[:, CG['q' + nm]:CG['q' + nm] + 64] = pb[:, 64 * h:64 * h + 64]
        W[:, CG['k' + nm]:CG['k' + nm] + 64] = pb[:, 512 + 64 * h:512 + 64 * h + 64]
    W[:, CG['vAB']:CG['vAB'] + 128] = pb[:, 1024 + 128 * g:1024 + 128 * g + 128]
    W[:, CG['f']] = pb[:, 1536 + 2 * g]
    W[:, CG['f'] + 32] = pb[:, 1536 + 2 * g + 1]
    pc = w_in[:, PC0:]
    W[:, CG['r']:CG['r'] + 64] = pc[:, 0:256][:, hs]
    W[:, CG['k']:CG['k'] + 64] = pc[:, 256:512][:, hs]
    W[:, CG['v']:CG['v'] + 64] = pc[:, 512:768][:, hs]
    W[:, CG['wd']:CG['wd'] + 64] = pc[:, 768:832]
    W[:, CG['ad']:CG['ad'] + 64] = pc[:, 832:896]
    W[:, CG['gd']:CG['gd'] + 128] = pc[:, 896:1024]
    W[:, CG['vall']:CG['vall'] + 256] = pc[:, 512:768]
    P = np.zeros((128, 32), np.float32)
    for k in range(4):
        P[0:64, k] = I['lru_conv_w'][l][k, hs]
    P[0:64, 4] = I['lru_conv_b'][l][hs]
    P[0:64, 5] = I['lru_gate_a_b'][l][hs]
    P[0:64, 6] = I['lru_gate_x_b'][l][hs]
    P[0:64, 7] = I['lru_lambda'][l][hs]
    P[0, 8] = I['fox_f_bias'][l][2 * g]
    P[32, 8] = I['fox_f_bias'][l][2 * g + 1]
    mu = I['rwkv_mu'][l]
    P[0:64, 9] = mu[0:256][hs]
    P[0:64, 10] = mu[256:512][hs]
    P[0:64, 11] = mu[512:768][hs]
    P[0:64, 12] = mu[768:832]
    P[0:64, 13] = mu[832:896]
    P[:, 14] = mu[896:1024]
    P[0:64, 15] = I['rwkv_w0'][l][hs]
    P[0:64, 16] = I['rwkv_a0'][l][hs]
    P[0:64, 17] = I['rwkv_k_k'][l][hs]
    P[0:64, 18] = I['rwkv_k_a'][l][hs]
    P[0:64, 19] = I['rwkv_r_k'][l][g]
    P[0:64, 20] = I['rwkv_ln_w'][l][hs]
    P[0:64, 21] = I['rwkv_ln_b'][l][hs]
    P[:, 22] = mu[512:640]
    P[:, 23] = mu[640:768]
    M = np.zeros((128, 448), np.float32)
    M[0:64, 0:64] = I['lru_gate_a_w'][l][g]
    M[0:64, 64:128] = I['lru_gate_x_w'][l][g]
    M[0:64, 128:192] = I['rwkv_w_up'][l][:, hs]
    M[0:64, 192:256] = I['rwkv_a_up'][l][:, hs]
    M[:, 256:320] = I['rwkv_g_up'][l][:, hs]
    if l >= 1:
        P[0:64, 24] = I['rwkv_v0'][l - 1][hs]
        vd = I['rwkv_v_down'][l - 1]
        M[:, 320:352] = vd[0:128]
        M[:, 352:384] = vd[128:256]
        M[0:32, 384:448] = I['rwkv_v_up'][l - 1][:, hs]
    return W, P, M


GROUPS = [[0, 1, 2, 3], [4, 5, 6, 7]]
LAYER_KEYS = ('w', 'gpre', 'pcols', 'pmats', 'w_out', 'gpost1', 'w_up', 'w_dn', 'conv', 'gpre2', 'gpost2')
LAYER_SHAPES = {'w': [D, NCOLS], 'gpre': [D], 'pcols': [128, 32], 'pmats': [128, 448], 'w_out': [D, D], 'gpost1': [D],
                'w_up': [D, 2 * DFF], 'w_dn': [DFF, D], 'conv': [3, 2 * DFF], 'gpre2': [D], 'gpost2': [D]}


def build_fused(parts='PGMCTUg'):
    C = Ctx("fused")
    S = C.S
    h0 = C.din("h0", [TOK, D])
    rowmask = C.din("rowmask", [128, 1])
    ident = C.din("ident", [128, 128])
    cmask = C.din("cmask", [128, 512])
    L = [{k: C.din(f"{k}_{l}", LAYER_SHAPES[k]) for k in LAYER_KEYS} for l in range(2)]
    out = C.dout("out", [16 * 128, D])
    xn_loc = C.dint("xn_loc", [D, TOK], BF16)
    xn_all = C.dint("xn_all", [4 * D, TOK], BF16)
    cat_loc = C.dint("cat_loc", [256, TP], BF16)
    cat_all = C.dint("cat_all", [D, TP], BF16)
    h_dram = C.dint("h_dram", [TOK, D], F32)
    xn2_dram = C.dint("xn2_dram", [D, TOK], BF16)
    vf_dram = C.dint("vf_dram", [64, TP], F32)
    base = {'ident': ident, 'rowmask': rowmask, 'cmask': cmask}
    C.qoff = C.nc.sync.snap((C.nc.sync.partition_id() % 4) * 2048)

    def gather(src, dst, key, rows):
        n = src.ap().shape[0] // rows
        for k in range(n):
            S.collective("AllGather", [src.ap()[k * rows:(k + 1) * rows, :].opt()], [dst.ap()[4 * rows * k:4 * rows * (k + 1), :].opt()], GROUPS, [key])

    if 'P' in parts:
      with C.phase("P"):
        emit_tok(C, 'P', dict(base, h_in=h0, xnT_out=xn_loc.ap()))
    if 'G' in parts:
        gather(xn_loc, xn_all, 'xn_all', 128)
    for l in range(2):
        if 'M' in parts:
          with C.phase(f"M{l}"):
            emit_mix(C, l, dict(base, xn_all=xn_all.ap(), w=L[l]['w'], gpre=L[l]['gpre'], pcols=L[l]['pcols'], pmats=L[l]['pmats'],
                                cat_loc=cat_loc.ap(), vf_dram=vf_dram.ap()))
        if 'C' in parts:
            gather(cat_loc, cat_all, 'cat_all', 32)
        if 'T' in parts:
          with C.phase(f"T1_{l}"):
            emit_tok(C, 'T1', dict(base, h_in=(h0 if l == 0 else h_dram.ap()), cat_all=cat_all.ap(), w_out=L[l]['w_out'], gpost=L[l]['gpost1'],
                                   h_out=h_dram.ap(), xnT_out=xn2_dram.ap()))
        if 'U' in parts:
          with C.phase(f"T2_{l}"):
            io = dict(base, h_in=h_dram.ap(), xnT_in=xn2_dram.ap(), w_up=L[l]['w_up'], w_dn=L[l]['w_dn'], conv=L[l]['conv'], gpre=L[l]['gpre2'],
                      gpost=L[l]['gpost2'])
            if l == 0:
                io.update(h_out=h_dram.ap(), xnT_out=xn_loc.ap())
            else:
                io.update(out_final=out)
            emit_tok(C, 'T2', io)
        if l == 0 and 'g' in parts:
            gather(xn_loc, xn_all, 'xn_all', 128)
    S.wait_collectives()
    S.finish()
    C.es.close()
    print("fused build: inst", S.n_inst, "waits", S.n_wait, flush=True)
    return C.nc


def cat_perm():
    loc = []
    for g in range(4):
        loc.append(list(range(64 * g, 64 * g + 64)) + list(range(256 + 128 * g, 256 + 128 * g + 128)) + list(range(768 + 64 * g, 768 + 64 * g + 64)))
    p = []
    for k in range(8):
        for r in range(4):
            p += loc[r][k * 32:(k + 1) * 32]
    return np.array(p)


def kernel(**I):
    I = {k: np.asarray(v) for k, v in I.items()}
    ident, cm = consts()
    perm = cat_perm()
    in_maps = []
    for core in range(8):
        b, q = core // 4, core % 4
        x = I['x'][b]
        h0 = np.zeros((TOK, D), np.float32)
        if q == 0:
            h0[112:128] = I['meta_tokens']
        else:
            h0[0:128] = x[(16 * q - 1) * 128:16 * q * 128]
        h0[128:] = x[16 * q * 128:(16 * q + 16) * 128]
        rm = np.ones((128, 1), np.float32)
        if q == 0:
            rm[:112] = 0.0
        d = {"h0": h0, "rowmask": rm, "ident": ident, "cmask": cm}
        for l in range(2):
            W, P, M = mix_inputs(l, q, I)
            vals = {'w': W, 'gpre': I['norm_mix_pre'][l], 'pcols': P, 'pmats': M, 'w_out': np.ascontiguousarray(I['w_out'][l][perm]),
                    'gpost1': I['norm_mix_post'][l], 'w_up': I['ffn_up'][l], 'w_dn': I['ffn_down'][l], 'conv': I['ffn_conv'][l],
                    'gpre2': I['norm_ffn_pre'][l], 'gpost2': I['norm_ffn_post'][l]}
            for k, v in vals.items():
                d[f"{k}_{l}"] = np.ascontiguousarray(v, dtype=np.float32)
        in_maps.append(d)
    nc = build_fused()
    res = run_bass_kernel_spmd(nc, in_maps, core_ids=list(range(8))).results
    out = np.zeros((2, 8192, D), np.float32)
    for core in range(8):
        b, q = core // 4, core % 4
        out[b, 2048 * q:2048 * q + 2048] = res[core]["out"]
    return out
```

```python
import contextlib
import numpy as np
import ml_dtypes
import concourse.bass as bass
import concourse.mybir as mybir
from concourse.bass_utils import run_bass_kernel_spmd

F32 = mybir.dt.float32
BF16 = mybir.dt.bfloat16
AF = mybir.ActivationFunctionType
ALU = mybir.AluOpType

D = 1024
T_REAL = 8208
NTT = 17
TOK = NTT * 128
OWN = 2052
TP = 8320
DFF = 2816
EPS = 1e-6


class Sched:
    EPOCH = 10 ** 9

    def __init__(self, nc, es):
        self.nc = nc
        self.es = es
        self.eng = {'pe': nc.tensor, 'act': nc.scalar, 'dve': nc.vector, 'pool': nc.gpsimd, 'sp': nc.sync}
        self.sem = {}
        self.cnt = {}
        self.nsem = 0
        for e in ['pe', 'act', 'dve', 'pool']:
            self._newsem(e)
        self.dsem = {}
        self.dcnt = {}
        self.waited = {}
        self.lastw = {}
        self.readers = {}
        self.n_inst = 0
        self.n_wait = 0

    def _newsem(self, e):
        self.nsem += 1
        self.sem[e] = self.es.enter_context(self.nc.semaphore(f"s_{e}_{self.nsem}"))
        self.cnt[e] = 0

    def _wait(self, ec, tok):
        kind, sem, val, ep = tok
        if kind == 'e' and ep == 'pe' and ec == 'pe':
            return
        key = (ec, id(sem))
        if self.waited.get(key, 0) >= val:
            return
        self.waited[key] = val
        self.eng[ec].wait_ge(sem, val)
        self.n_wait += 1

    def _deps(self, ec, reads, writes):
        for k in reads:
            t = self.lastw.get(k)
            if t is not None:
                self._wait(ec, t)
        for k in writes:
            t = self.lastw.get(k)
            if t is not None:
                self._wait(ec, t)
            for t in self.readers.get(k, {}).values():
                self._wait(ec, t)

    def _update(self, reads, writes, tok):
        for k in writes:
            self.lastw[k] = tok
            self.readers[k] = {}
        for k in reads:
            if k in writes:
                continue
            self.readers.setdefault(k, {})[(tok[0], tok[3] if tok[0] == 'e' else id(tok[1]))] = tok

    def op(self, e, reads, writes, fn, sig=True):
        self._deps(e, reads, writes)
        inst = fn(self.eng[e])
        self.n_inst += 1
        if sig:
            if self.cnt[e] >= self.EPOCH:
                self._newsem(e)
            self.cnt[e] += 1
            inst.then_inc(self.sem[e], 1)
            tok = ('e', self.sem[e], self.cnt[e], e)
        else:
            tok = ('e', self.sem[e], self.cnt[e] + 1, e)
        self._update(reads, writes, tok)
        return tok

    def dma(self, q, out, in_, reads, writes, sem, **kw):
        self._deps(q, reads, writes)
        if sem not in self.dsem:
            self.dsem[sem] = self.es.enter_context(self.nc.semaphore(f"d_{sem}"))
            self.dcnt[sem] = 0
        inst = self.eng[q].dma_start(out=out, in_=in_, **kw)
        self.n_inst += 1
        self.dcnt[sem] += 16
        inst.then_inc(self.dsem[sem], 16)
        tok = ('d', self.dsem[sem], self.dcnt[sem], q)
        self._update(reads, writes, tok)
        return tok

    def barrier(self):
        toks = []
        for e in ['pe', 'act', 'dve', 'pool']:
            if self.cnt[e] > 0:
                toks.append(('e', self.sem[e], self.cnt[e], '__'))
        for s_, h in self.dsem.items():
            toks.append(('d', h, self.dcnt[s_], '__'))
        for ec in ['pe', 'act', 'dve', 'pool', 'sp']:
            for t in toks:
                self._wait(ec, t)
        self.lastw = {}
        self.readers = {}

    def collective(self, kind, ins, outs, groups, writes):
        self.ncc = getattr(self, 'ncc', 0) + 1
        sem = self.es.enter_context(self.nc.semaphore(f"cc{self.ncc}"))
        inst = self.nc.gpsimd.collective_compute(kind, ALU.bypass, replica_groups=groups, ins=ins, outs=outs)
        inst.then_inc(sem)
        self.n_inst += 1
        tok = ('d', sem, 1, 'pool')
        self.cctoks = getattr(self, 'cctoks', []) + [tok]
        return tok

    def wait_collectives(self):
        for ec in ['pe', 'act', 'dve', 'pool', 'sp']:
            for t in getattr(self, 'cctoks', []):
                self._wait(ec, t)

    def finish(self, q='sp'):
        for s, h in self.dsem.items():
            self.eng[q].wait_ge(h, self.dcnt[s])
        for e in ['pe', 'act', 'dve', 'pool']:
            if self.cnt[e] > 0:
                self.eng[q].wait_ge(self.sem[e], self.cnt[e])


class Ctx:
    def __init__(self, name):
        self.nc = bass.Bass("TRN2", target_bir_lowering=False)
        self.es = contextlib.ExitStack()
        self.S = Sched(self.nc, self.es)
        self.rr = 0
        self.pid = 0
        self.pes = self.es

    def din(self, name, shape, dt=F32):
        return self.nc.dram_tensor(name, list(shape), dt, kind="ExternalInput").ap()

    def dout(self, name, shape, dt=F32):
        return self.nc.dram_tensor(name, list(shape), dt, kind="ExternalOutput").ap()

    def sb(self, name, shape, dt=F32):
        return self.pes.enter_context(self.nc.sbuf_tensor(f'sb{self.pid}_' + name, list(shape), dt))

    def ps(self, name, shape, dt=F32):
        return self.pes.enter_context(self.nc.psum_tensor(f'ps{self.pid}_' + name, list(shape), dt))

    def dint(self, name, shape, dt=F32):
        return self.nc.dram_tensor(name, list(shape), dt)

    @contextlib.contextmanager
    def phase(self, name):
        self.pid += 1
        self.S.wait_collectives()
        self.pes = contextlib.ExitStack()
        for a in ('wstage', 'wsi'):
            if hasattr(self, a):
                delattr(self, a)
        try:
            yield
        finally:
            self.S.barrier()
            self.pes.close()
            self.pes = self.es

    def anyeng(self, choices=('dve', 'pool')):
        self.rr += 1
        return choices[self.rr % len(choices)]


def load_cast_weight(C, w_dram, nk, ncol, dst_bf, key, gcol=None, piece=1024, sem='wld'):
    S = C.S
    if not hasattr(C, 'wstage'):
        C.wstage = [C.sb(f"wstage{i}", [128, 1408], F32) for i in range(3)]
        C.wsi = 0
    wv = w_dram.rearrange("(c p) n -> p c n", p=128)
    for c in range(nk):
        for n0 in range(0, ncol, piece):
            n1 = min(ncol, n0 + piece)
            si = C.wsi % 3
            C.wsi += 1
            st = C.wstage[si]
            S.dma('sp', st[:, 0:n1 - n0], wv[:, c, n0:n1], [], [f'wst{si}'], f'wld{si}')
            e = C.anyeng(('dve', 'act'))
            rd = [f'wst{si}'] + ([] if gcol is None else ['gcols'])
            if gcol is None:
                if e == 'act':
                    S.op(e, rd, [key], lambda en, st=st, c=c, n0=n0, n1=n1: en.copy(out=dst_bf[:, c, n0:n1], in_=st[:, 0:n1 - n0]))
                else:
                    S.op(e, rd, [key], lambda en, st=st, c=c, n0=n0, n1=n1: en.tensor_copy(out=dst_bf[:, c, n0:n1], in_=st[:, 0:n1 - n0]))
            else:
                if e == 'act':
                    S.op(e, rd, [key], lambda en, st=st, c=c, n0=n0, n1=n1: en.mul(out=dst_bf[:, c, n0:n1], in_=st[:, 0:n1 - n0], mul=gcol[:, c:c + 1]))
                else:
                    S.op(e, rd, [key], lambda en, st=st, c=c, n0=n0, n1=n1: en.tensor_scalar(out=dst_bf[:, c, n0:n1], in0=st[:, 0:n1 - n0], scalar1=gcol[:, c:c + 1], scalar2=None, op0=ALU.mult))


def rows_to_cols(C, rows_sb, nrows, length, dst, ident, key_rows, key_dst, ps, ps_key):
    S = C.S
    nch = length // 128
    for c in range(nch):
        S.op('pe', [key_rows, 'ident'], [ps_key],
             lambda en, c=c: en.matmul(ps[:, c * nrows:(c + 1) * nrows], lhsT=rows_sb[0:nrows, c * 128:(c + 1) * 128], rhs=ident[0:nrows, 0:nrows], start=True, stop=True))
    S.op('dve', [ps_key], [key_dst], lambda en: en.tensor_copy(out=dst[:, 0:nch, 0:nrows], in_=ps[:, 0:nch * nrows].rearrange("p (c r) -> p c r", r=nrows)))


def emit_rstd(C, src_ap_list, ss, rs, keys_src, slot, tag):
    S = C.S
    junk = C.junk
    for j, (ap, k) in enumerate(zip(src_ap_list, keys_src)):
        n = ap.shape[-1]
        S.op('act', [k], [f'junk', f'ss{tag}{slot}_{j}'],
             lambda en, ap=ap, j=j, n=n: en.activation(out=junk[:, 0:n], in_=ap, func=AF.Square, accum_out=ss[:, j:j + 1]))
    if len(src_ap_list) == 2:
        S.op('dve', [f'ss{tag}{slot}_0', f'ss{tag}{slot}_1'], [f'ss{tag}{slot}_0'],
             lambda en: en.tensor_tensor(out=ss[:, 0:1], in0=ss[:, 0:1], in1=ss[:, 1:2], op=ALU.add))
    S.op('dve', [f'ss{tag}{slot}_0'], [f'rs{tag}{slot}'],
         lambda en: en.tensor_scalar(out=rs[:, 0:1], in0=ss[:, 0:1], scalar1=1.0 / D, scalar2=EPS, op0=ALU.mult, op1=ALU.add))
    S.op('act', [f'rs{tag}{slot}'], [f'rs{tag}{slot}'], lambda en: en.activation(out=rs[:, 0:1], in_=rs[:, 0:1], func=AF.Sqrt))
    S.op('dve', [f'rs{tag}{slot}'], [f'rs{tag}{slot}'], lambda en: en.reciprocal(out=rs[:, 0:1], in_=rs[:, 0:1]))


def emit_norm_xnT(C, h_ap, h_key, slot, xnT_dram, col0, ncols=128, mask0=False):
    S = C.S
    ss = C.ssn[slot]
    rs = C.rsn[slot]
    xnb = C.xnb[slot]
    tp = C.tp[slot]
    xst = C.xst[slot]
    emit_rstd(C, [h_ap], ss, rs, [h_key], slot, 'n')
    if mask0:
        S.op('dve', [f'rsn{slot}', 'rowmask'], [f'rsn{slot}'], lambda en: en.tensor_tensor(out=rs[:, 0:1], in0=rs[:, 0:1], in1=C.rowmask[:, 0:1], op=ALU.mult))
    S.op('dve', [h_key, f'rsn{slot}'], [f'xnb{slot}'],
         lambda en: en.tensor_scalar(out=xnb[:], in0=h_ap, scalar1=rs[:, 0:1], scalar2=None, op0=ALU.mult))
    for c in range(8):
        S.op('pe', [f'xnb{slot}', 'identb'], [f'tp{slot}'],
             lambda en, c=c: en.transpose(out=tp[:, c, :], in_=xnb[:, c * 128:(c + 1) * 128], identity=C.identb[:]), sig=(c == 7))
    S.op('act', [f'tp{slot}'], [f'xst{slot}'], lambda en: en.copy(out=xst[:], in_=tp[:]))
    xv = xnT_dram.rearrange("(c p) t -> p c t", p=128)
    S.dma('sp', xv[:, :, col0:col0 + ncols], xst[:, :, 0:ncols], [f'xst{slot}'], ['xn2'], f'xo{slot}')


def alloc_norm_bufs(C, io):
    C.junk = C.sb("junk", [128, 1024], BF16)
    C.ssn = [C.sb(f"ssn{i}", [128, 2], F32) for i in range(2)]
    C.rsn = [C.sb(f"rsn{i}", [128, 1], F32) for i in range(2)]
    C.xnb = [C.sb(f"xnb{i}", [128, 1024], BF16) for i in range(2)]
    C.tp = [C.ps(f"tp{i}", [128, 8, 128], BF16) for i in range(2)]
    C.xst = [C.sb(f"xst{i}", [128, 8, 128], BF16) for i in range(2)]
    C.identb = C.sb("identb", [128, 128], BF16)
    C.identf = C.sb("identf", [128, 128], F32)
    ident_d = io['ident']
    C.rowmask = C.sb("rowmask", [128, 1], F32)
    C.S.dma('sp', C.rowmask[:], io['rowmask'][:, :], [], ['rowmask'], 'su_rowmask')
    C.S.dma('sp', C.identf[:], ident_d[:, :], [], ['ident'], 'su_ident')
    C.S.op('dve', ['ident'], ['identb'], lambda en: en.tensor_copy(out=C.identb[:], in_=C.identf[:]))


def emit_tok(C, mode, io):
    S = C.S
    h_in = io['h_in']
    xnT_out = io.get('xnT_out')
    h_out = io.get('h_out')
    out_final = io.get('out_final')
    gpost_d = io.get('gpost')
    alloc_norm_bufs(C, io)
    ht = [C.sb(f"ht{i}", [128, D], F32) for i in range(2)]
    if mode != 'P':
        gpost = C.sb("gpost_bc", [128, D], F32)
        S.dma('sp', gpost[:], gpost_d.partition_broadcast(128), [], ['gpost'], 'su_gpost')
        tmpn = [C.sb("tmpn0", [128, D], F32)] * 2
        ssm = [C.sb(f"ssm{i}", [128, 2], F32) for i in range(2)]
        rsm = [C.sb(f"rsm{i}", [128, 1], F32) for i in range(2)]
        mpA = [C.ps(f"mpA{i}", [128, 512], F32) for i in range(1)]
        mpB = [C.ps(f"mpB{i}", [128, 512], F32) for i in range(1)]

    def post_norm_residual(slot, i, hkey):
        emit_rstd(C, [mpA[0][:], mpB[0][:]], ssm[slot], rsm[slot], ['mpA', 'mpB'], slot, 'm')
        for half, mp, k in ((0, mpA[0], 'mpA'), (1, mpB[0], 'mpB')):
            S.op('dve', [k, f'rsm{slot}', 'gpost'], [f'tmpn_{half}'],
                 lambda en, half=half, mp=mp: en.scalar_tensor_tensor(out=tmpn[slot][:, half * 512:(half + 1) * 512], in0=mp[:], scalar=rsm[slot][:, 0:1],
                                                                      in1=gpost[:, half * 512:(half + 1) * 512], op0=ALU.mult, op1=ALU.mult))
        S.op('pool', ['tmpn_0', 'tmpn_1', hkey], [hkey],
             lambda en: en.tensor_tensor(out=ht[slot][:], in0=ht[slot][:], in1=tmpn[slot][:], op=ALU.add))
        if h_out is not None:
            S.dma('sp', h_out[i * 128:(i + 1) * 128, :], ht[slot][:], [hkey], [f'hd{i}'], f'ho{slot}')
        if out_final is not None and i >= 1:
            S.dma('sp', out_final[(i - 1) * 128:i * 128, :], ht[slot][:], [hkey], [], f'ho{slot}')

    issued = set()

    def load_h(i):
        if i >= NTT or ('h', i) in issued:
            return
        issued.add(('h', i))
        S.dma('sp', ht[i % 2][:], h_in[i * 128:(i + 1) * 128, :], [f'hd{i}'], [f'ht{i % 2}'], f'hl{i % 2}')

    if mode == 'P':
        for i in range(NTT):
            slot = i % 2
            load_h(i)
            load_h(i + 1)
            emit_norm_xnT(C, ht[slot][:], f'ht{slot}', slot, xnT_out, i * 128, mask0=(i == 0))
    elif mode == 'T1':
        catT = io['cat_all']
        w_out = io['w_out']
        qoff = C.qoff
        wo_bf = C.sb("wo_bf", [128, 8, D], BF16)
        load_cast_weight(C, w_out, 8, D, wo_bf, 'wo')
        ct = [C.sb(f"ct{i}", [128, 8, 128], BF16) for i in range(2)]
        cv = catT.rearrange("(c p) t -> p c t", p=128)
        for i in range(NTT):
            slot = i % 2
            def load_c(j):
                if j >= NTT or ('c', j) in issued:
                    return
                issued.add(('c', j))
                S.dma('sp', ct[j % 2][:], catT[:, bass.ds(qoff + j * 128, 128)].rearrange("(c p) t -> p c t", p=128), ['cat_all'], [f'ct{j % 2}'], f'cl{j % 2}')
            load_h(i)
            load_c(i)
            load_h(i + 1)
            load_c(i + 1)
            for half, mp, k in ((0, mpA[0], 'mpA'), (1, mpB[0], 'mpB')):
                for c in range(8):
                    S.op('pe', [f'ct{slot}', 'wo'], [k],
                         lambda en, c=c, half=half, mp=mp: en.matmul(mp[:], lhsT=ct[slot][:, c, :], rhs=wo_bf[:, c, half * 512:(half + 1) * 512], start=(c == 0), stop=(c == 7)),
                         sig=(c == 7))
            post_norm_residual(slot, i, f'ht{slot}')
            emit_norm_xnT(C, ht[slot][:], f'ht{slot}', slot, xnT_out, i * 128, mask0=(i == 0))
    else:
        xnT_in = io['xnT_in']
        w_up = io['w_up']
        w_dn = io['w_dn']
        conv_d = io['conv']
        gpre_d = io['gpre']
        NCH = 2 * DFF // 128
        load_cast_weight
        C.wstage = [C.sb(f"wstage{i}", [128, 1408], F32) for i in range(3)]
        C.wsi = 0
        pcol = mpA[0]
        convc = C.sb("convc", [128, NCH, 3], F32)
        gcol = C.sb("gcol", [128, 8, 1], F32)
        for pi in range(4):
            S.dma('sp', C.wstage[0][0:3, :], conv_d[:, pi * 1408:(pi + 1) * 1408], [], ['wst0'], 'wld0')
            for c in range(11):
                S.op('pe', ['wst0', 'ident'], ['mpA'],
                     lambda en, c=c: en.matmul(pcol[:, c * 3:(c + 1) * 3], lhsT=C.wstage[0][0:3, c * 128:(c + 1) * 128], rhs=C.identf[0:3, 0:3], start=True, stop=True))
            S.op('dve', ['mpA'], ['convc'], lambda en, pi=pi: en.tensor_copy(out=convc[:, pi * 11:(pi + 1) * 11, :], in_=pcol[:, 0:33].rearrange("p (c r) -> p c r", r=3)))
        S.dma('sp', C.wstage[1][0:1, 0:D], gpre_d.rearrange("(o n) -> o n", o=1), [], ['wst1'], 'wld1')
        for c in range(8):
            S.op('pe', ['wst1', 'ident'], ['mpA'],
                 lambda en, c=c: en.matmul(pcol[:, c:c + 1], lhsT=C.wstage[1][0:1, c * 128:(c + 1) * 128], rhs=C.identf[0:1, 0:1], start=True, stop=True))
        S.op('dve', ['mpA'], ['gcols'], lambda en: en.tensor_copy(out=gcol[:, :, 0], in_=pcol[:, 0:8]))
        wup_bf = C.sb("wup_bf", [128, 8, 2 * DFF], BF16)
        wdn_bf = C.sb("wdn_bf", [128, DFF // 128, D], BF16)
        load_cast_weight(C, w_up, 8, 2 * DFF, wup_bf, 'wup', gcol=gcol[:].rearrange("p c o -> p (c o)"), piece=1408)
        load_cast_weight(C, w_dn, DFF // 128, D, wdn_bf, 'wdn')
        NB = 256
        xin = [C.sb(f"xin{i}", [128, 8, NB], BF16) for i in range(2)]
        xv = xnT_in.rearrange("(c p) t -> p c t", p=128)
        fps = [C.ps(f"fps{i}", [128, 512], F32) for i in range(4)]
        fraw = [C.sb(f"fraw{i}", [128, NB + 2], F32) for i in range(4)]
        fcv = [C.sb(f"fcv{i}", [128, NB], F32) for i in range(4)]
        gel = [C.sb(f"gel{i}", [128, NB], F32) for i in range(2)]
        halo = C.sb("halo", [128, NCH, 2], F32)
        ptmp = C.sb("ptmp", [128, NB], F32)
        S.op('pool', [], [f'halo{ch}' for ch in range(NCH)], lambda en: en.memset(halo[:], 0.0))
        hid = C.sb("hid", [128, DFF // 128, NB], BF16)
        blocks = [(b0, min(NB, TOK - b0)) for b0 in range(0, TOK, NB)]
        it = 0
        for bi, (b0, nb) in enumerate(blocks):
            xs = bi % 2
            def load_x(bj):
                if bj >= len(blocks) or ('x', bj) in issued:
                    return
                issued.add(('x', bj))
                b0_, nb_ = blocks[bj]
                S.dma('sp', xin[bj % 2][:, :, 0:nb_], xv[:, :, b0_:b0_ + nb_], ['xn2'], [f'xin{bj % 2}'], f'xl{bj % 2}')
            load_x(bi)
            load_x(bi + 1)
            for cc in range(DFF // 128):
                fc = []
                for wi, ch in enumerate((cc, DFF // 128 + cc)):
                    fi = (it % 2) * 2 + wi
                    pk = f'fps{fi}'
                    for c in range(8):
                        S.op('pe', [f'xin{xs}', 'wup'], [pk],
                             lambda en, c=c, ch=ch, fi=fi: en.matmul(fps[fi][:, 0:nb], lhsT=wup_bf[:, c, ch * 128:(ch + 1) * 128], rhs=xin[xs][:, c, 0:nb], start=(c == 0), stop=(c == 7)),
                             sig=(c == 7))
                    S.op('pool', [f'halo{ch}'], [f'fraw{fi}h'], lambda en, fi=fi, ch=ch: en.tensor_copy(out=fraw[fi][:, 0:2], in_=halo[:, ch, :]))
                    S.op('act', [pk], [f'fraw{fi}'], lambda en, fi=fi: en.copy(out=fraw[fi][:, 2:2 + nb], in_=fps[fi][:, 0:nb]))
                    if wi == 0:
                        S.op('act', [pk, 'convc'], [f'fcv{fi}'], lambda en, fi=fi, ch=ch: en.mul(out=fcv[fi][:, 0:nb], in_=fps[fi][:, 0:nb], mul=convc[:, ch, 2:3]))
                        taps = (0, 1)
                    else:
                        S.op('dve', [f'fraw{fi}', 'convc'], [f'fcv{fi}'],
                             lambda en, fi=fi, ch=ch: en.tensor_scalar(out=fcv[fi][:, 0:nb], in0=fraw[fi][:, 2:2 + nb], scalar1=convc[:, ch, 2:3], scalar2=None, op0=ALU.mult))
                        taps = (0, 1)
                    for k in taps:
                        S.op('dve', [f'fraw{fi}', f'fraw{fi}h', 'convc', f'fcv{fi}'], [f'fcv{fi}'],
                             lambda en, fi=fi, ch=ch, k=k: en.scalar_tensor_tensor(out=fcv[fi][:, 0:nb], in0=fraw[fi][:, k:k + nb], scalar=convc[:, ch, k:k + 1],
                                                                                  in1=fcv[fi][:, 0:nb], op0=ALU.mult, op1=ALU.add))
                    S.op('pool', [f'fraw{fi}'], [f'halo{ch}'], lambda en, fi=fi, ch=ch: en.tensor_copy(out=halo[:, ch, :], in_=fraw[fi][:, nb:nb + 2]))
                    fc.append(fi)
                gs = it % 2
                S.op('act', [f'fcv{fc[0]}'], [f'gel{gs}'], lambda en, gs=gs, f0=fc[0]: en.activation(out=gel[gs][:, 0:nb], in_=fcv[f0][:, 0:nb], func=AF.Gelu_apprx_tanh))
                S.op('dve', [f'gel{gs}', f'fcv{fc[1]}'], [f'hid{cc}'],
                     lambda en, gs=gs, f1=fc[1], cc=cc: en.tensor_tensor(out=hid[:, cc, 0:nb], in0=gel[gs][:, 0:nb], in1=fcv[f1][:, 0:nb], op=ALU.mult))
                it += 1
            for ti in range(nb // 128):
                i = (b0 // 128) + ti
                slot = i % 2
                load_h(i)
                load_h(i + 1)
                for half, mp, k in ((0, mpA[0], 'mpA'), (1, mpB[0], 'mpB')):
                    for cc in range(DFF // 128):
                        S.op('pe', [f'hid{cc}', 'wdn'], [k],
                             lambda en, cc=cc, half=half, mp=mp, ti=ti: en.matmul(mp[:], lhsT=hid[:, cc, ti * 128:(ti + 1) * 128], rhs=wdn_bf[:, cc, half * 512:(half + 1) * 512],
                                                                           start=(cc == 0), stop=(cc == DFF // 128 - 1)),
                             sig=(cc == DFF // 128 - 1))
                post_norm_residual(slot, i, f'ht{slot}')
                if xnT_out is not None:
                    emit_norm_xnT(C, ht[slot][:], f'ht{slot}', slot, xnT_out, i * 128, mask0=(i == 0))


NCOLS = 1280
CG = {'xa': 0, 'ya': 64, 'r': 128, 'k': 192, 'v': 256, 'wd': 320, 'ad': 384, 'gd': 448, 'qA': 576, 'kA': 640, 'qB': 704, 'kB': 768,
      'vAB': 832, 'f': 960, 'vall': 1024}
GN_EPS = 64e-5
DECAY_C = -0.6065306597126334


def emit_mix(C, layer, io):
    S = C.S
    NBLK = TP // 128
    xn_all = io['xn_all']
    w_d = io['w']
    gpre_d = io['gpre']
    pcols_d = io['pcols']
    pmats_d = io['pmats']
    cmask_d = io['cmask']
    ident_d = io['ident']
    oT = io['cat_loc']
    vown = io['vf_dram']
    vfirst = io['vf_dram']
    debug = False
    identf = C.sb("identf", [128, 128], F32)
    pcols = C.sb("pcols", [128, 32], F32)
    pmats = C.sb("pmats", [128, 448], F32)
    cmask = C.sb("cmask", [128, 512], F32)
    S.dma('sp', identf[:], ident_d[:, :], [], ['ident'], 'su0')
    S.dma('sp', pcols[:], pcols_d[:, :], [], ['pcols'], 'su1')
    S.dma('sp', pmats[:], pmats_d[:, :], [], ['pmats'], 'su2')
    S.dma('sp', cmask[:], cmask_d[:, :], [], ['cmask'], 'su3')
    pb = [C.ps(f"pb{i}", [128, 512], F32) for i in range(8)]
    C.wstage = [C.sb(f"wstage{i}", [128, 1408], F32) for i in range(3)]
    C.wsi = 0
    gcol = C.sb("gcol", [128, 8], F32)
    S.dma('sp', C.wstage[1][0:1, 0:D], gpre_d.rearrange("(o n) -> o n", o=1), [], ['wst1'], 'wld1')
    for c in range(8):
        S.op('pe', ['wst1', 'ident'], ['pb0'],
             lambda en, c=c: en.matmul(pb[0][:, c:c + 1], lhsT=C.wstage[1][0:1, c * 128:(c + 1) * 128], rhs=identf[0:1, 0:1], start=True, stop=True))
    S.op('dve', ['pb0'], ['gcols'], lambda en: en.tensor_copy(out=gcol[:], in_=pb[0][:, 0:8]))
    w_bf = C.sb("w_bf", [128, 8, NCOLS], BF16)
    load_cast_weight(C, w_d, 8, NCOLS, w_bf, 'w', gcol=gcol[:], piece=1280)
    causb = C.sb("causb", [128, 128], BF16)
    S.op('dve', ['cmask'], ['causb'], lambda en: en.tensor_copy(out=causb[:], in_=cmask[:, 0:128]))
    ones = C.sb("ones", [128, 128], F32)
    S.op('pool', [], ['ones'], lambda en: en.memset(ones[:], 1.0))
    MASK2 = cmask[0:64, 128:256]
    MASKL = cmask[0:64, 256:320]
    RMASK = cmask[0:64, 320:448]
    pc = lambda j, n=64: pcols[0:n, j:j + 1]
    dcol = C.sb("dcol", [128, 12], F32)
    S.op('act', ['pcols'], ['dcol'], lambda en: en.activation(out=dcol[0:64, 0:1], in_=pc(7), func=AF.Sigmoid))
    S.op('act', ['dcol'], ['dcol'], lambda en: en.activation(out=dcol[0:64, 0:1], in_=dcol[0:64, 0:1], func=AF.Ln))
    S.op('dve', ['dcol'], ['dcol'], lambda en: en.tensor_scalar(out=dcol[0:64, 1:2], in0=dcol[0:64, 0:1], scalar1=16.0, scalar2=None, op0=ALU.mult))
    S.op('dve', ['dcol'], ['dcol'], lambda en: en.tensor_scalar(out=dcol[0:64, 0:1], in0=dcol[0:64, 0:1], scalar1=8.0, scalar2=None, op0=ALU.mult))
    S.op('dve', ['pcols', 'dcol'], ['dcol'], lambda en: en.tensor_scalar(out=dcol[0:64, 2:3], in0=pcols[0:64, 8:9], scalar1=-1.0, scalar2=None, op0=ALU.mult))
    S.op('dve', ['pcols', 'dcol'], ['dcol'], lambda en: en.tensor_scalar(out=dcol[0:64, 3:4], in0=pc(18), scalar1=-1.0, scalar2=1.0, op0=ALU.mult, op1=ALU.add))
    CL, CL2, NFB, OMKA = dcol[0:64, 0:1], dcol[0:64, 1:2], dcol[0:33, 2:3], dcol[0:64, 3:4]
    for jj, src in enumerate((5, 6, 15, 16, 24)):
        S.op('dve', ['pcols', 'dcol'], ['dcol'], lambda en, jj=jj, src=src: en.tensor_scalar(out=dcol[0:64, 4 + jj:5 + jj], in0=pcols[0:64, src:src + 1], scalar1=-1.0, scalar2=None, op0=ALU.mult))
    NB5, NB6, NB15, NB16, NB24 = (dcol[0:64, 4 + jj:5 + jj] for jj in range(5))

    def sigm(out_ap, out_key, in_ap, in_keys, nbias=None, scale=1.0):
        if nbias is None:
            S.op('act', in_keys, [out_key], lambda en: en.activation(out=out_ap, in_=in_ap, func=AF.Exp, scale=-scale))
        else:
            S.op('act', in_keys + ['dcol'], [out_key], lambda en: en.activation(out=out_ap, in_=in_ap, func=AF.Exp, scale=-scale, bias=nbias))
        S.op('dve', [out_key], [out_key], lambda en: en.tensor_scalar(out=out_ap, in0=out_ap, scalar1=1.0, scalar2=None, op0=ALU.add))
        S.op('dve', [out_key], [out_key], lambda en: en.reciprocal(out=out_ap, in_=out_ap))
    MU4 = C.sb("MU4", [64, 4, 128], F32)
    MUB = C.sb("MUB", [128, 4, 128], F32)
    S.op('pool', [], ['MUB'], lambda en: en.memset(MUB[:], 0.0))
    for g in range(4):
        S.op('dve', ['pcols'], ['MU4'], lambda en, g=g: en.tensor_copy(out=MU4[:, g, :], in_=pcols[0:64, 9 + g:10 + g].to_broadcast([64, 128])))
    S.op('dve', ['pcols', 'MUB'], ['MUB'], lambda en: en.tensor_copy(out=MUB[0:64, 0, :], in_=pcols[0:64, 13:14].to_broadcast([64, 128])))
    for g, j in ((1, 14), (2, 22), (3, 23)):
        S.op('dve', ['pcols', 'MUB'], ['MUB'], lambda en, g=g, j=j: en.tensor_copy(out=MUB[:, g, :], in_=pcols[:, j:j + 1].to_broadcast([128, 128])))
    KT = [C.sb(f"KT{h}", [64, TP], BF16) for h in range(2)]
    VA = [C.sb(f"VA{h}", [128, NBLK, 65], BF16) for h in range(2)]
    for h in range(2):
        S.op('pool', [], [f'VA{h}'], lambda en, h=h: en.memset(VA[h][:], 1.0))
    ctm = C.sb("ctm", [128, NBLK, 2], F32)
    cb = C.sb("cbrow", [33, 129], F32)
    S.op('dve', [], ['cb'], lambda en: en.memset(cb[:], 0.0))
    H = C.sb("Hst", [64, 64], F32)
    S.op('dve', [], ['H'], lambda en: en.memset(H[:], 0.0))
    hl = C.sb("hlru", [64, 129], F32)
    S.op('dve', [], ['hl'], lambda en: en.memset(hl[:], 0.0))
    xa_raw = C.sb("xa_raw", [64, 131], F32)
    S.op('dve', [], ['xa_raw'], lambda en: en.memset(xa_raw[:], 0.0))
    raw4 = C.sb("raw4", [64, 4, 129], F32)
    S.op('dve', [], ['raw4'], lambda en: en.memset(raw4[:], 0.0))
    rawB = C.sb("rawB", [128, 4, 129], F32)
    S.op('dve', [], ['rawB'], lambda en: en.memset(rawB[:], 0.0))
    xb = [C.sb(f"xb{i}", [128, 8, 128], BF16) for i in range(2)]
    ost = [C.sb(f"ost{i}", [64, 4, 128], BF16) for i in range(2)]
    xv = xn_all.rearrange("(c r p) t -> r p c t", r=4, p=128)
    ov = oT.rearrange("(g p) t -> p g t", p=64)
    cnt = [0]

    def T(shape, dt=F32, name=None):
        cnt[0] += 1
        return C.sb(name or f"t{cnt[0]}", shape, dt)

    u = T([64, 128]); rg = T([64, 128]); ig = T([64, 128]); aa = T([64, 128]); a2 = T([64, 128]); bbv = T([64, 128]); gy = T([64, 128]); ysb = T([64, 128])
    qT = [T([64, 128], BF16) for _ in range(2)]
    e1 = T([33, 128]); rj = T([128, 4]); Bn = [T([128, NBLK]) for _ in range(2)]
    Pm = [T([128, 128], BF16) for _ in range(3)]
    Osb = T([65, 2, 128]); rec = T([64, 128])
    d4 = T([64, 4, 128]); mx4 = T([64, 4, 128]); dB = T([128, 4, 128]); mxB = T([128, 4, 128])
    wdt = T([64, 128]); sg = T([128, 128]); logw = T([64, 128]); av = T([64, 128]); gg = T([64, 128])
    kkr = T([64, 128]); sq = T([64, 128]); nrm = T([64, 128]); kk = T([64, 128]); k2 = T([64, 128]); tmpk = T([64, 128]); bneg = T([64, 128])
    v2 = T([64, 128]); vd = T([32, 128]); sv = T([64, 128]); vf = T([64, 128])
    L = T([64, 128]); eL = T([64, 128]); eLn = T([64, 128]); eLx = T([64, 128]); eLC = T([64, 128])
    kr = T([64, 2, 2, 64]); kt = T([64, 128]); btn = T([64, 128]); khat = T([64, 128]); bhn = T([64, 128])
    tmC = [T([64, 3, 64]) for _ in range(2)]; ABkC = [T([64, 128]) for _ in range(2)]; ABbC = [T([64, 128]) for _ in range(2)]
    PqC = [[T([64, 2, 64]) for _ in range(2)] for _ in range(2)]; WC = [T([64, 64]) for _ in range(2)]
    Xsb = T([64, 64]); Usb = T([64, 64]); osb = T([64, 128])
    mean = T([64, 128]); cen = T([64, 128]); msq = T([64, 128]); var = T([64, 128]); rk = T([64, 128]); bon = T([64, 128])
    r_ = mx4[:, 0, :]; k_ = mx4[:, 1, :]; v_ = mx4[:, 2, :]; wd_ = mx4[:, 3, :]
    pm_ = lambda j, n=64, k=64: pmats[0:k, j * 64:j * 64 + n]

    def proj(bank, grp, col0, m, xs, last=True):
        for c in range(8):
            S.op('pe', [f'xb{xs}', 'w'], [f'pb{bank}'],
                 lambda en, c=c: en.matmul(pb[bank][0:m, grp * 128:(grp + 1) * 128], lhsT=w_bf[:, c, col0:col0 + m], rhs=xb[xs][:, c, :], start=(c == 0), stop=(c == 7)),
                 sig=(c == 7))

    def dv(reads, writes, fn, e='dve'):
        S.op(e, reads, writes, fn)

    for n in range(NBLK):
        t0 = n * 128
        xs = n % 2
        os_ = ost[xs]
        def load_xb(nn):
            rk_, sl_ = (0, 0) if nn == 0 else ((nn - 1) // 16, (nn - 1) % 16 + 1)
            S.dma('sp', xb[nn % 2][:], xv[rk_, :, :, sl_ * 128:(sl_ + 1) * 128], ['xn_all'], [f'xb{nn % 2}'], f'xl{nn % 2}')
        if n == 0:
            load_xb(0)
        for g, nm in enumerate(('r', 'k', 'v', 'wd')):
            proj(0, g, CG[nm], 64, xs)
        proj(1, 0, CG['ad'], 64, xs)
        proj(1, 1, CG['gd'], 128, xs)
        if layer == 1:
            proj(1, 2, CG['vall'], 128, xs)
            proj(1, 3, CG['vall'] + 128, 128, xs)
        proj(2, 0, CG['xa'], 64, xs)
        proj(2, 1, CG['ya'], 64, xs)
        proj(2, 2, CG['f'], 33, xs)
        for g, nm in enumerate(('qA', 'kA', 'qB', 'kB')):
            proj(3, g, CG[nm], 64, xs)
        for c in range(8):
            S.op('pe', [f'xb{xs}', 'w'], ['pb2'],
                 lambda en, c=c: en.matmul(pb[2][:, 384:512], lhsT=xb[xs][:, c, :], rhs=w_bf[:, c, CG['vAB']:CG['vAB'] + 128], start=(c == 0), stop=(c == 7)), sig=(c == 7))
        if n + 1 < NBLK:
            load_xb(n + 1)
        S.op('act', ['pb0'], ['raw4'], lambda en: en.copy(out=raw4[:, :, 1:129], in_=pb[0][0:64, :].rearrange("p (g t) -> p g t", g=4)))
        ng = 4 if layer == 1 else 2
        S.op('act', ['pb1'], ['rawB'], lambda en: en.copy(out=rawB[:, 0:ng, 1:129], in_=pb[1][:, 0:ng * 128].rearrange("p (g t) -> p g t", g=ng)))
        S.op('act', ['pb2'], ['xa_raw'], lambda en: en.copy(out=xa_raw[:, 3:131], in_=pb[2][0:64, 0:128]))
        S.op('act', ['pb2'], ['ysb'], lambda en: en.copy(out=ysb[:], in_=pb[2][0:64, 128:256]))
        dv(['ysb'], ['gy'], lambda en: en.tensor_tensor(out=gy[:], in0=ysb[:], in1=ysb[:], op=ALU.mult))
        dv(['gy'], ['gy'], lambda en: en.tensor_scalar(out=gy[:], in0=gy[:], scalar1=0.044715, scalar2=1.0, op0=ALU.mult, op1=ALU.add))
        dv(['gy', 'ysb'], ['gy'], lambda en: en.tensor_tensor(out=gy[:], in0=gy[:], in1=ysb[:], op=ALU.mult))
        sigm(gy[:], 'gy', gy[:], ['gy'], scale=1.5957691216057308)
        dv(['gy', 'ysb'], ['gy'], lambda en: en.tensor_tensor(out=gy[:], in0=gy[:], in1=ysb[:], op=ALU.mult))
        S.op('act', ['pb2', 'dcol'], ['e1'], lambda en: en.activation(out=e1[:], in_=pb[2][0:33, 256:384], func=AF.Exp, bias=NFB, scale=-1.0))
        for h in range(2):
            dv(['pb3'], [f'qT{h}'], lambda en, h=h: en.tensor_scalar(out=qT[h][:], in0=pb[3][0:64, (2 * h) * 128:(2 * h + 1) * 128], scalar1=0.125, scalar2=None, op0=ALU.mult))
            S.op('act', ['pb3'], [f'KT{h}'], lambda en, h=h: en.copy(out=KT[h][:, t0:t0 + 128], in_=pb[3][0:64, (2 * h + 1) * 128:(2 * h + 2) * 128]))
            dv(['pb2'], [f'VA{h}'], lambda en, h=h: en.tensor_copy(out=VA[h][:, n, 0:64], in_=pb[2][:, 384 + h * 64:384 + (h + 1) * 64]))
            if n == 0:
                dv([f'VA{h}'], [f'VA{h}'], lambda en, h=h: en.memset(VA[h][0:112, 0, :], 0.0))
        def lru_gen():
            dv(['xa_raw', 'pcols'], ['u'], lambda en: en.tensor_scalar(out=u[:], in0=xa_raw[:, 0:128], scalar1=pc(0), scalar2=pc(4), op0=ALU.mult, op1=ALU.add))
            for k in (1, 2, 3):
                dv(['xa_raw', 'pcols', 'u'], ['u'], lambda en, k=k: en.scalar_tensor_tensor(out=u[:], in0=xa_raw[:, k:k + 128], scalar=pc(k), in1=u[:], op0=ALU.mult, op1=ALU.add))
            dv(['xa_raw', 'u'], ['xa_raw'], lambda en: en.tensor_copy(out=xa_raw[:, 0:3], in_=xa_raw[:, 128:131]))
            yield
            S.op('pe', ['u', 'pmats'], ['pb2'], lambda en: en.matmul(pb[2][0:64, 0:128], lhsT=pm_(0), rhs=u[:], start=True, stop=True))
            S.op('pe', ['u', 'pmats'], ['pb2'], lambda en: en.matmul(pb[2][0:64, 128:256], lhsT=pm_(1), rhs=u[:], start=True, stop=True))
            sigm(rg[:], 'rg', pb[2][0:64, 0:128], ['pb2'], nbias=NB5)
            sigm(ig[:], 'ig', pb[2][0:64, 128:256], ['pb2'], nbias=NB6)
            yield
            S.op('act', ['rg', 'dcol'], ['aa'], lambda en: en.activation(out=aa[:], in_=rg[:], func=AF.Exp, scale=CL))
            S.op('act', ['rg', 'dcol'], ['a2'], lambda en: en.activation(out=a2[:], in_=rg[:], func=AF.Exp, scale=CL2))
            yield
            dv(['a2'], ['a2'], lambda en: en.tensor_scalar(out=a2[:], in0=a2[:], scalar1=-1.0, scalar2=1.0, op0=ALU.mult, op1=ALU.add))
            dv(['a2'], ['a2'], lambda en: en.tensor_scalar(out=a2[:], in0=a2[:], scalar1=1e-30, scalar2=None, op0=ALU.max))
            S.op('act', ['a2'], ['a2'], lambda en: en.activation(out=a2[:], in_=a2[:], func=AF.Ln))
            S.op('act', ['a2'], ['a2'], lambda en: en.activation(out=a2[:], in_=a2[:], func=AF.Exp, scale=0.5))
            dv(['ig', 'u'], ['bbv'], lambda en: en.tensor_tensor(out=bbv[:], in0=ig[:], in1=u[:], op=ALU.mult))
            dv(['bbv', 'a2'], ['bbv'], lambda en: en.tensor_tensor(out=bbv[:], in0=bbv[:], in1=a2[:], op=ALU.mult))
            yield
            if n == 0:
                dv(['bbv'], ['bbv'], lambda en: en.memset(bbv[:, 0:112], 0.0))
            dv(['aa', 'bbv', 'hl'], ['hl'], lambda en: en.tensor_tensor_scan(out=hl[:, 1:129], data0=aa[:], data1=bbv[:], initial=hl[:, 0:1], op0=ALU.mult, op1=ALU.add))
            dv(['hl', 'gy'], [f'ost{xs}_0'], lambda en: en.tensor_tensor(out=os_[:, 0, :], in0=hl[:, 1:129], in1=gy[:], op=ALU.mult))
            dv(['hl'], ['hl'], lambda en: en.tensor_copy(out=hl[:, 0:1], in_=hl[:, 128:129]))

            yield

        def fox_gen():
            S.op('act', ['e1'], ['e1'], lambda en: en.activation(out=e1[:], in_=e1[:], func=AF.Ln, bias=1.0))
            dv(['e1', 'cb', 'ones'], ['cb'], lambda en: en.tensor_tensor_scan(out=cb[:, 1:129], data0=ones[0:33, :], data1=e1[:], initial=cb[:, 0:1], op0=ALU.mult, op1=ALU.subtract))
            for h, hr in ((0, 0), (1, 32)):
                S.dma('sp', ctm[:, n, h:h + 1], cb[hr:hr + 1, 1:129], ['cb'], ['ctm'], f'ctmd{h}')
                S.op('pe', ['cb', 'ones'], ['pb5'], lambda en, h=h, hr=hr: en.matmul(pb[5][:, 386 + h:387 + h], lhsT=ones[hr:hr + 1, 0:128], rhs=cb[hr:hr + 1, 128:129], start=True, stop=True))
            dv(['pb5'], ['rj'], lambda en: en.tensor_copy(out=rj[:, 0:2], in_=pb[5][:, 386:388]))
            dv(['cb'], ['cb'], lambda en: en.tensor_copy(out=cb[:, 0:1], in_=cb[:, 128:129]))
            for h in range(2):
                dv(['ctm', 'rj'], [f'Bn{h}'], lambda en, h=h: en.tensor_scalar(out=Bn[h][:, 0:n + 1], in0=ctm[:, 0:n + 1, h], scalar1=-1.0, scalar2=rj[:, h:h + 1], op0=ALU.mult, op1=ALU.add))
            yield
            for h in range(2):
                SB = (5, 7, 4)

                def s_mm(i):
                    bk = SB[i % 3]
                    S.op('pe', [f'KT{h}', f'qT{h}'], [f'pb{bk}'], lambda en, i=i, bk=bk: en.matmul(pb[bk][:, 0:128], lhsT=KT[h][:, i * 128:(i + 1) * 128], rhs=qT[h][:], start=True, stop=True))
                s_mm(0)
                if n >= 1:
                    s_mm(1)
                for i in range(n + 1):
                    if i + 2 <= n:
                        s_mm(i + 2)
                    bk = SB[i % 3]
                    pm = Pm[i % 3]
                    pk = f'Pm{i % 3}'
                    S.op('act', [f'pb{bk}', f'Bn{h}'], [pk], lambda en, i=i, bk=bk, pm=pm: en.activation(out=pm[:], in_=pb[bk][:, 0:128], func=AF.Exp, bias=Bn[h][:, i:i + 1]))
                    if i == n:
                        S.op('dve', [pk, 'causb'], [pk], lambda en, pm=pm: en.tensor_tensor(out=pm[:], in0=pm[:], in1=causb[:], op=ALU.mult))
                    S.op('pe', [f'VA{h}', pk], ['pb6'], lambda en, i=i, pm=pm: en.matmul(pb[6][0:65, 0:128], lhsT=VA[h][:, i, :], rhs=pm[:], start=(i == 0), stop=(i == n)))
                    yield
                S.op('act', ['pb6'], ['Osb'], lambda en: en.copy(out=Osb[:, h, :], in_=pb[6][0:65, 0:128]))
                S.op('pe', ['Osb', 'ones'], ['pb5'], lambda en: en.matmul(pb[5][0:64, 256:384], lhsT=ones[64:65, 0:64], rhs=Osb[64:65, h, :], start=True, stop=True))
                dv(['pb5'], ['rec'], lambda en: en.tensor_scalar(out=rec[:], in0=pb[5][0:64, 256:384], scalar1=1e-30, scalar2=None, op0=ALU.add))
                dv(['rec'], ['rec'], lambda en: en.reciprocal(out=rec[:], in_=rec[:]))
                dv(['rec', 'Osb'], [f'ost{xs}_{1 + h}'], lambda en: en.tensor_tensor(out=os_[:, 1 + h, :], in0=Osb[0:64, h, :], in1=rec[:], op=ALU.mult))
                yield

        def rwkv_gen():
            dv(['raw4'], ['d4'], lambda en: en.tensor_tensor(out=d4[:], in0=raw4[:, :, 0:128], in1=raw4[:, :, 1:129], op=ALU.subtract))
            dv(['d4', 'MU4'], ['d4'], lambda en: en.tensor_tensor(out=d4[:], in0=d4[:], in1=MU4[:], op=ALU.mult))
            dv(['d4', 'raw4'], ['mx4'], lambda en: en.tensor_tensor(out=mx4[:], in0=d4[:], in1=raw4[:, :, 1:129], op=ALU.add))
            dv(['raw4', 'mx4'], ['raw4'], lambda en: en.tensor_copy(out=raw4[:, :, 0:1], in_=raw4[:, :, 128:129]))
            dv(['rawB'], ['dB'], lambda en: en.tensor_tensor(out=dB[:, 0:ng], in0=rawB[:, 0:ng, 0:128], in1=rawB[:, 0:ng, 1:129], op=ALU.subtract))
            dv(['dB', 'MUB'], ['dB'], lambda en: en.tensor_tensor(out=dB[:, 0:ng], in0=dB[:, 0:ng], in1=MUB[:, 0:ng], op=ALU.mult))
            dv(['dB', 'rawB'], ['mxB'], lambda en: en.tensor_tensor(out=mxB[:, 0:ng], in0=dB[:, 0:ng], in1=rawB[:, 0:ng, 1:129], op=ALU.add))
            dv(['rawB', 'mxB'], ['rawB'], lambda en: en.tensor_copy(out=rawB[:, 0:ng, 0:1], in_=rawB[:, 0:ng, 128:129]))
            yield
            sigm(wdt[:], 'wdt', wd_, ['mx4'], scale=2.0)
            dv(['wdt'], ['wdt'], lambda en: en.tensor_scalar(out=wdt[:], in0=wdt[:], scalar1=2.0, scalar2=-1.0, op0=ALU.mult, op1=ALU.add))
            sigm(sg[:], 'sg', mxB[:, 1, :], ['mxB'])
            S.op('pe', ['wdt', 'pmats'], ['pb0'], lambda en: en.matmul(pb[0][0:64, 0:128], lhsT=pm_(2), rhs=wdt[:], start=True, stop=True))
            S.op('pe', ['mxB', 'pmats'], ['pb0'], lambda en: en.matmul(pb[0][0:64, 128:256], lhsT=pm_(3), rhs=mxB[0:64, 0, :], start=True, stop=True))
            S.op('pe', ['sg', 'pmats'], ['pb0'], lambda en: en.matmul(pb[0][0:64, 256:384], lhsT=pm_(4, 64, 128), rhs=sg[:], start=True, stop=True))
            dv(['mx4', 'pcols'], ['kkr'], lambda en: en.tensor_scalar(out=kkr[:], in0=k_, scalar1=pc(17), scalar2=None, op0=ALU.mult))
            dv(['kkr'], ['sq'], lambda en: en.tensor_tensor(out=sq[:], in0=kkr[:], in1=kkr[:], op=ALU.mult))
            S.op('pe', ['sq', 'ones'], ['pb0'], lambda en: en.matmul(pb[0][0:64, 384:512], lhsT=ones[0:64, 0:64], rhs=sq[:], start=True, stop=True))
            yield
            sigm(logw[:], 'logw', pb[0][0:64, 0:128], ['pb0'], nbias=NB15)
            dv(['logw'], ['logw'], lambda en: en.tensor_scalar(out=logw[:], in0=logw[:], scalar1=DECAY_C, scalar2=None, op0=ALU.mult))
            sigm(av[:], 'av', pb[0][0:64, 128:256], ['pb0'], nbias=NB16)
            S.op('act', ['pb0'], ['gg'], lambda en: en.copy(out=gg[:], in_=pb[0][0:64, 256:384]))
            dv(['pb0'], ['nrm'], lambda en: en.tensor_scalar(out=nrm[:], in0=pb[0][0:64, 384:512], scalar1=1e-24, scalar2=None, op0=ALU.max))
            S.op('act', ['nrm'], ['nrm'], lambda en: en.activation(out=nrm[:], in_=nrm[:], func=AF.Ln))
            S.op('act', ['nrm'], ['nrm'], lambda en: en.activation(out=nrm[:], in_=nrm[:], func=AF.Exp, scale=-0.5))
            dv(['kkr', 'nrm'], ['kk'], lambda en: en.tensor_tensor(out=kk[:], in0=kkr[:], in1=nrm[:], op=ALU.mult))
            dv(['av', 'pcols', 'dcol'], ['tmpk'], lambda en: en.tensor_scalar(out=tmpk[:], in0=av[:], scalar1=pc(18), scalar2=OMKA, op0=ALU.mult, op1=ALU.add))
            dv(['tmpk', 'mx4'], ['k2'], lambda en: en.tensor_tensor(out=k2[:], in0=k_, in1=tmpk[:], op=ALU.mult))
            dv(['kk', 'av'], ['bneg'], lambda en: en.scalar_tensor_tensor(out=bneg[:], in0=kk[:], scalar=-1.0, in1=av[:], op0=ALU.mult, op1=ALU.mult))
            yield
            if layer == 0:
                dv(['mx4'], ['v2'], lambda en: en.tensor_copy(out=v2[:], in_=v_))
                S.dma('sp', vown[:, t0:t0 + 128], v2[:], ['v2'], ['vf_dram'], 'vo')
            else:
                S.dma('sp', vf[:], vfirst[:, t0:t0 + 128], ['vf_dram'], ['vf'], 'vfl')
                for c in range(2):
                    S.op('pe', ['mxB', 'pmats'], ['pb1'], lambda en, c=c: en.matmul(pb[1][0:32, 0:128], lhsT=pmats[:, 320 + c * 32:352 + c * 32], rhs=mxB[:, 2 + c, :], start=(c == 0), stop=(c == 1)))
                S.op('act', ['pb1'], ['vd'], lambda en: en.copy(out=vd[:], in_=pb[1][0:32, 0:128]))
                S.op('pe', ['vd', 'pmats'], ['pb1'], lambda en: en.matmul(pb[1][0:64, 128:256], lhsT=pmats[0:32, 384:448], rhs=vd[:], start=True, stop=True))
                sigm(sv[:], 'sv', pb[1][0:64, 128:256], ['pb1'], nbias=NB24)
                dv(['vf', 'mx4'], ['vf'], lambda en: en.tensor_tensor(out=vf[:], in0=vf[:], in1=v_, op=ALU.subtract))
                dv(['vf', 'sv'], ['vf'], lambda en: en.tensor_tensor(out=vf[:], in0=vf[:], in1=sv[:], op=ALU.mult))
                dv(['vf', 'mx4'], ['v2'], lambda en: en.tensor_tensor(out=v2[:], in0=vf[:], in1=v_, op=ALU.add))
            yield
            dv(['logw', 'cmask'], ['L'], lambda en: en.tensor_tensor_scan(out=L[:], data0=RMASK, data1=logw[:], initial=0.0, op0=ALU.mult, op1=ALU.add))
            S.op('act', ['L'], ['eL'], lambda en: en.activation(out=eL[:], in_=L[:], func=AF.Exp))
            S.op('act', ['L'], ['eLn'], lambda en: en.activation(out=eLn[:], in_=L[:], func=AF.Exp, scale=-1.0))
            dv(['L', 'logw'], ['eLx'], lambda en: en.tensor_tensor(out=eLx[:], in0=L[:], in1=logw[:], op=ALU.subtract))
            S.op('act', ['eLx'], ['eLx'], lambda en: en.activation(out=eLx[:], in_=eLx[:], func=AF.Exp))
            for c in range(2):
                S.op('act', ['L'], ['eLC'], lambda en, c=c: en.activation(out=eLC[:, c * 64:(c + 1) * 64], in_=L[:, c * 64:(c + 1) * 64], func=AF.Exp, scale=-1.0, bias=L[:, c * 64 + 63:c * 64 + 64]))
            v3 = lambda ap: ap.rearrange("p (c t) -> p c t", c=2)
            dv(['kk', 'eLx'], ['kr'], lambda en: en.tensor_tensor(out=kr[:, :, 0, :], in0=v3(kk[:]), in1=v3(eLx[:]), op=ALU.mult))
            dv(['mx4', 'eL'], ['kr'], lambda en: en.tensor_tensor(out=kr[:, :, 1, :], in0=v3(r_), in1=v3(eL[:]), op=ALU.mult))
            dv(['k2', 'eLn'], ['kt'], lambda en: en.tensor_tensor(out=kt[:], in0=k2[:], in1=eLn[:], op=ALU.mult), e='pool')
            dv(['bneg', 'eLn'], ['btn'], lambda en: en.tensor_tensor(out=btn[:], in0=bneg[:], in1=eLn[:], op=ALU.mult), e='pool')
            dv(['k2', 'eLC'], ['khat'], lambda en: en.tensor_tensor(out=khat[:], in0=k2[:], in1=eLC[:], op=ALU.mult), e='pool')
            dv(['bneg', 'eLC'], ['bhn'], lambda en: en.tensor_tensor(out=bhn[:], in0=bneg[:], in1=eLC[:], op=ALU.mult), e='pool')
            yield
            for c in range(2):
                cs = slice(c * 64, (c + 1) * 64)
                tm, ABk, ABb, W, Pq = tmC[c], ABkC[c], ABbC[c], WC[c], PqC[c]
                kT, kAk, kAb, kW = f'tm{c}', f'ABk{c}', f'ABb{c}', f'W{c}'
                for j, (src, key) in enumerate(((v2, 'v2'), (khat, 'khat'), (bhn, 'bhn'))):
                    S.op('pe', [key, 'ident'], ['pb1'], lambda en, j=j, src=src: en.matmul(pb[1][0:64, j * 64:(j + 1) * 64], lhsT=src[:, cs], rhs=identf[0:64, 0:64], start=True, stop=True))
                S.op('act', ['pb1'], [kT], lambda en: en.copy(out=tm[:], in_=pb[1][0:64, 0:192].rearrange("p (j t) -> p j t", j=3)))
                krc = kr[:, c, :, :].rearrange("p a t -> p (a t)")
                S.op('pe', ['kt', 'kr'], ['pb2'], lambda en: en.matmul(pb[2][0:64, 0:128], lhsT=kt[:, cs], rhs=krc, start=True, stop=True))
                S.op('pe', ['btn', 'kr'], ['pb2'], lambda en: en.matmul(pb[2][0:64, 128:256], lhsT=btn[:, cs], rhs=krc, start=True, stop=True))
                S.op('pe', ['btn', 'kr'], ['pb2'], lambda en: en.matmul(pb[2][0:64, 256:320], lhsT=kr[:, c, 0, :], rhs=btn[:, cs], start=True, stop=True))
                dv(['pb2', 'cmask'], [kAk], lambda en: en.tensor_tensor(out=ABk[:], in0=pb[2][0:64, 0:128], in1=MASK2, op=ALU.mult))
                dv(['pb2', 'cmask'], [kAb], lambda en: en.tensor_tensor(out=ABb[:], in0=pb[2][0:64, 128:256], in1=MASK2, op=ALU.mult))
                dv([kAb], [f'Pq{c}_0'], lambda en: en.tensor_copy(out=Pq[0][:, 0, :], in_=ABb[:, 0:64]))
                dv(['pb2', 'cmask', f'Pq{c}_0'], [f'Pq{c}_0'], lambda en: en.tensor_tensor(out=Pq[0][:, 1, :], in0=pb[2][0:64, 256:320], in1=MASKL, op=ALU.mult))
                dv([kAb, 'ident'], [kW], lambda en: en.tensor_tensor(out=W[:], in0=ABb[:, 0:64], in1=identf[0:64, 0:64], op=ALU.add))
                yield
                for lvl in range(5):
                    cur, nxt = Pq[lvl % 2], Pq[(lvl + 1) % 2]
                    ck, nk = f'Pq{c}_{lvl % 2}', f'Pq{c}_{(lvl + 1) % 2}'
                    S.op('pe', [ck], ['pb3'], lambda en, cur=cur: en.matmul(pb[3][0:64, 0:64], lhsT=cur[:, 1, :], rhs=cur[:, 0, :], start=True, stop=True))
                    S.op('pe', [ck], ['pb3'], lambda en, cur=cur: en.matmul(pb[3][0:64, 64:128], lhsT=cur[:, 0, :], rhs=cur[:, 1, :], start=True, stop=True))
                    S.op('act', ['pb3'], [nk], lambda en, nxt=nxt: en.copy(out=nxt[:], in_=pb[3][0:64, 0:128].rearrange("p (a t) -> p a t", a=2)))
                    S.op('pe', [nk, kW], ['pb3'], lambda en, nxt=nxt: en.matmul(pb[3][0:64, 128:192], lhsT=nxt[:, 1, :], rhs=W[:], start=True, stop=True))
                    dv(['pb3', kW], [kW], lambda en: en.tensor_tensor(out=W[:], in0=pb[3][0:64, 128:192], in1=W[:], op=ALU.add))
                    yield
                pre_done[c] = True

        def rwkv_chain():
            for c in range(2):
                while not pre_done[c]:
                    yield
                cs = slice(c * 64, (c + 1) * 64)
                tm, ABk, ABb, W = tmC[c], ABkC[c], ABbC[c], WC[c]
                kT, kAk, kAb, kW = f'tm{c}', f'ABk{c}', f'ABb{c}', f'W{c}'
                S.op('pe', [kAk, kT], ['pb0'], lambda en: en.matmul(pb[0][0:64, 0:64], lhsT=ABk[:, 0:64], rhs=tm[:, 0, :], start=True, stop=False))
                S.op('pe', ['kr', 'H'], ['pb0'], lambda en: en.matmul(pb[0][0:64, 0:64], lhsT=kr[:, c, 0, :], rhs=H[:], start=False, stop=True))
                S.op('act', ['pb0'], ['Xsb'], lambda en: en.copy(out=Xsb[:], in_=pb[0][0:64, 0:64]))
                yield
                S.op('pe', [kW, 'Xsb'], ['pb0'], lambda en: en.matmul(pb[0][0:64, 64:128], lhsT=W[:], rhs=Xsb[:], start=True, stop=True))
                S.op('act', ['pb0'], ['Usb'], lambda en: en.copy(out=Usb[:], in_=pb[0][0:64, 64:128]))
                yield
                S.op('pe', ['H', 'kr'], ['pb0'], lambda en: en.matmul(pb[0][0:64, 192:256], lhsT=H[:], rhs=kr[:, c, 1, :], start=True, stop=False))
                S.op('pe', [kT, kAk], ['pb0'], lambda en: en.matmul(pb[0][0:64, 192:256], lhsT=tm[:, 0, :], rhs=ABk[:, 64:128], start=False, stop=False))
                S.op('pe', ['Usb', kAb], ['pb0'], lambda en: en.matmul(pb[0][0:64, 192:256], lhsT=Usb[:], rhs=ABb[:, 64:128], start=False, stop=True))
                S.op('act', ['pb0'], ['osb'], lambda en: en.copy(out=osb[:, cs], in_=pb[0][0:64, 192:256]))
                yield
                S.op('pe', [kT], ['pb0'], lambda en: en.matmul(pb[0][0:64, 128:192], lhsT=tm[:, 1, :], rhs=tm[:, 0, :], start=True, stop=False))
                S.op('pe', [kT, 'Usb'], ['pb0'], lambda en: en.matmul(pb[0][0:64, 128:192], lhsT=tm[:, 2, :], rhs=Usb[:], start=False, stop=True))
                dv(['pb0', 'H', 'eL'], ['H'], lambda en: en.scalar_tensor_tensor(out=H[:], in0=H[:], scalar=eL[:, c * 64 + 63:c * 64 + 64], in1=pb[0][0:64, 128:192], op0=ALU.mult, op1=ALU.add))
                yield
            yield
            S.op('pe', ['osb', 'ones'], ['pb1'], lambda en: en.matmul(pb[1][0:64, 0:128], lhsT=ones[0:64, 0:64], rhs=osb[:], start=True, stop=True))
            dv(['osb'], ['sq'], lambda en: en.tensor_tensor(out=sq[:], in0=osb[:], in1=osb[:], op=ALU.mult))
            S.op('pe', ['sq', 'ones'], ['pb1'], lambda en: en.matmul(pb[1][0:64, 128:256], lhsT=ones[0:64, 0:64], rhs=sq[:], start=True, stop=True))
            dv(['mx4', 'pcols', 'k2'], ['rk'], lambda en: en.scalar_tensor_tensor(out=rk[:], in0=r_, scalar=pc(19), in1=k2[:], op0=ALU.mult, op1=ALU.mult))
            S.op('pe', ['rk', 'ones'], ['pb1'], lambda en: en.matmul(pb[1][0:64, 256:384], lhsT=ones[0:64, 0:64], rhs=rk[:], start=True, stop=True))
            dv(['pb1'], ['mean'], lambda en: en.tensor_scalar(out=mean[:], in0=pb[1][0:64, 0:128], scalar1=1.0 / 64, scalar2=None, op0=ALU.mult))
            dv(['osb', 'mean'], ['cen'], lambda en: en.tensor_tensor(out=cen[:], in0=osb[:], in1=mean[:], op=ALU.subtract))
            dv(['mean'], ['msq'], lambda en: en.tensor_tensor(out=msq[:], in0=mean[:], in1=mean[:], op=ALU.mult))
            dv(['pb1', 'msq'], ['var'], lambda en: en.scalar_tensor_tensor(out=var[:], in0=pb[1][0:64, 128:256], scalar=1.0 / 64, in1=msq[:], op0=ALU.mult, op1=ALU.subtract))
            dv(['var'], ['var'], lambda en: en.tensor_scalar(out=var[:], in0=var[:], scalar1=GN_EPS, scalar2=None, op0=ALU.add))
            S.op('act', ['var'], ['var'], lambda en: en.activation(out=var[:], in_=var[:], func=AF.Ln))
            S.op('act', ['var'], ['var'], lambda en: en.activation(out=var[:], in_=var[:], func=AF.Exp, scale=-0.5))
            dv(['cen', 'var'], ['cen'], lambda en: en.tensor_tensor(out=cen[:], in0=cen[:], in1=var[:], op=ALU.mult))
            dv(['cen', 'pcols'], ['cen'], lambda en: en.tensor_scalar(out=cen[:], in0=cen[:], scalar1=pc(20), scalar2=pc(21), op0=ALU.mult, op1=ALU.add))
            dv(['pb1', 'v2'], ['bon'], lambda en: en.tensor_tensor(out=bon[:], in0=pb[1][0:64, 256:384], in1=v2[:], op=ALU.mult))
            dv(['cen', 'bon'], ['cen'], lambda en: en.tensor_tensor(out=cen[:], in0=cen[:], in1=bon[:], op=ALU.add))
            dv(['cen', 'gg'], [f'ost{xs}_3'], lambda en: en.tensor_tensor(out=os_[:, 3, :], in0=cen[:], in1=gg[:], op=ALU.mult))

        pre_done = [False, False]
        gfox, grw = fox_gen(), [lru_gen(), rwkv_gen(), rwkv_chain()]
        kfox = max(1, (2 * (n + 1) + 2) // 16)
        fox_alive = True
        while fox_alive or grw:
            if fox_alive:
                for _ in range(kfox):
                    try:
                        next(gfox)
                    except StopIteration:
                        fox_alive = False
                        break
            for g_ in list(grw):
                try:
                    next(g_)
                except StopIteration:
                    grw.remove(g_)
        S.dma('sp', ov[:, :, t0:t0 + 128], os_[:], [f'ost{xs}_{j}' for j in range(4)], ['cat_loc'], f'oo{xs}')


A_COLS, B_COLS = 512, 1544
PC0 = A_COLS + B_COLS


def consts():
    ident = np.eye(128, dtype=np.float32)
    cm = np.zeros((128, 512), np.float32)
    r = np.arange(128)[:, None]
    c = np.arange(128)[None, :]
    cm[:, 0:128] = (r <= c)
    r64 = np.arange(64)[:, None]
    c64 = np.arange(64)[None, :]
    cm[0:64, 128:192] = (r64 < c64)
    cm[0:64, 192:256] = (r64 <= c64)
    cm[0:64, 256:320] = (r64 > c64)
    cm[0:64, 320:448] = 1.0
    cm[0:64, 320] = 0.0
    cm[0:64, 384] = 0.0
    return ident, cm


def mix_inputs(l, g, I):
    w_in = I['w_in'][l]
    W = np.zeros((D, NCOLS), np.float32)
    hs = slice(64 * g, 64 * g + 64)
    W[:, CG['xa']:CG['xa'] + 64] = w_in[:, 0:256][:, hs]
    W[:, CG['ya']:CG['ya'] + 64] = w_in[:, 256:512][:, hs]
    pb = w_in[:, A_COLS:A_COLS + B_COLS]
    for nm, h in (('A', 2 * g), ('B', 2 * g + 1)):
        W[:, CG['q' + nm]:CG['q' + nm] + 64] = pb[:, 64 * h:64 * h + 64]
        W[:, CG['k' + nm]:CG['k' + nm] + 64] = pb[:, 512 + 64 * h:512 + 64 * h + 64]
    W[:, CG['vAB']:CG['vAB'] + 128] = pb[:, 1024 + 128 * g:1024 + 128 * g + 128]
    W[:, CG['f']] = pb[:, 1536 + 2 * g]
    W[:, CG['f'] + 32] = pb[:, 1536 + 2 * g + 1]
    pc = w_in[:, PC0:]
    W[:, CG['r']:CG['r'] + 64] = pc[:, 0:256][:, hs]
    W[:, CG['k']:CG['k'] + 64] = pc[:, 256:512][:, hs]
    W[:, CG['v']:CG['v'] + 64] = pc[:, 512:768][:, hs]
    W[:, CG['wd']:CG['wd'] + 64] = pc[:, 768:832]
    W[:, CG['ad']:CG['ad'] + 64] = pc[:, 832:896]
    W[:, CG['gd']:CG['gd'] + 128] = pc[:, 896:1024]
    W[:, CG['vall']:CG['vall'] + 256] = pc[:, 512:768]
    P = np.zeros((128, 32), np.float32)
    for k in range(4):
        P[0:64, k] = I['lru_conv_w'][l][k, hs]
    P[0:64, 4] = I['lru_conv_b'][l][hs]
    P[0:64, 5] = I['lru_gate_a_b'][l][hs]
    P[0:64, 6] = I['lru_gate_x_b'][l][hs]
    P[0:64, 7] = I['lru_lambda'][l][hs]
    P[0, 8] = I['fox_f_bias'][l][2 * g]
    P[32, 8] = I['fox_f_bias'][l][2 * g + 1]
    mu = I['rwkv_mu'][l]
    P[0:64, 9] = mu[0:256][hs]
    P[0:64, 10] = mu[256:512][hs]
    P[0:64, 11] = mu[512:768][hs]
    P[0:64, 12] = mu[768:832]
    P[0:64, 13] = mu[832:896]
    P[:, 14] = mu[896:1024]
    P[0:64, 15] = I['rwkv_w0'][l][hs]
    P[0:64, 16] = I['rwkv_a0'][l][hs]
    P[0:64, 17] = I['rwkv_k_k'][l][hs]
    P[0:64, 18] = I['rwkv_k_a'][l][hs]
    P[0:64, 19] = I['rwkv_r_k'][l][g]
    P[0:64, 20] = I['rwkv_ln_w'][l][hs]
    P[0:64, 21] = I['rwkv_ln_b'][l][hs]
    P[:, 22] = mu[512:640]
    P[:, 23] = mu[640:768]
    M = np.zeros((128, 448), np.float32)
    M[0:64, 0:64] = I['lru_gate_a_w'][l][g]
    M[0:64, 64:128] = I['lru_gate_x_w'][l][g]
    M[0:64, 128:192] = I['rwkv_w_up'][l][:, hs]
    M[0:64, 192:256] = I['rwkv_a_up'][l][:, hs]
    M[:, 256:320] = I['rwkv_g_up'][l][:, hs]
    if l >= 1:
        P[0:64, 24] = I['rwkv_v0'][l - 1][hs]
        vd = I['rwkv_v_down'][l - 1]
        M[:, 320:352] = vd[0:128]
        M[:, 352:384] = vd[128:256]
        M[0:32, 384:448] = I['rwkv_v_up'][l - 1][:, hs]
    return W, P, M


GROUPS = [[0, 1, 2, 3], [4, 5, 6, 7]]
LAYER_KEYS = ('w', 'gpre', 'pcols', 'pmats', 'w_out', 'gpost1', 'w_up', 'w_dn', 'conv', 'gpre2', 'gpost2')
LAYER_SHAPES = {'w': [D, NCOLS], 'gpre': [D], 'pcols': [128, 32], 'pmats': [128, 448], 'w_out': [D, D], 'gpost1': [D],
                'w_up': [D, 2 * DFF], 'w_dn': [DFF, D], 'conv': [3, 2 * DFF], 'gpre2': [D], 'gpost2': [D]}


def build_fused(parts='PGMCTUg'):
    C = Ctx("fused")
    S = C.S
    h0 = C.din("h0", [TOK, D])
    rowmask = C.din("rowmask", [128, 1])
    ident = C.din("ident", [128, 128])
    cmask = C.din("cmask", [128, 512])
    L = [{k: C.din(f"{k}_{l}", LAYER_SHAPES[k]) for k in LAYER_KEYS} for l in range(2)]
    out = C.dout("out", [16 * 128, D])
    xn_loc = C.dint("xn_loc", [D, TOK], BF16)
    xn_all = C.dint("xn_all", [4 * D, TOK], BF16)
    cat_loc = C.dint("cat_loc", [256, TP], BF16)
    cat_all = C.dint("cat_all", [D, TP], BF16)
    h_dram = C.dint("h_dram", [TOK, D], F32)
    xn2_dram = C.dint("xn2_dram", [D, TOK], BF16)
    vf_dram = C.dint("vf_dram", [64, TP], F32)
    base = {'ident': ident, 'rowmask': rowmask, 'cmask': cmask}
    C.qoff = C.nc.sync.snap((C.nc.sync.partition_id() % 4) * 2048)

    def gather(src, dst, key, rows):
        n = src.ap().shape[0] // rows
        for k in range(n):
            S.collective("AllGather", [src.ap()[k * rows:(k + 1) * rows, :].opt()], [dst.ap()[4 * rows * k:4 * rows * (k + 1), :].opt()], GROUPS, [key])

    if 'P' in parts:
      with C.phase("P"):
        emit_tok(C, 'P', dict(base, h_in=h0, xnT_out=xn_loc.ap()))
    if 'G' in parts:
        gather(xn_loc, xn_all, 'xn_all', 128)
    for l in range(2):
        if 'M' in parts:
          with C.phase(f"M{l}"):
            emit_mix(C, l, dict(base, xn_all=xn_all.ap(), w=L[l]['w'], gpre=L[l]['gpre'], pcols=L[l]['pcols'], pmats=L[l]['pmats'],
                                cat_loc=cat_loc.ap(), vf_dram=vf_dram.ap()))
        if 'C' in parts:
            gather(cat_loc, cat_all, 'cat_all', 32)
        if 'T' in parts:
          with C.phase(f"T1_{l}"):
            emit_tok(C, 'T1', dict(base, h_in=(h0 if l == 0 else h_dram.ap()), cat_all=cat_all.ap(), w_out=L[l]['w_out'], gpost=L[l]['gpost1'],
                                   h_out=h_dram.ap(), xnT_out=xn2_dram.ap()))
        if 'U' in parts:
          with C.phase(f"T2_{l}"):
            io = dict(base, h_in=h_dram.ap(), xnT_in=xn2_dram.ap(), w_up=L[l]['w_up'], w_dn=L[l]['w_dn'], conv=L[l]['conv'], gpre=L[l]['gpre2'],
                      gpost=L[l]['gpost2'])
            if l == 0:
                io.update(h_out=h_dram.ap(), xnT_out=xn_loc.ap())
            else:
                io.update(out_final=out)
            emit_tok(C, 'T2', io)
        if l == 0 and 'g' in parts:
            gather(xn_loc, xn_all, 'xn_all', 128)
    S.wait_collectives()
    S.finish()
    C.es.close()
    print("fused build: inst", S.n_inst, "waits", S.n_wait, flush=True)
    return C.nc


def cat_perm():
    loc = []
    for g in range(4):
        loc.append(list(range(64 * g, 64 * g + 64)) + list(range(256 + 128 * g, 256 + 128 * g + 128)) + list(range(768 + 64 * g, 768 + 64 * g + 64)))
    p = []
    for k in range(8):
        for r in range(4):
            p += loc[r][k * 32:(k + 1) * 32]
    return np.array(p)


def kernel(**I):
    I = {k: np.asarray(v) for k, v in I.items()}
    ident, cm = consts()
    perm = cat_perm()
    in_maps = []
    for core in range(8):
        b, q = core // 4, core % 4
        x = I['x'][b]
        h0 = np.zeros((TOK, D), np.float32)
        if q == 0:
            h0[112:128] = I['meta_tokens']
        else:
            h0[0:128] = x[(16 * q - 1) * 128:16 * q * 128]
        h0[128:] = x[16 * q * 128:(16 * q + 16) * 128]
        rm = np.ones((128, 1), np.float32)
        if q == 0:
            rm[:112] = 0.0
        d = {"h0": h0, "rowmask": rm, "ident": ident, "cmask": cm}
        for l in range(2):
            W, P, M = mix_inputs(l, q, I)
            vals = {'w': W, 'gpre': I['norm_mix_pre'][l], 'pcols': P, 'pmats': M, 'w_out': np.ascontiguousarray(I['w_out'][l][perm]),
                    'gpost1': I['norm_mix_post'][l], 'w_up': I['ffn_up'][l], 'w_dn': I['ffn_down'][l], 'conv': I['ffn_conv'][l],
                    'gpre2': I['norm_ffn_pre'][l], 'gpost2': I['norm_ffn_post'][l]}
            for k, v in vals.items():
                d[f"{k}_{l}"] = np.ascontiguousarray(v, dtype=np.float32)
        in_maps.append(d)
    nc = build_fused()
    res = run_bass_kernel_spmd(nc, in_maps, core_ids=list(range(8))).results
    out = np.zeros((2, 8192, D), np.float32)
    for core in range(8):
        b, q = core // 4, core % 4
        out[b, 2048 * q:2048 * q + 2048] = res[core]["out"]
    return out
```

```python
import contextlib
import numpy as np
import ml_dtypes
import concourse.bass as bass
import concourse.mybir as mybir
from concourse.bass_utils import run_bass_kernel_spmd

F32 = mybir.dt.float32
BF16 = mybir.dt.bfloat16
AF = mybir.ActivationFunctionType
ALU = mybir.AluOpType

D = 1024
T_REAL = 8208
NTT = 17
TOK = NTT * 128
OWN = 2052
TP = 8320
DFF = 2816
EPS = 1e-6


class Sched:
    EPOCH = 10 ** 9

    def __init__(self, nc, es):
        self.nc = nc
        self.es = es
        self.eng = {'pe': nc.tensor, 'act': nc.scalar, 'dve': nc.vector, 'pool': nc.gpsimd, 'sp': nc.sync}
        self.sem = {}
        self.cnt = {}
        self.nsem = 0
        for e in ['pe', 'act', 'dve', 'pool']:
            self._newsem(e)
        self.dsem = {}
        self.dcnt = {}
        self.waited = {}
        self.lastw = {}
        self.readers = {}
        self.n_inst = 0
        self.n_wait = 0

    def _newsem(self, e):
        self.nsem += 1
        self.sem[e] = self.es.enter_context(self.nc.semaphore(f"s_{e}_{self.nsem}"))
        self.cnt[e] = 0

    def _wait(self, ec, tok):
        kind, sem, val, ep = tok
        if kind == 'e' and ep == 'pe' and ec == 'pe':
            return
        key = (ec, id(sem))
        if self.waited.get(key, 0) >= val:
            return
        self.waited[key] = val
        self.eng[ec].wait_ge(sem, val)
        self.n_wait += 1

    def _deps(self, ec, reads, writes):
        for k in reads:
            t = self.lastw.get(k)
            if t is not None:
                self._wait(ec, t)
        for k in writes:
            t = self.lastw.get(k)
            if t is not None:
                self._wait(ec, t)
            for t in self.readers.get(k, {}).values():
                self._wait(ec, t)

    def _update(self, reads, writes, tok):
        for k in writes:
            self.lastw[k] = tok
            self.readers[k] = {}
        for k in reads:
            if k in writes:
                continue
            self.readers.setdefault(k, {})[(tok[0], tok[3] if tok[0] == 'e' else id(tok[1]))] = tok

    def op(self, e, reads, writes, fn, sig=True):
        self._deps(e, reads, writes)
        inst = fn(self.eng[e])
        self.n_inst += 1
        if sig:
            if self.cnt[e] >= self.EPOCH:
                self._newsem(e)
            self.cnt[e] += 1
            inst.then_inc(self.sem[e], 1)
            tok = ('e', self.sem[e], self.cnt[e], e)
        else:
            tok = ('e', self.sem[e], self.cnt[e] + 1, e)
        self._update(reads, writes, tok)
        return tok

    def dma(self, q, out, in_, reads, writes, sem, **kw):
        self._deps(q, reads, writes)
        if sem not in self.dsem:
            self.dsem[sem] = self.es.enter_context(self.nc.semaphore(f"d_{sem}"))
            self.dcnt[sem] = 0
        inst = self.eng[q].dma_start(out=out, in_=in_, **kw)
        self.n_inst += 1
        self.dcnt[sem] += 16
        inst.then_inc(self.dsem[sem], 16)
        tok = ('d', self.dsem[sem], self.dcnt[sem], q)
        self._update(reads, writes, tok)
        return tok

    def barrier(self):
        toks = []
        for e in ['pe', 'act', 'dve', 'pool']:
            if self.cnt[e] > 0:
                toks.append(('e', self.sem[e], self.cnt[e], '__'))
        for s_, h in self.dsem.items():
            toks.append(('d', h, self.dcnt[s_], '__'))
        for ec in ['pe', 'act', 'dve', 'pool', 'sp']:
            for t in toks:
                self._wait(ec, t)
        self.lastw = {}
        self.readers = {}

    def collective(self, kind, ins, outs, groups, writes):
        self.ncc = getattr(self, 'ncc', 0) + 1
        sem = self.es.enter_context(self.nc.semaphore(f"cc{self.ncc}"))
        inst = self.nc.gpsimd.collective_compute(kind, ALU.bypass, replica_groups=groups, ins=ins, outs=outs)
        inst.then_inc(sem)
        self.n_inst += 1
        tok = ('d', sem, 1, 'pool')
        self.cctoks = getattr(self, 'cctoks', []) + [tok]
        return tok

    def wait_collectives(self):
        for ec in ['pe', 'act', 'dve', 'pool', 'sp']:
            for t in getattr(self, 'cctoks', []):
                self._wait(ec, t)

    def finish(self, q='sp'):
        for s, h in self.dsem.items():
            self.eng[q].wait_ge(h, self.dcnt[s])
        for e in ['pe', 'act', 'dve', 'pool']:
            if self.cnt[e] > 0:
                self.eng[q].wait_ge(self.sem[e], self.cnt[e])


class Ctx:
    def __init__(self, name):
        self.nc = bass.Bass("TRN2", target_bir_lowering=False)
        self.es = contextlib.ExitStack()
        self.S = Sched(self.nc, self.es)
        self.rr = 0
        self.pid = 0
        self.pes = self.es

    def din(self, name, shape, dt=F32):
        return self.nc.dram_tensor(name, list(shape), dt, kind="ExternalInput").ap()

    def dout(self, name, shape, dt=F32):
        return self.nc.dram_tensor(name, list(shape), dt, kind="ExternalOutput").ap()

    def sb(self, name, shape, dt=F32):
        return self.pes.enter_context(self.nc.sbuf_tensor(f'sb{self.pid}_' + name, list(shape), dt))

    def ps(self, name, shape, dt=F32):
        return self.pes.enter_context(self.nc.psum_tensor(f'ps{self.pid}_' + name, list(shape), dt))

    def dint(self, name, shape, dt=F32):
        return self.nc.dram_tensor(name, list(shape), dt)

    @contextlib.contextmanager
    def phase(self, name):
        self.pid += 1
        self.S.wait_collectives()
        self.pes = contextlib.ExitStack()
        for a in ('wstage', 'wsi'):
            if hasattr(self, a):
                delattr(self, a)
        try:
            yield
        finally:
            self.S.barrier()
            self.pes.close()
            self.pes = self.es

    def anyeng(self, choices=('dve', 'pool')):
        self.rr += 1
        return choices[self.rr % len(choices)]


def load_cast_weight(C, w_dram, nk, ncol, dst_bf, key, gcol=None, piece=1024, sem='wld'):
    S = C.S
    if not hasattr(C, 'wstage'):
        C.wstage = [C.sb(f"wstage{i}", [128, 1408], F32) for i in range(3)]
        C.wsi = 0
    wv = w_dram.rearrange("(c p) n -> p c n", p=128)
    for c in range(nk):
        for n0 in range(0, ncol, piece):
            n1 = min(ncol, n0 + piece)
            si = C.wsi % 3
            C.wsi += 1
            st = C.wstage[si]
            S.dma('sp', st[:, 0:n1 - n0], wv[:, c, n0:n1], [], [f'wst{si}'], f'wld{si}')
            e = C.anyeng(('dve', 'act'))
            rd = [f'wst{si}'] + ([] if gcol is None else ['gcols'])
            if gcol is None:
                if e == 'act':
                    S.op(e, rd, [key], lambda en, st=st, c=c, n0=n0, n1=n1: en.copy(out=dst_bf[:, c, n0:n1], in_=st[:, 0:n1 - n0]))
                else:
                    S.op(e, rd, [key], lambda en, st=st, c=c, n0=n0, n1=n1: en.tensor_copy(out=dst_bf[:, c, n0:n1], in_=st[:, 0:n1 - n0]))
            else:
                if e == 'act':
                    S.op(e, rd, [key], lambda en, st=st, c=c, n0=n0, n1=n1: en.mul(out=dst_bf[:, c, n0:n1], in_=st[:, 0:n1 - n0], mul=gcol[:, c:c + 1]))
                else:
                    S.op(e, rd, [key], lambda en, st=st, c=c, n0=n0, n1=n1: en.tensor_scalar(out=dst_bf[:, c, n0:n1], in0=st[:, 0:n1 - n0], scalar1=gcol[:, c:c + 1], scalar2=None, op0=ALU.mult))


def rows_to_cols(C, rows_sb, nrows, length, dst, ident, key_rows, key_dst, ps, ps_key):
    S = C.S
    nch = length // 128
    for c in range(nch):
        S.op('pe', [key_rows, 'ident'], [ps_key],
             lambda en, c=c: en.matmul(ps[:, c * nrows:(c + 1) * nrows], lhsT=rows_sb[0:nrows, c * 128:(c + 1) * 128], rhs=ident[0:nrows, 0:nrows], start=True, stop=True))
    S.op('dve', [ps_key], [key_dst], lambda en: en.tensor_copy(out=dst[:, 0:nch, 0:nrows], in_=ps[:, 0:nch * nrows].rearrange("p (c r) -> p c r", r=nrows)))


def emit_rstd(C, src_ap_list, ss, rs, keys_src, slot, tag):
    S = C.S
    junk = C.junk
    for j, (ap, k) in enumerate(zip(src_ap_list, keys_src)):
        n = ap.shape[-1]
        S.op('act', [k], [f'junk', f'ss{tag}{slot}_{j}'],
             lambda en, ap=ap, j=j, n=n: en.activation(out=junk[:, 0:n], in_=ap, func=AF.Square, accum_out=ss[:, j:j + 1]))
    if len(src_ap_list) == 2:
        S.op('dve', [f'ss{tag}{slot}_0', f'ss{tag}{slot}_1'], [f'ss{tag}{slot}_0'],
             lambda en: en.tensor_tensor(out=ss[:, 0:1], in0=ss[:, 0:1], in1=ss[:, 1:2], op=ALU.add))
    S.op('dve', [f'ss{tag}{slot}_0'], [f'rs{tag}{slot}'],
         lambda en: en.tensor_scalar(out=rs[:, 0:1], in0=ss[:, 0:1], scalar1=1.0 / D, scalar2=EPS, op0=ALU.mult, op1=ALU.add))
    S.op('act', [f'rs{tag}{slot}'], [f'rs{tag}{slot}'], lambda en: en.activation(out=rs[:, 0:1], in_=rs[:, 0:1], func=AF.Sqrt))
    S.op('dve', [f'rs{tag}{slot}'], [f'rs{tag}{slot}'], lambda en: en.reciprocal(out=rs[:, 0:1], in_=rs[:, 0:1]))


def emit_norm_xnT(C, h_ap, h_key, slot, xnT_dram, col0, ncols=128, mask0=False):
    S = C.S
    ss = C.ssn[slot]
    rs = C.rsn[slot]
    xnb = C.xnb[slot]
    tp = C.tp[slot]
    xst = C.xst[slot]
    emit_rstd(C, [h_ap], ss, rs, [h_key], slot, 'n')
    if mask0:
        S.op('dve', [f'rsn{slot}', 'rowmask'], [f'rsn{slot}'], lambda en: en.tensor_tensor(out=rs[:, 0:1], in0=rs[:, 0:1], in1=C.rowmask[:, 0:1], op=ALU.mult))
    S.op('dve', [h_key, f'rsn{slot}'], [f'xnb{slot}'],
         lambda en: en.tensor_scalar(out=xnb[:], in0=h_ap, scalar1=rs[:, 0:1], scalar2=None, op0=ALU.mult))
    for c in range(8):
        S.op('pe', [f'xnb{slot}', 'identb'], [f'tp{slot}'],
             lambda en, c=c: en.transpose(out=tp[:, c, :], in_=xnb[:, c * 128:(c + 1) * 128], identity=C.identb[:]), sig=(c == 7))
    S.op('act', [f'tp{slot}'], [f'xst{slot}'], lambda en: en.copy(out=xst[:], in_=tp[:]))
    xv = xnT_dram.rearrange("(c p) t -> p c t", p=128)
    S.dma('sp', xv[:, :, col0:col0 + ncols], xst[:, :, 0:ncols], [f'xst{slot}'], ['xn2'], f'xo{slot}')


def alloc_norm_bufs(C, io):
    C.junk = C.sb("junk", [128, 1024], BF16)
    C.ssn = [C.sb(f"ssn{i}", [128, 2], F32) for i in range(2)]
    C.rsn = [C.sb(f"rsn{i}", [128, 1], F32) for i in range(2)]
    C.xnb = [C.sb(f"xnb{i}", [128, 1024], BF16) for i in range(2)]
    C.tp = [C.ps(f"tp{i}", [128, 8, 128], BF16) for i in range(2)]
    C.xst = [C.sb(f"xst{i}", [128, 8, 128], BF16) for i in range(2)]
    C.identb = C.sb("identb", [128, 128], BF16)
    C.identf = C.sb("identf", [128, 128], F32)
    ident_d = io['ident']
    C.rowmask = C.sb("rowmask", [128, 1], F32)
    C.S.dma('sp', C.rowmask[:], io['rowmask'][:, :], [], ['rowmask'], 'su_rowmask')
    C.S.dma('sp', C.identf[:], ident_d[:, :], [], ['ident'], 'su_ident')
    C.S.op('dve', ['ident'], ['identb'], lambda en: en.tensor_copy(out=C.identb[:], in_=C.identf[:]))


def emit_tok(C, mode, io):
    S = C.S
    h_in = io['h_in']
    xnT_out = io.get('xnT_out')
    h_out = io.get('h_out')
    out_final = io.get('out_final')
    gpost_d = io.get('gpost')
    alloc_norm_bufs(C, io)
    ht = [C.sb(f"ht{i}", [128, D], F32) for i in range(2)]
    if mode != 'P':
        gpost = C.sb("gpost_bc", [128, D], F32)
        S.dma('sp', gpost[:], gpost_d.partition_broadcast(128), [], ['gpost'], 'su_gpost')
        tmpn = [C.sb("tmpn0", [128, D], F32)] * 2
        ssm = [C.sb(f"ssm{i}", [128, 2], F32) for i in range(2)]
        rsm = [C.sb(f"rsm{i}", [128, 1], F32) for i in range(2)]
        mpA = [C.ps(f"mpA{i}", [128, 512], F32) for i in range(1)]
        mpB = [C.ps(f"mpB{i}", [128, 512], F32) for i in range(1)]

    def post_norm_residual(slot, i, hkey):
        emit_rstd(C, [mpA[0][:], mpB[0][:]], ssm[slot], rsm[slot], ['mpA', 'mpB'], slot, 'm')
        for half, mp, k in ((0, mpA[0], 'mpA'), (1, mpB[0], 'mpB')):
            S.op('dve', [k, f'rsm{slot}', 'gpost'], [f'tmpn_{half}'],
                 lambda en, half=half, mp=mp: en.scalar_tensor_tensor(out=tmpn[slot][:, half * 512:(half + 1) * 512], in0=mp[:], scalar=rsm[slot][:, 0:1],
                                                                      in1=gpost[:, half * 512:(half + 1) * 512], op0=ALU.mult, op1=ALU.mult))
        S.op('pool', ['tmpn_0', 'tmpn_1', hkey], [hkey],
             lambda en: en.tensor_tensor(out=ht[slot][:], in0=ht[slot][:], in1=tmpn[slot][:], op=ALU.add))
        if h_out is not None:
            S.dma('sp', h_out[i * 128:(i + 1) * 128, :], ht[slot][:], [hkey], [f'hd{i}'], f'ho{slot}')
        if out_final is not None and i >= 1:
            S.dma('sp', out_final[(i - 1) * 128:i * 128, :], ht[slot][:], [hkey], [], f'ho{slot}')

    issued = set()

    def load_h(i):
        if i >= NTT or ('h', i) in issued:
            return
        issued.add(('h', i))
        S.dma('sp', ht[i % 2][:], h_in[i * 128:(i + 1) * 128, :], [f'hd{i}'], [f'ht{i % 2}'], f'hl{i % 2}')

    if mode == 'P':
        for i in range(NTT):
            slot = i % 2
            load_h(i)
            load_h(i + 1)
            emit_norm_xnT(C, ht[slot][:], f'ht{slot}', slot, xnT_out, i * 128, mask0=(i == 0))
    elif mode == 'T1':
        catT = io['cat_all']
        w_out = io['w_out']
        qoff = C.qoff
        wo_bf = C.sb("wo_bf", [128, 8, D], BF16)
        load_cast_weight(C, w_out, 8, D, wo_bf, 'wo')
        ct = [C.sb(f"ct{i}", [128, 8, 128], BF16) for i in range(2)]
        cv = catT.rearrange("(c p) t -> p c t", p=128)
        for i in range(NTT):
            slot = i % 2
            def load_c(j):
                if j >= NTT or ('c', j) in issued:
                    return
                issued.add(('c', j))
                S.dma('sp', ct[j % 2][:], catT[:, bass.ds(qoff + j * 128, 128)].rearrange("(c p) t -> p c t", p=128), ['cat_all'], [f'ct{j % 2}'], f'cl{j % 2}')
            load_h(i)
            load_c(i)
            load_h(i + 1)
            load_c(i + 1)
            for half, mp, k in ((0, mpA[0], 'mpA'), (1, mpB[0], 'mpB')):
                for c in range(8):
                    S.op('pe', [f'ct{slot}', 'wo'], [k],
                         lambda en, c=c, half=half, mp=mp: en.matmul(mp[:], lhsT=ct[slot][:, c, :], rhs=wo_bf[:, c, half * 512:(half + 1) * 512], start=(c == 0), stop=(c == 7)),
                         sig=(c == 7))
            post_norm_residual(slot, i, f'ht{slot}')
            emit_norm_xnT(C, ht[slot][:], f'ht{slot}', slot, xnT_out, i * 128, mask0=(i == 0))
    else:
        xnT_in = io['xnT_in']
        w_up = io['w_up']
        w_dn = io['w_dn']
        conv_d = io['conv']
        gpre_d = io['gpre']
        NCH = 2 * DFF // 128
        load_cast_weight
        C.wstage = [C.sb(f"wstage{i}", [128, 1408], F32) for i in range(3)]
        C.wsi = 0
        pcol = mpA[0]
        convc = C.sb("convc", [128, NCH, 3], F32)
        gcol = C.sb("gcol", [128, 8, 1], F32)
        for pi in range(4):
            S.dma('sp', C.wstage[0][0:3, :], conv_d[:, pi * 1408:(pi + 1) * 1408], [], ['wst0'], 'wld0')
            for c in range(11):
                S.op('pe', ['wst0', 'ident'], ['mpA'],
                     lambda en, c=c: en.matmul(pcol[:, c * 3:(c + 1) * 3], lhsT=C.wstage[0][0:3, c * 128:(c + 1) * 128], rhs=C.identf[0:3, 0:3], start=True, stop=True))
            S.op('dve', ['mpA'], ['convc'], lambda en, pi=pi: en.tensor_copy(out=convc[:, pi * 11:(pi + 1) * 11, :], in_=pcol[:, 0:33].rearrange("p (c r) -> p c r", r=3)))
        S.dma('sp', C.wstage[1][0:1, 0:D], gpre_d.rearrange("(o n) -> o n", o=1), [], ['wst1'], 'wld1')
        for c in range(8):
            S.op('pe', ['wst1', 'ident'], ['mpA'],
                 lambda en, c=c: en.matmul(pcol[:, c:c + 1], lhsT=C.wstage[1][0:1, c * 128:(c + 1) * 128], rhs=C.identf[0:1, 0:1], start=True, stop=True))
        S.op('dve', ['mpA'], ['gcols'], lambda en: en.tensor_copy(out=gcol[:, :, 0], in_=pcol[:, 0:8]))
        wup_bf = C.sb("wup_bf", [128, 8, 2 * DFF], BF16)
        wdn_bf = C.sb("wdn_bf", [128, DFF // 128, D], BF16)
        load_cast_weight(C, w_up, 8, 2 * DFF, wup_bf, 'wup', gcol=gcol[:].rearrange("p c o -> p (c o)"), piece=1408)
        load_cast_weight(C, w_dn, DFF // 128, D, wdn_bf, 'wdn')
        NB = 256
        xin = [C.sb(f"xin{i}", [128, 8, NB], BF16) for i in range(2)]
        xv = xnT_in.rearrange("(c p) t -> p c t", p=128)
        fps = [C.ps(f"fps{i}", [128, 512], F32) for i in range(4)]
        fraw = [C.sb(f"fraw{i}", [128, NB + 2], F32) for i in range(4)]
        fcv = [C.sb(f"fcv{i}", [128, NB], F32) for i in range(4)]
        gel = [C.sb(f"gel{i}", [128, NB], F32) for i in range(2)]
        halo = C.sb("halo", [128, NCH, 2], F32)
        ptmp = C.sb("ptmp", [128, NB], F32)
        S.op('pool', [], [f'halo{ch}' for ch in range(NCH)], lambda en: en.memset(halo[:], 0.0))
        hid = C.sb("hid", [128, DFF // 128, NB], BF16)
        blocks = [(b0, min(NB, TOK - b0)) for b0 in range(0, TOK, NB)]
        it = 0
        for bi, (b0, nb) in enumerate(blocks):
            xs = bi % 2
            def load_x(bj):
                if bj >= len(blocks) or ('x', bj) in issued:
                    return
                issued.add(('x', bj))
                b0_, nb_ = blocks[bj]
                S.dma('sp', xin[bj % 2][:, :, 0:nb_], xv[:, :, b0_:b0_ + nb_], ['xn2'], [f'xin{bj % 2}'], f'xl{bj % 2}')
            load_x(bi)
            load_x(bi + 1)
            for cc in range(DFF // 128):
                fc = []
                for wi, ch in enumerate((cc, DFF // 128 + cc)):
                    fi = (it % 2) * 2 + wi
                    pk = f'fps{fi}'
                    for c in range(8):
                        S.op('pe', [f'xin{xs}', 'wup'], [pk],
                             lambda en, c=c, ch=ch, fi=fi: en.matmul(fps[fi][:, 0:nb], lhsT=wup_bf[:, c, ch * 128:(ch + 1) * 128], rhs=xin[xs][:, c, 0:nb], start=(c == 0), stop=(c == 7)),
                             sig=(c == 7))
                    S.op('pool', [f'halo{ch}'], [f'fraw{fi}h'], lambda en, fi=fi, ch=ch: en.tensor_copy(out=fraw[fi][:, 0:2], in_=halo[:, ch, :]))
                    S.op('act', [pk], [f'fraw{fi}'], lambda en, fi=fi: en.copy(out=fraw[fi][:, 2:2 + nb], in_=fps[fi][:, 0:nb]))
                    if wi == 0:
                        S.op('act', [pk, 'convc'], [f'fcv{fi}'], lambda en, fi=fi, ch=ch: en.mul(out=fcv[fi][:, 0:nb], in_=fps[fi][:, 0:nb], mul=convc[:, ch, 2:3]))
                        taps = (0, 1)
                    else:
                        S.op('dve', [f'fraw{fi}', 'convc'], [f'fcv{fi}'],
                             lambda en, fi=fi, ch=ch: en.tensor_scalar(out=fcv[fi][:, 0:nb], in0=fraw[fi][:, 2:2 + nb], scalar1=convc[:, ch, 2:3], scalar2=None, op0=ALU.mult))
                        taps = (0, 1)
                    for k in taps:
                        S.op('dve', [f'fraw{fi}', f'fraw{fi}h', 'convc', f'fcv{fi}'], [f'fcv{fi}'],
                             lambda en, fi=fi, ch=ch, k=k: en.scalar_tensor_tensor(out=fcv[fi][:, 0:nb], in0=fraw[fi][:, k:k + nb], scalar=convc[:, ch, k:k + 1],
                                                                                  in1=fcv[fi][:, 0:nb], op0=ALU.mult, op1=ALU.add))
                    S.op('pool', [f'fraw{fi}'], [f'halo{ch}'], lambda en, fi=fi, ch=ch: en.tensor_copy(out=halo[:, ch, :], in_=fraw[fi][:, nb:nb + 2]))
                    fc.append(fi)
                gs = it % 2
                S.op('act', [f'fcv{fc[0]}'], [f'gel{gs}'], lambda en, gs=gs, f0=fc[0]: en.activation(out=gel[gs][:, 0:nb], in_=fcv[f0][:, 0:nb], func=AF.Gelu_apprx_tanh))
                S.op('dve', [f'gel{gs}', f'fcv{fc[1]}'], [f'hid{cc}'],
                     lambda en, gs=gs, f1=fc[1], cc=cc: en.tensor_tensor(out=hid[:, cc, 0:nb], in0=gel[gs][:, 0:nb], in1=fcv[f1][:, 0:nb], op=ALU.mult))
                it += 1
            for ti in range(nb // 128):
                i = (b0 // 128) + ti
                slot = i % 2
                load_h(i)
                load_h(i + 1)
                for half, mp, k in ((0, mpA[0], 'mpA'), (1, mpB[0], 'mpB')):
                    for cc in range(DFF // 128):
                        S.op('pe', [f'hid{cc}', 'wdn'], [k],
                             lambda en, cc=cc, half=half, mp=mp, ti=ti: en.matmul(mp[:], lhsT=hid[:, cc, ti * 128:(ti + 1) * 128], rhs=wdn_bf[:, cc, half * 512:(half + 1) * 512],
                                                                           start=(cc == 0), stop=(cc == DFF // 128 - 1)),
                             sig=(cc == DFF // 128 - 1))
                post_norm_residual(slot, i, f'ht{slot}')
                if xnT_out is not None:
                    emit_norm_xnT(C, ht[slot][:], f'ht{slot}', slot, xnT_out, i * 128, mask0=(i == 0))


NCOLS = 1280
CG = {'xa': 0, 'ya': 64, 'r': 128, 'k': 192, 'v': 256, 'wd': 320, 'ad': 384, 'gd': 448, 'qA': 576, 'kA': 640, 'qB': 704, 'kB': 768,
      'vAB': 832, 'f': 960, 'vall': 1024}
GN_EPS = 64e-5
DECAY_C = -0.6065306597126334


def emit_mix(C, layer, io):
    S = C.S
    NBLK = TP // 128
    xn_all = io['xn_all']
    w_d = io['w']
    gpre_d = io['gpre']
    pcols_d = io['pcols']
    pmats_d = io['pmats']
    cmask_d = io['cmask']
    ident_d = io['ident']
    oT = io['cat_loc']
    vown = io['vf_dram']
    vfirst = io['vf_dram']
    debug = False
    identf = C.sb("identf", [128, 128], F32)
    pcols = C.sb("pcols", [128, 32], F32)
    pmats = C.sb("pmats", [128, 448], F32)
    cmask = C.sb("cmask", [128, 512], F32)
    S.dma('sp', identf[:], ident_d[:, :], [], ['ident'], 'su0')
    S.dma('sp', pcols[:], pcols_d[:, :], [], ['pcols'], 'su1')
    S.dma('sp', pmats[:], pmats_d[:, :], [], ['pmats'], 'su2')
    S.dma('sp', cmask[:], cmask_d[:, :], [], ['cmask'], 'su3')
    pb = [C.ps(f"pb{i}", [128, 512], F32) for i in range(8)]
    C.wstage = [C.sb(f"wstage{i}", [128, 1408], F32) for i in range(3)]
    C.wsi = 0
    gcol = C.sb("gcol", [128, 8], F32)
    S.dma('sp', C.wstage[1][0:1, 0:D], gpre_d.rearrange("(o n) -> o n", o=1), [], ['wst1'], 'wld1')
    for c in range(8):
        S.op('pe', ['wst1', 'ident'], ['pb0'],
             lambda en, c=c: en.matmul(pb[0][:, c:c + 1], lhsT=C.wstage[1][0:1, c * 128:(c + 1) * 128], rhs=identf[0:1, 0:1], start=True, stop=True))
    S.op('dve', ['pb0'], ['gcols'], lambda en: en.tensor_copy(out=gcol[:], in_=pb[0][:, 0:8]))
    w_bf = C.sb("w_bf", [128, 8, NCOLS], BF16)
    load_cast_weight(C, w_d, 8, NCOLS, w_bf, 'w', gcol=gcol[:], piece=1280)
    causb = C.sb("causb", [128, 128], BF16)
    S.op('dve', ['cmask'], ['causb'], lambda en: en.tensor_copy(out=causb[:], in_=cmask[:, 0:128]))
    ones = C.sb("ones", [128, 128], F32)
    S.op('pool', [], ['ones'], lambda en: en.memset(ones[:], 1.0))
    MASK2 = cmask[0:64, 128:256]
    MASKL = cmask[0:64, 256:320]
    RMASK = cmask[0:64, 320:448]
    pc = lambda j, n=64: pcols[0:n, j:j + 1]
    dcol = C.sb("dcol", [128, 12], F32)
    S.op('act', ['pcols'], ['dcol'], lambda en: en.activation(out=dcol[0:64, 0:1], in_=pc(7), func=AF.Sigmoid))
    S.op('act', ['dcol'], ['dcol'], lambda en: en.activation(out=dcol[0:64, 0:1], in_=dcol[0:64, 0:1], func=AF.Ln))
    S.op('dve', ['dcol'], ['dcol'], lambda en: en.tensor_scalar(out=dcol[0:64, 1:2], in0=dcol[0:64, 0:1], scalar1=16.0, scalar2=None, op0=ALU.mult))
    S.op('dve', ['dcol'], ['dcol'], lambda en: en.tensor_scalar(out=dcol[0:64, 0:1], in0=dcol[0:64, 0:1], scalar1=8.0, scalar2=None, op0=ALU.mult))
    S.op('dve', ['pcols', 'dcol'], ['dcol'], lambda en: en.tensor_scalar(out=dcol[0:64, 2:3], in0=pcols[0:64, 8:9], scalar1=-1.0, scalar2=None, op0=ALU.mult))
    S.op('dve', ['pcols', 'dcol'], ['dcol'], lambda en: en.tensor_scalar(out=dcol[0:64, 3:4], in0=pc(18), scalar1=-1.0, scalar2=1.0, op0=ALU.mult, op1=ALU.add))
    CL, CL2, NFB, OMKA = dcol[0:64, 0:1], dcol[0:64, 1:2], dcol[0:33, 2:3], dcol[0:64, 3:4]
    for jj, src in enumerate((5, 6, 15, 16, 24)):
        S.op('dve', ['pcols', 'dcol'], ['dcol'], lambda en, jj=jj, src=src: en.tensor_scalar(out=dcol[0:64, 4 + jj:5 + jj], in0=pcols[0:64, src:src + 1], scalar1=-1.0, scalar2=None, op0=ALU.mult))
    NB5, NB6, NB15, NB16, NB24 = (dcol[0:64, 4 + jj:5 + jj] for jj in range(5))

    def sigm(out_ap, out_key, in_ap, in_keys, nbias=None, scale=1.0):
        if nbias is None:
            S.op('act', in_keys, [out_key], lambda en: en.activation(out=out_ap, in_=in_ap, func=AF.Exp, scale=-scale))
        else:
            S.op('act', in_keys + ['dcol'], [out_key], lambda en: en.activation(out=out_ap, in_=in_ap, func=AF.Exp, scale=-scale, bias=nbias))
        S.op('dve', [out_key], [out_key], lambda en: en.tensor_scalar(out=out_ap, in0=out_ap, scalar1=1.0, scalar2=None, op0=ALU.add))
        S.op('dve', [out_key], [out_key], lambda en: en.reciprocal(out=out_ap, in_=out_ap))
    MU4 = C.sb("MU4", [64, 4, 128], F32)
    MUB = C.sb("MUB", [128, 4, 128], F32)
    S.op('pool', [], ['MUB'], lambda en: en.memset(MUB[:], 0.0))
    for g in range(4):
        S.op('dve', ['pcols'], ['MU4'], lambda en, g=g: en.tensor_copy(out=MU4[:, g, :], in_=pcols[0:64, 9 + g:10 + g].to_broadcast([64, 128])))
    S.op('dve', ['pcols', 'MUB'], ['MUB'], lambda en: en.tensor_copy(out=MUB[0:64, 0, :], in_=pcols[0:64, 13:14].to_broadcast([64, 128])))
    for g, j in ((1, 14), (2, 22), (3, 23)):
        S.op('dve', ['pcols', 'MUB'], ['MUB'], lambda en, g=g, j=j: en.tensor_copy(out=MUB[:, g, :], in_=pcols[:, j:j + 1].to_broadcast([128, 128])))
    KT = [C.sb(f"KT{h}", [64, TP], BF16) for h in range(2)]
    VA = [C.sb(f"VA{h}", [128, NBLK, 65], BF16) for h in range(2)]
    for h in range(2):
        S.op('pool', [], [f'VA{h}'], lambda en, h=h: en.memset(VA[h][:], 1.0))
    ctm = C.sb("ctm", [128, NBLK, 2], F32)
    cb = C.sb("cbrow", [33, 129], F32)
    S.op('dve', [], ['cb'], lambda en: en.memset(cb[:], 0.0))
    H = C.sb("Hst", [64, 64], F32)
    S.op('dve', [], ['H'], lambda en: en.memset(H[:], 0.0))
    hl = C.sb("hlru", [64, 129], F32)
    S.op('dve', [], ['hl'], lambda en: en.memset(hl[:], 0.0))
    xa_raw = C.sb("xa_raw", [64, 131], F32)
    S.op('dve', [], ['xa_raw'], lambda en: en.memset(xa_raw[:], 0.0))
    raw4 = C.sb("raw4", [64, 4, 129], F32)
    S.op('dve', [], ['raw4'], lambda en: en.memset(raw4[:], 0.0))
    rawB = C.sb("rawB", [128, 4, 129], F32)
    S.op('dve', [], ['rawB'], lambda en: en.memset(rawB[:], 0.0))
    xb = [C.sb(f"xb{i}", [128, 8, 128], BF16) for i in range(2)]
    ost = [C.sb(f"ost{i}", [64, 4, 128], BF16) for i in range(2)]
    xv = xn_all.rearrange("(c r p) t -> r p c t", r=4, p=128)
    ov = oT.rearrange("(g p) t -> p g t", p=64)
    cnt = [0]

    def T(shape, dt=F32, name=None):
        cnt[0] += 1
        return C.sb(name or f"t{cnt[0]}", shape, dt)

    u = T([64, 128]); rg = T([64, 128]); ig = T([64, 128]); aa = T([64, 128]); a2 = T([64, 128]); bbv = T([64, 128]); gy = T([64, 128]); ysb = T([64, 128])
    qT = [T([64, 128], BF16) for _ in range(2)]
    e1 = T([33, 128]); rj = T([128, 4]); Bn = [T([128, NBLK]) for _ in range(2)]
    Pm = [T([128, 128], BF16) for _ in range(3)]
    Osb = T([65, 2, 128]); rec = T([64, 128])
    d4 = T([64, 4, 128]); mx4 = T([64, 4, 128]); dB = T([128, 4, 128]); mxB = T([128, 4, 128])
    wdt = T([64, 128]); sg = T([128, 128]); logw = T([64, 128]); av = T([64, 128]); gg = T([64, 128])
    kkr = T([64, 128]); sq = T([64, 128]); nrm = T([64, 128]); kk = T([64, 128]); k2 = T([64, 128]); tmpk = T([64, 128]); bneg = T([64, 128])
    v2 = T([64, 128]); vd = T([32, 128]); sv = T([64, 128]); vf = T([64, 128])
    L = T([64, 128]); eL = T([64, 128]); eLn = T([64, 128]); eLx = T([64, 128]); eLC = T([64, 128])
    kr = T([64, 2, 2, 64]); kt = T([64, 128]); btn = T([64, 128]); khat = T([64, 128]); bhn = T([64, 128])
    tmC = [T([64, 3, 64]) for _ in range(2)]; ABkC = [T([64, 128]) for _ in range(2)]; ABbC = [T([64, 128]) for _ in range(2)]
    PqC = [[T([64, 2, 64]) for _ in range(2)] for _ in range(2)]; WC = [T([64, 64]) for _ in range(2)]
    Xsb = T([64, 64]); Usb = T([64, 64]); osb = T([64, 128])
    mean = T([64, 128]); cen = T([64, 128]); msq = T([64, 128]); var = T([64, 128]); rk = T([64, 128]); bon = T([64, 128])
    r_ = mx4[:, 0, :]; k_ = mx4[:, 1, :]; v_ = mx4[:, 2, :]; wd_ = mx4[:, 3, :]
    pm_ = lambda j, n=64, k=64: pmats[0:k, j * 64:j * 64 + n]

    def proj(bank, grp, col0, m, xs, last=True):
        for c in range(8):
            S.op('pe', [f'xb{xs}', 'w'], [f'pb{bank}'],
                 lambda en, c=c: en.matmul(pb[bank][0:m, grp * 128:(grp + 1) * 128], lhsT=w_bf[:, c, col0:col0 + m], rhs=xb[xs][:, c, :], start=(c == 0), stop=(c == 7)),
                 sig=(c == 7))

    def dv(reads, writes, fn, e='dve'):
        S.op(e, reads, writes, fn)

    for n in range(NBLK):
        t0 = n * 128
        xs = n % 2
        os_ = ost[xs]
        def load_xb(nn):
            rk_, sl_ = (0, 0) if nn == 0 else ((nn - 1) // 16, (nn - 1) % 16 + 1)
            S.dma('sp', xb[nn % 2][:], xv[rk_, :, :, sl_ * 128:(sl_ + 1) * 128], ['xn_all'], [f'xb{nn % 2}'], f'xl{nn % 2}')
        if n == 0:
            load_xb(0)
        for g, nm in enumerate(('r', 'k', 'v', 'wd')):
            proj(0, g, CG[nm], 64, xs)
        proj(1, 0, CG['ad'], 64, xs)
        proj(1, 1, CG['gd'], 128, xs)
        if layer == 1:
            proj(1, 2, CG['vall'], 128, xs)
            proj(1, 3, CG['vall'] + 128, 128, xs)
        proj(2, 0, CG['xa'], 64, xs)
        proj(2, 1, CG['ya'], 64, xs)
        proj(2, 2, CG['f'], 33, xs)
        for g, nm in enumerate(('qA', 'kA', 'qB', 'kB')):
            proj(3, g, CG[nm], 64, xs)
        for c in range(8):
            S.op('pe', [f'xb{xs}', 'w'], ['pb2'],
                 lambda en, c=c: en.matmul(pb[2][:, 384:512], lhsT=xb[xs][:, c, :], rhs=w_bf[:, c, CG['vAB']:CG['vAB'] + 128], start=(c == 0), stop=(c == 7)), sig=(c == 7))
        if n + 1 < NBLK:
            load_xb(n + 1)
        S.op('act', ['pb0'], ['raw4'], lambda en: en.copy(out=raw4[:, :, 1:129], in_=pb[0][0:64, :].rearrange("p (g t) -> p g t", g=4)))
        ng = 4 if layer == 1 else 2
        S.op('act', ['pb1'], ['rawB'], lambda en: en.copy(out=rawB[:, 0:ng, 1:129], in_=pb[1][:, 0:ng * 128].rearrange("p (g t) -> p g t", g=ng)))
        S.op('act', ['pb2'], ['xa_raw'], lambda en: en.copy(out=xa_raw[:, 3:131], in_=pb[2][0:64, 0:128]))
        S.op('act', ['pb2'], ['ysb'], lambda en: en.copy(out=ysb[:], in_=pb[2][0:64, 128:256]))
        dv(['ysb'], ['gy'], lambda en: en.tensor_tensor(out=gy[:], in0=ysb[:], in1=ysb[:], op=ALU.mult))
        dv(['gy'], ['gy'], lambda en: en.tensor_scalar(out=gy[:], in0=gy[:], scalar1=0.044715, scalar2=1.0, op0=ALU.mult, op1=ALU.add))
        dv(['gy', 'ysb'], ['gy'], lambda en: en.tensor_tensor(out=gy[:], in0=gy[:], in1=ysb[:], op=ALU.mult))
        sigm(gy[:], 'gy', gy[:], ['gy'], scale=1.5957691216057308)
        dv(['gy', 'ysb'], ['gy'], lambda en: en.tensor_tensor(out=gy[:], in0=gy[:], in1=ysb[:], op=ALU.mult))
        S.op('act', ['pb2', 'dcol'], ['e1'], lambda en: en.activation(out=e1[:], in_=pb[2][0:33, 256:384], func=AF.Exp, bias=NFB, scale=-1.0))
        for h in range(2):
            dv(['pb3'], [f'qT{h}'], lambda en, h=h: en.tensor_scalar(out=qT[h][:], in0=pb[3][0:64, (2 * h) * 128:(2 * h + 1) * 128], scalar1=0.125, scalar2=None, op0=ALU.mult))
            S.op('act', ['pb3'], [f'KT{h}'], lambda en, h=h: en.copy(out=KT[h][:, t0:t0 + 128], in_=pb[3][0:64, (2 * h + 1) * 128:(2 * h + 2) * 128]))
            dv(['pb2'], [f'VA{h}'], lambda en, h=h: en.tensor_copy(out=VA[h][:, n, 0:64], in_=pb[2][:, 384 + h * 64:384 + (h + 1) * 64]))
            if n == 0:
                dv([f'VA{h}'], [f'VA{h}'], lambda en, h=h: en.memset(VA[h][0:112, 0, :], 0.0))
        def lru_gen():
            dv(['xa_raw', 'pcols'], ['u'], lambda en: en.tensor_scalar(out=u[:], in0=xa_raw[:, 0:128], scalar1=pc(0), scalar2=pc(4), op0=ALU.mult, op1=ALU.add))
            for k in (1, 2, 3):
                dv(['xa_raw', 'pcols', 'u'], ['u'], lambda en, k=k: en.scalar_tensor_tensor(out=u[:], in0=xa_raw[:, k:k + 128], scalar=pc(k), in1=u[:], op0=ALU.mult, op1=ALU.add))
            dv(['xa_raw', 'u'], ['xa_raw'], lambda en: en.tensor_copy(out=xa_raw[:, 0:3], in_=xa_raw[:, 128:131]))
            yield
            S.op('pe', ['u', 'pmats'], ['pb2'], lambda en: en.matmul(pb[2][0:64, 0:128], lhsT=pm_(0), rhs=u[:], start=True, stop=True))
            S.op('pe', ['u', 'pmats'], ['pb2'], lambda en: en.matmul(pb[2][0:64, 128:256], lhsT=pm_(1), rhs=u[:], start=True, stop=True))
            sigm(rg[:], 'rg', pb[2][0:64, 0:128], ['pb2'], nbias=NB5)
            sigm(ig[:], 'ig', pb[2][0:64, 128:256], ['pb2'], nbias=NB6)
            yield
            S.op('act', ['rg', 'dcol'], ['aa'], lambda en: en.activation(out=aa[:], in_=rg[:], func=AF.Exp, scale=CL))
            S.op('act', ['rg', 'dcol'], ['a2'], lambda en: en.activation(out=a2[:], in_=rg[:], func=AF.Exp, scale=CL2))
            yield
            dv(['a2'], ['a2'], lambda en: en.tensor_scalar(out=a2[:], in0=a2[:], scalar1=-1.0, scalar2=1.0, op0=ALU.mult, op1=ALU.add))
            dv(['a2'], ['a2'], lambda en: en.tensor_scalar(out=a2[:], in0=a2[:], scalar1=1e-30, scalar2=None, op0=ALU.max))
            S.op('act', ['a2'], ['a2'], lambda en: en.activation(out=a2[:], in_=a2[:], func=AF.Ln))
            S.op('act', ['a2'], ['a2'], lambda en: en.activation(out=a2[:], in_=a2[:], func=AF.Exp, scale=0.5))
            dv(['ig', 'u'], ['bbv'], lambda en: en.tensor_tensor(out=bbv[:], in0=ig[:], in1=u[:], op=ALU.mult))
            dv(['bbv', 'a2'], ['bbv'], lambda en: en.tensor_tensor(out=bbv[:], in0=bbv[:], in1=a2[:], op=ALU.mult))
            yield
            if n == 0:
                dv(['bbv'], ['bbv'], lambda en: en.memset(bbv[:, 0:112], 0.0))
            dv(['aa', 'bbv', 'hl'], ['hl'], lambda en: en.tensor_tensor_scan(out=hl[:, 1:129], data0=aa[:], data1=bbv[:], initial=hl[:, 0:1], op0=ALU.mult, op1=ALU.add))
            dv(['hl', 'gy'], [f'ost{xs}_0'], lambda en: en.tensor_tensor(out=os_[:, 0, :], in0=hl[:, 1:129], in1=gy[:], op=ALU.mult))
            dv(['hl'], ['hl'], lambda en: en.tensor_copy(out=hl[:, 0:1], in_=hl[:, 128:129]))

            yield

        def fox_gen():
            S.op('act', ['e1'], ['e1'], lambda en: en.activation(out=e1[:], in_=e1[:], func=AF.Ln, bias=1.0))
            dv(['e1', 'cb', 'ones'], ['cb'], lambda en: en.tensor_tensor_scan(out=cb[:, 1:129], data0=ones[0:33, :], data1=e1[:], initial=cb[:, 0:1], op0=ALU.mult, op1=ALU.subtract))
            for h, hr in ((0, 0), (1, 32)):
                S.dma('sp', ctm[:, n, h:h + 1], cb[hr:hr + 1, 1:129], ['cb'], ['ctm'], f'ctmd{h}')
                S.op('pe', ['cb', 'ones'], ['pb5'], lambda en, h=h, hr=hr: en.matmul(pb[5][:, 386 + h:387 + h], lhsT=ones[hr:hr + 1, 0:128], rhs=cb[hr:hr + 1, 128:129], start=True, stop=True))
            dv(['pb5'], ['rj'], lambda en: en.tensor_copy(out=rj[:, 0:2], in_=pb[5][:, 386:388]))
            dv(['cb'], ['cb'], lambda en: en.tensor_copy(out=cb[:, 0:1], in_=cb[:, 128:129]))
            for h in range(2):
                dv(['ctm', 'rj'], [f'Bn{h}'], lambda en, h=h: en.tensor_scalar(out=Bn[h][:, 0:n + 1], in0=ctm[:, 0:n + 1, h], scalar1=-1.0, scalar2=rj[:, h:h + 1], op0=ALU.mult, op1=ALU.add))
            yield
            for h in range(2):
                SB = (5, 7, 4)

                def s_mm(i):
                    bk = SB[i % 3]
                    S.op('pe', [f'KT{h}', f'qT{h}'], [f'pb{bk}'], lambda en, i=i, bk=bk: en.matmul(pb[bk][:, 0:128], lhsT=KT[h][:, i * 128:(i + 1) * 128], rhs=qT[h][:], start=True, stop=True))
                s_mm(0)
                if n >= 1:
                    s_mm(1)
                for i in range(n + 1):
                    if i + 2 <= n:
                        s_mm(i + 2)
                    bk = SB[i % 3]
                    pm = Pm[i % 3]
                    pk = f'Pm{i % 3}'
                    S.op('act', [f'pb{bk}', f'Bn{h}'], [pk], lambda en, i=i, bk=bk, pm=pm: en.activation(out=pm[:], in_=pb[bk][:, 0:128], func=AF.Exp, bias=Bn[h][:, i:i + 1]))
                    if i == n:
                        S.op('dve', [pk, 'causb'], [pk], lambda en, pm=pm: en.tensor_tensor(out=pm[:], in0=pm[:], in1=causb[:], op=ALU.mult))
                    S.op('pe', [f'VA{h}', pk], ['pb6'], lambda en, i=i, pm=pm: en.matmul(pb[6][0:65, 0:128], lhsT=VA[h][:, i, :], rhs=pm[:], start=(i == 0), stop=(i == n)))
                    yield
                S.op('act', ['pb6'], ['Osb'], lambda en: en.copy(out=Osb[:, h, :], in_=pb[6][0:65, 0:128]))
                S.op('pe', ['Osb', 'ones'], ['pb5'], lambda en: en.matmul(pb[5][0:64, 256:384], lhsT=ones[64:65, 0:64], rhs=Osb[64:65, h, :], start=True, stop=True))
                dv(['pb5'], ['rec'], lambda en: en.tensor_scalar(out=rec[:], in0=pb[5][0:64, 256:384], scalar1=1e-30, scalar2=None, op0=ALU.add))
                dv(['rec'], ['rec'], lambda en: en.reciprocal(out=rec[:], in_=rec[:]))
                dv(['rec', 'Osb'], [f'ost{xs}_{1 + h}'], lambda en: en.tensor_tensor(out=os_[:, 1 + h, :], in0=Osb[0:64, h, :], in1=rec[:], op=ALU.mult))
                yield

        def rwkv_gen():
            dv(['raw4'], ['d4'], lambda en: en.tensor_tensor(out=d4[:], in0=raw4[:, :, 0:128], in1=raw4[:, :, 1:129], op=ALU.subtract))
            dv(['d4', 'MU4'], ['d4'], lambda en: en.tensor_tensor(out=d4[:], in0=d4[:], in1=MU4[:], op=ALU.mult))
            dv(['d4', 'raw4'], ['mx4'], lambda en: en.tensor_tensor(out=mx4[:], in0=d4[:], in1=raw4[:, :, 1:129], op=ALU.add))
            dv(['raw4', 'mx4'], ['raw4'], lambda en: en.tensor_copy(out=raw4[:, :, 0:1], in_=raw4[:, :, 128:129]))
            dv(['rawB'], ['dB'], lambda en: en.tensor_tensor(out=dB[:, 0:ng], in0=rawB[:, 0:ng, 0:128], in1=rawB[:, 0:ng, 1:129], op=ALU.subtract))
            dv(['dB', 'MUB'], ['dB'], lambda en: en.tensor_tensor(out=dB[:, 0:ng], in0=dB[:, 0:ng], in1=MUB[:, 0:ng], op=ALU.mult))
            dv(['dB', 'rawB'], ['mxB'], lambda en: en.tensor_tensor(out=mxB[:, 0:ng], in0=dB[:, 0:ng], in1=rawB[:, 0:ng, 1:129], op=ALU.add))
            dv(['rawB', 'mxB'], ['rawB'], lambda en: en.tensor_copy(out=rawB[:, 0:ng, 0:1], in_=rawB[:, 0:ng, 128:129]))
            yield
            sigm(wdt[:], 'wdt', wd_, ['mx4'], scale=2.0)
            dv(['wdt'], ['wdt'], lambda en: en.tensor_scalar(out=wdt[:], in0=wdt[:], scalar1=2.0, scalar2=-1.0, op0=ALU.mult, op1=ALU.add))
            sigm(sg[:], 'sg', mxB[:, 1, :], ['mxB'])
            S.op('pe', ['wdt', 'pmats'], ['pb0'], lambda en: en.matmul(pb[0][0:64, 0:128], lhsT=pm_(2), rhs=wdt[:], start=True, stop=True))
            S.op('pe', ['mxB', 'pmats'], ['pb0'], lambda en: en.matmul(pb[0][0:64, 128:256], lhsT=pm_(3), rhs=mxB[0:64, 0, :], start=True, stop=True))
            S.op('pe', ['sg', 'pmats'], ['pb0'], lambda en: en.matmul(pb[0][0:64, 256:384], lhsT=pm_(4, 64, 128), rhs=sg[:], start=True, stop=True))
            dv(['mx4', 'pcols'], ['kkr'], lambda en: en.tensor_scalar(out=kkr[:], in0=k_, scalar1=pc(17), scalar2=None, op0=ALU.mult))
            dv(['kkr'], ['sq'], lambda en: en.tensor_tensor(out=sq[:], in0=kkr[:], in1=kkr[:], op=ALU.mult))
            S.op('pe', ['sq', 'ones'], ['pb0'], lambda en: en.matmul(pb[0][0:64, 384:512], lhsT=ones[0:64, 0:64], rhs=sq[:], start=True, stop=True))
            yield
            sigm(logw[:], 'logw', pb[0][0:64, 0:128], ['pb0'], nbias=NB15)
            dv(['logw'], ['logw'], lambda en: en.tensor_scalar(out=logw[:], in0=logw[:], scalar1=DECAY_C, scalar2=None, op0=ALU.mult))
            sigm(av[:], 'av', pb[0][0:64, 128:256], ['pb0'], nbias=NB16)
            S.op('act', ['pb0'], ['gg'], lambda en: en.copy(out=gg[:], in_=pb[0][0:64, 256:384]))
            dv(['pb0'], ['nrm'], lambda en: en.tensor_scalar(out=nrm[:], in0=pb[0][0:64, 384:512], scalar1=1e-24, scalar2=None, op0=ALU.max))
            S.op('act', ['nrm'], ['nrm'], lambda en: en.activation(out=nrm[:], in_=nrm[:], func=AF.Ln))
            S.op('act', ['nrm'], ['nrm'], lambda en: en.activation(out=nrm[:], in_=nrm[:], func=AF.Exp, scale=-0.5))
            dv(['kkr', 'nrm'], ['kk'], lambda en: en.tensor_tensor(out=kk[:], in0=kkr[:], in1=nrm[:], op=ALU.mult))
            dv(['av', 'pcols', 'dcol'], ['tmpk'], lambda en: en.tensor_scalar(out=tmpk[:], in0=av[:], scalar1=pc(18), scalar2=OMKA, op0=ALU.mult, op1=ALU.add))
            dv(['tmpk', 'mx4'], ['k2'], lambda en: en.tensor_tensor(out=k2[:], in0=k_, in1=tmpk[:], op=ALU.mult))
            dv(['kk', 'av'], ['bneg'], lambda en: en.scalar_tensor_tensor(out=bneg[:], in0=kk[:], scalar=-1.0, in1=av[:], op0=ALU.mult, op1=ALU.mult))
            yield
            if layer == 0:
                dv(['mx4'], ['v2'], lambda en: en.tensor_copy(out=v2[:], in_=v_))
                S.dma('sp', vown[:, t0:t0 + 128], v2[:], ['v2'], ['vf_dram'], 'vo')
            else:
                S.dma('sp', vf[:], vfirst[:, t0:t0 + 128], ['vf_dram'], ['vf'], 'vfl')
                for c in range(2):
                    S.op('pe', ['mxB', 'pmats'], ['pb1'], lambda en, c=c: en.matmul(pb[1][0:32, 0:128], lhsT=pmats[:, 320 + c * 32:352 + c * 32], rhs=mxB[:, 2 + c, :], start=(c == 0), stop=(c == 1)))
                S.op('act', ['pb1'], ['vd'], lambda en: en.copy(out=vd[:], in_=pb[1][0:32, 0:128]))
                S.op('pe', ['vd', 'pmats'], ['pb1'], lambda en: en.matmul(pb[1][0:64, 128:256], lhsT=pmats[0:32, 384:448], rhs=vd[:], start=True, stop=True))
                sigm(sv[:], 'sv', pb[1][0:64, 128:256], ['pb1'], nbias=NB24)
                dv(['vf', 'mx4'], ['vf'], lambda en: en.tensor_tensor(out=vf[:], in0=vf[:], in1=v_, op=ALU.subtract))
                dv(['vf', 'sv'], ['vf'], lambda en: en.tensor_tensor(out=vf[:], in0=vf[:], in1=sv[:], op=ALU.mult))
                dv(['vf', 'mx4'], ['v2'], lambda en: en.tensor_tensor(out=v2[:], in0=vf[:], in1=v_, op=ALU.add))
            yield
            dv(['logw', 'cmask'], ['L'], lambda en: en.tensor_tensor_scan(out=L[:], data0=RMASK, data1=logw[:], initial=0.0, op0=ALU.mult, op1=ALU.add))
            S.op('act', ['L'], ['eL'], lambda en: en.activation(out=eL[:], in_=L[:], func=AF.Exp))
            S.op('act', ['L'], ['eLn'], lambda en: en.activation(out=eLn[:], in_=L[:], func=AF.Exp, scale=-1.0))
            dv(['L', 'logw'], ['eLx'], lambda en: en.tensor_tensor(out=eLx[:], in0=L[:], in1=logw[:], op=ALU.subtract))
            S.op('act', ['eLx'], ['eLx'], lambda en: en.activation(out=eLx[:], in_=eLx[:], func=AF.Exp))
            for c in range(2):
                S.op('act', ['L'], ['eLC'], lambda en, c=c: en.activation(out=eLC[:, c * 64:(c + 1) * 64], in_=L[:, c * 64:(c + 1) * 64], func=AF.Exp, scale=-1.0, bias=L[:, c * 64 + 63:c * 64 + 64]))
            v3 = lambda ap: ap.rearrange("p (c t) -> p c t", c=2)
            dv(['kk', 'eLx'], ['kr'], lambda en: en.tensor_tensor(out=kr[:, :, 0, :], in0=v3(kk[:]), in1=v3(eLx[:]), op=ALU.mult))
            dv(['mx4', 'eL'], ['kr'], lambda en: en.tensor_tensor(out=kr[:, :, 1, :], in0=v3(r_), in1=v3(eL[:]), op=ALU.mult))
            dv(['k2', 'eLn'], ['kt'], lambda en: en.tensor_tensor(out=kt[:], in0=k2[:], in1=eLn[:], op=ALU.mult), e='pool')
            dv(['bneg', 'eLn'], ['btn'], lambda en: en.tensor_tensor(out=btn[:], in0=bneg[:], in1=eLn[:], op=ALU.mult), e='pool')
            dv(['k2', 'eLC'], ['khat'], lambda en: en.tensor_tensor(out=khat[:], in0=k2[:], in1=eLC[:], op=ALU.mult), e='pool')
            dv(['bneg', 'eLC'], ['bhn'], lambda en: en.tensor_tensor(out=bhn[:], in0=bneg[:], in1=eLC[:], op=ALU.mult), e='pool')
            yield
            for c in range(2):
                cs = slice(c * 64, (c + 1) * 64)
                tm, ABk, ABb, W, Pq = tmC[c], ABkC[c], ABbC[c], WC[c], PqC[c]
                kT, kAk, kAb, kW = f'tm{c}', f'ABk{c}', f'ABb{c}', f'W{c}'
                for j, (src, key) in enumerate(((v2, 'v2'), (khat, 'khat'), (bhn, 'bhn'))):
                    S.op('pe', [key, 'ident'], ['pb1'], lambda en, j=j, src=src: en.matmul(pb[1][0:64, j * 64:(j + 1) * 64], lhsT=src[:, cs], rhs=identf[0:64, 0:64], start=True, stop=True))
                S.op('act', ['pb1'], [kT], lambda en: en.copy(out=tm[:], in_=pb[1][0:64, 0:192].rearrange("p (j t) -> p j t", j=3)))
                krc = kr[:, c, :, :].rearrange("p a t -> p (a t)")
                S.op('pe', ['kt', 'kr'], ['pb2'], lambda en: en.matmul(pb[2][0:64, 0:128], lhsT=kt[:, cs], rhs=krc, start=True, stop=True))
                S.op('pe', ['btn', 'kr'], ['pb2'], lambda en: en.matmul(pb[2][0:64, 128:256], lhsT=btn[:, cs], rhs=krc, start=True, stop=True))
                S.op('pe', ['btn', 'kr'], ['pb2'], lambda en: en.matmul(pb[2][0:64, 256:320], lhsT=kr[:, c, 0, :], rhs=btn[:, cs], start=True, stop=True))
                dv(['pb2', 'cmask'], [kAk], lambda en: en.tensor_tensor(out=ABk[:], in0=pb[2][0:64, 0:128], in1=MASK2, op=ALU.mult))
                dv(['pb2', 'cmask'], [kAb], lambda en: en.tensor_tensor(out=ABb[:], in0=pb[2][0:64, 128:256], in1=MASK2, op=ALU.mult))
                dv([kAb], [f'Pq{c}_0'], lambda en: en.tensor_copy(out=Pq[0][:, 0, :], in_=ABb[:, 0:64]))
                dv(['pb2', 'cmask', f'Pq{c}_0'], [f'Pq{c}_0'], lambda en: en.tensor_tensor(out=Pq[0][:, 1, :], in0=pb[2][0:64, 256:320], in1=MASKL, op=ALU.mult))
                dv([kAb, 'ident'], [kW], lambda en: en.tensor_tensor(out=W[:], in0=ABb[:, 0:64], in1=identf[0:64, 0:64], op=ALU.add))
                yield
                for lvl in range(5):
                    cur, nxt = Pq[lvl % 2], Pq[(lvl + 1) % 2]
                    ck, nk = f'Pq{c}_{lvl % 2}', f'Pq{c}_{(lvl + 1) % 2}'
                    S.op('pe', [ck], ['pb3'], lambda en, cur=cur: en.matmul(pb[3][0:64, 0:64], lhsT=cur[:, 1, :], rhs=cur[:, 0, :], start=True, stop=True))
                    S.op('pe', [ck], ['pb3'], lambda en, cur=cur: en.matmul(pb[3][0:64, 64:128], lhsT=cur[:, 0, :], rhs=cur[:, 1, :], start=True, stop=True))
                    S.op('act', ['pb3'], [nk], lambda en, nxt=nxt: en.copy(out=nxt[:], in_=pb[3][0:64, 0:128].rearrange("p (a t) -> p a t", a=2)))
                    S.op('pe', [nk, kW], ['pb3'], lambda en, nxt=nxt: en.matmul(pb[3][0:64, 128:192], lhsT=nxt[:, 1, :], rhs=W[:], start=True, stop=True))
                    dv(['pb3', kW], [kW], lambda en: en.tensor_tensor(out=W[:], in0=pb[3][0:64, 128:192], in1=W[:], op=ALU.add))
                    yield
                pre_done[c] = True

        def rwkv_chain():
            for c in range(2):
                while not pre_done[c]:
                    yield
                cs = slice(c * 64, (c + 1) * 64)
                tm, ABk, ABb, W = tmC[c], ABkC[c], ABbC[c], WC[c]
                kT, kAk, kAb, kW = f'tm{c}', f'ABk{c}', f'ABb{c}', f'W{c}'
                S.op('pe', [kAk, kT], ['pb0'], lambda en: en.matmul(pb[0][0:64, 0:64], lhsT=ABk[:, 0:64], rhs=tm[:, 0, :], start=True, stop=False))
                S.op('pe', ['kr', 'H'], ['pb0'], lambda en: en.matmul(pb[0][0:64, 0:64], lhsT=kr[:, c, 0, :], rhs=H[:], start=False, stop=True))
                S.op('act', ['pb0'], ['Xsb'], lambda en: en.copy(out=Xsb[:], in_=pb[0][0:64, 0:64]))
                yield
                S.op('pe', [kW, 'Xsb'], ['pb0'], lambda en: en.matmul(pb[0][0:64, 64:128], lhsT=W[:], rhs=Xsb[:], start=True, stop=True))
                S.op('act', ['pb0'], ['Usb'], lambda en: en.copy(out=Usb[:], in_=pb[0][0:64, 64:128]))
                yield
                S.op('pe', ['H', 'kr'], ['pb0'], lambda en: en.matmul(pb[0][0:64, 192:256], lhsT=H[:], rhs=kr[:, c, 1, :], start=True, stop=False))
                S.op('pe', [kT, kAk], ['pb0'], lambda en: en.matmul(pb[0][0:64, 192:256], lhsT=tm[:, 0, :], rhs=ABk[:, 64:128], start=False, stop=False))
                S.op('pe', ['Usb', kAb], ['pb0'], lambda en: en.matmul(pb[0][0:64, 192:256], lhsT=Usb[:], rhs=ABb[:, 64:128], start=False, stop=True))
                S.op('act', ['pb0'], ['osb'], lambda en: en.copy(out=osb[:, cs], in_=pb[0][0:64, 192:256]))
                yield
                S.op('pe', [kT], ['pb0'], lambda en: en.matmul(pb[0][0:64, 128:192], lhsT=tm[:, 1, :], rhs=tm[:, 0, :], start=True, stop=False))
                S.op('pe', [kT, 'Usb'], ['pb0'], lambda en: en.matmul(pb[0][0:64, 128:192], lhsT=tm[:, 2, :], rhs=Usb[:], start=False, stop=True))
                dv(['pb0', 'H', 'eL'], ['H'], lambda en: en.scalar_tensor_tensor(out=H[:], in0=H[:], scalar=eL[:, c * 64 + 63:c * 64 + 64], in1=pb[0][0:64, 128:192], op0=ALU.mult, op1=ALU.add))
                yield
            yield
            S.op('pe', ['osb', 'ones'], ['pb1'], lambda en: en.matmul(pb[1][0:64, 0:128], lhsT=ones[0:64, 0:64], rhs=osb[:], start=True, stop=True))
            dv(['osb'], ['sq'], lambda en: en.tensor_tensor(out=sq[:], in0=osb[:], in1=osb[:], op=ALU.mult))
            S.op('pe', ['sq', 'ones'], ['pb1'], lambda en: en.matmul(pb[1][0:64, 128:256], lhsT=ones[0:64, 0:64], rhs=sq[:], start=True, stop=True))
            dv(['mx4', 'pcols', 'k2'], ['rk'], lambda en: en.scalar_tensor_tensor(out=rk[:], in0=r_, scalar=pc(19), in1=k2[:], op0=ALU.mult, op1=ALU.mult))
            S.op('pe', ['rk', 'ones'], ['pb1'], lambda en: en.matmul(pb[1][0:64, 256:384], lhsT=ones[0:64, 0:64], rhs=rk[:], start=True, stop=True))
            dv(['pb1'], ['mean'], lambda en: en.tensor_scalar(out=mean[:], in0=pb[1][0:64, 0:128], scalar1=1.0 / 64, scalar2=None, op0=ALU.mult))
            dv(['osb', 'mean'], ['cen'], lambda en: en.tensor_tensor(out=cen[:], in0=osb[:], in1=mean[:], op=ALU.subtract))
            dv(['mean'], ['msq'], lambda en: en.tensor_tensor(out=msq[:], in0=mean[:], in1=mean[:], op=ALU.mult))
            dv(['pb1', 'msq'], ['var'], lambda en: en.scalar_tensor_tensor(out=var[:], in0=pb[1][0:64, 128:256], scalar=1.0 / 64, in1=msq[:], op0=ALU.mult, op1=ALU.subtract))
            dv(['var'], ['var'], lambda en: en.tensor_scalar(out=var[:], in0=var[:], scalar1=GN_EPS, scalar2=None, op0=ALU.add))
            S.op('act', ['var'], ['var'], lambda en: en.activation(out=var[:], in_=var[:], func=AF.Ln))
            S.op('act', ['var'], ['var'], lambda en: en.activation(out=var[:], in_=var[:], func=AF.Exp, scale=-0.5))
            dv(['cen', 'var'], ['cen'], lambda en: en.tensor_tensor(out=cen[:], in0=cen[:], in1=var[:], op=ALU.mult))
            dv(['cen', 'pcols'], ['cen'], lambda en: en.tensor_scalar(out=cen[:], in0=cen[:], scalar1=pc(20), scalar2=pc(21), op0=ALU.mult, op1=ALU.add))
            dv(['pb1', 'v2'], ['bon'], lambda en: en.tensor_tensor(out=bon[:], in0=pb[1][0:64, 256:384], in1=v2[:], op=ALU.mult))
            dv(['cen', 'bon'], ['cen'], lambda en: en.tensor_tensor(out=cen[:], in0=cen[:], in1=bon[:], op=ALU.add))
            dv(['cen', 'gg'], [f'ost{xs}_3'], lambda en: en.tensor_tensor(out=os_[:, 3, :], in0=cen[:], in1=gg[:], op=ALU.mult))

        pre_done = [False, False]
        gfox, grw = fox_gen(), [lru_gen(), rwkv_gen(), rwkv_chain()]
        kfox = max(1, (2 * (n + 1) + 2) // 10)
        fox_alive = True
        while fox_alive or grw:
            if fox_alive:
                for _ in range(kfox):
                    try:
                        next(gfox)
                    except StopIteration:
                        fox_alive = False
                        break
            for g_ in list(grw):
                try:
                    next(g_)
                except StopIteration:
                    grw.remove(g_)
        S.dma('sp', ov[:, :, t0:t0 + 128], os_[:], [f'ost{xs}_{j}' for j in range(4)], ['cat_loc'], f'oo{xs}')


A_COLS, B_COLS = 512, 1544
PC0 = A_COLS + B_COLS


def consts():
    ident = np.eye(128, dtype=np.float32)
    cm = np.zeros((128, 512), np.float32)
    r = np.arange(128)[:, None]
    c = np.arange(128)[None, :]
    cm[:, 0:128] = (r <= c)
    r64 = np.arange(64)[:, None]
    c64 = np.arange(64)[None, :]
    cm[0:64, 128:192] = (r64 < c64)
    cm[0:64, 192:256] = (r64 <= c64)
    cm[0:64, 256:320] = (r64 > c64)
    cm[0:64, 320:448] = 1.0
    cm[0:64, 320] = 0.0
    cm[0:64, 384] = 0.0
    return ident, cm


def mix_inputs(l, g, I):
    w_in = I['w_in'][l]
    W = np.zeros((D, NCOLS), np.float32)
    hs = slice(64 * g, 64 * g + 64)
    W[:, CG['xa']:CG['xa'] + 64] = w_in[:, 0:256][:, hs]
    W[:, CG['ya']:CG['ya'] + 64] = w_in[:, 256:512][:, hs]
    pb = w_in[:, A_COLS:A_COLS + B_COLS]
    for nm, h in (('A', 2 * g), ('B', 2 * g + 1)):
        W[:, CG['q' + nm]:CG['q' + nm] + 64] = pb[:, 64 * h:64 * h + 64]
        W[:, CG['k' + nm]:CG['k' + nm] + 64] = pb[:, 512 + 64 * h:512 + 64 * h + 64]
    W[:, CG['vAB']:CG['vAB'] + 128] = pb[:, 1024 + 128 * g:1024 + 128 * g + 128]
    W[:, CG['f']] = pb[:, 1536 + 2 * g]
    W[:, CG['f'] + 32] = pb[:, 1536 + 2 * g + 1]
    pc = w_in[:, PC0:]
    W[:, CG['r']:CG['r'] + 64] = pc[:, 0:256][:, hs]
    W[:, CG['k']:CG['k'] + 64] = pc[:, 256:512][:, hs]
    W[:, CG['v']:CG['v'] + 64] = pc[:, 512:768][:, hs]
    W[:, CG['wd']:CG['wd'] + 64] = pc[:, 768:832]
    W[:, CG['ad']:CG['ad'] + 64] = pc[:, 832:896]
    W[:, CG['gd']:CG['gd'] + 128] = pc[:, 896:1024]
    W[:, CG['vall']:CG['vall'] + 256] = pc[:, 512:768]
    P = np.zeros((128, 32), np.float32)
    for k in range(4):
        P[0:64, k] = I['lru_conv_w'][l][k, hs]
    P[0:64, 4] = I['lru_conv_b'][l][hs]
    P[0:64, 5] = I['lru_gate_a_b'][l][hs]
    P[0:64, 6] = I['lru_gate_x_b'][l][hs]
    P[0:64, 7] = I['lru_lambda'][l][hs]
    P[0, 8] = I['fox_f_bias'][l][2 * g]
    P[32, 8] = I['fox_f_bias'][l][2 * g + 1]
    mu = I['rwkv_mu'][l]
    P[0:64, 9] = mu[0:256][hs]
    P[0:64, 10] = mu[256:512][hs]
    P[0:64, 11] = mu[512:768][hs]
    P[0:64, 12] = mu[768:832]
    P[0:64, 13] = mu[832:896]
    P[:, 14] = mu[896:1024]
    P[0:64, 15] = I['rwkv_w0'][l][hs]
    P[0:64, 16] = I['rwkv_a0'][l][hs]
    P[0:64, 17] = I['rwkv_k_k'][l][hs]
    P[0:64, 18] = I['rwkv_k_a'][l][hs]
    P[0:64, 19] = I['rwkv_r_k'][l][g]
    P[0:64, 20] = I['rwkv_ln_w'][l][hs]
    P[0:64, 21] = I['rwkv_ln_b'][l][hs]
    P[:, 22] = mu[512:640]
    P[:, 23] = mu[640:768]
    M = np.zeros((128, 448), np.float32)
    M[0:64, 0:64] = I['lru_gate_a_w'][l][g]
    M[0:64, 64:128] = I['lru_gate_x_w'][l][g]
    M[0:64, 128:192] = I['rwkv_w_up'][l][:, hs]
    M[0:64, 192:256] = I['rwkv_a_up'][l][:, hs]
    M[:, 256:320] = I['rwkv_g_up'][l][:, hs]
    if l >= 1:
        P[0:64, 24] = I['rwkv_v0'][l - 1][hs]
        vd = I['rwkv_v_down'][l - 1]
        M[:, 320:352] = vd[0:128]
        M[:, 352:384] = vd[128:256]
        M[0:32, 384:448] = I['rwkv_v_up'][l - 1][:, hs]
    return W, P, M


GROUPS = [[0, 1, 2, 3], [4, 5, 6, 7]]
LAYER_KEYS = ('w', 'gpre', 'pcols', 'pmats', 'w_out', 'gpost1', 'w_up', 'w_dn', 'conv', 'gpre2', 'gpost2')
LAYER_SHAPES = {'w': [D, NCOLS], 'gpre': [D], 'pcols': [128, 32], 'pmats': [128, 448], 'w_out': [D, D], 'gpost1': [D],
                'w_up': [D, 2 * DFF], 'w_dn': [DFF, D], 'conv': [3, 2 * DFF], 'gpre2': [D], 'gpost2': [D]}


def build_fused(parts='PGMCTUg'):
    C = Ctx("fused")
    S = C.S
    h0 = C.din("h0", [TOK, D])
    rowmask = C.din("rowmask", [128, 1])
    ident = C.din("ident", [128, 128])
    cmask = C.din("cmask", [128, 512])
    L = [{k: C.din(f"{k}_{l}", LAYER_SHAPES[k]) for k in LAYER_KEYS} for l in range(2)]
    out = C.dout("out", [16 * 128, D])
    xn_loc = C.dint("xn_loc", [D, TOK], BF16)
    xn_all = C.dint("xn_all", [4 * D, TOK], BF16)
    cat_loc = C.dint("cat_loc", [256, TP], BF16)
    cat_all = C.dint("cat_all", [D, TP], BF16)
    h_dram = C.dint("h_dram", [TOK, D], F32)
    xn2_dram = C.dint("xn2_dram", [D, TOK], BF16)
    vf_dram = C.dint("vf_dram", [64, TP], F32)
    base = {'ident': ident, 'rowmask': rowmask, 'cmask': cmask}
    C.qoff = C.nc.sync.snap((C.nc.sync.partition_id() % 4) * 2048)

    def gather(src, dst, key, rows):
        n = src.ap().shape[0] // rows
        for k in range(n):
            S.collective("AllGather", [src.ap()[k * rows:(k + 1) * rows, :].opt()], [dst.ap()[4 * rows * k:4 * rows * (k + 1), :].opt()], GROUPS, [key])

    if 'P' in parts:
      with C.phase("P"):
        emit_tok(C, 'P', dict(base, h_in=h0, xnT_out=xn_loc.ap()))
    if 'G' in parts:
        gather(xn_loc, xn_all, 'xn_all', 128)
    for l in range(2):
        if 'M' in parts:
          with C.phase(f"M{l}"):
            emit_mix(C, l, dict(base, xn_all=xn_all.ap(), w=L[l]['w'], gpre=L[l]['gpre'], pcols=L[l]['pcols'], pmats=L[l]['pmats'],
                                cat_loc=cat_loc.ap(), vf_dram=vf_dram.ap()))
        if 'C' in parts:
            gather(cat_loc, cat_all, 'cat_all', 32)
        if 'T' in parts:
          with C.phase(f"T1_{l}"):
            emit_tok(C, 'T1', dict(base, h_in=(h0 if l == 0 else h_dram.ap()), cat_all=cat_all.ap(), w_out=L[l]['w_out'], gpost=L[l]['gpost1'],
                                   h_out=h_dram.ap(), xnT_out=xn2_dram.ap()))
        if 'U' in parts:
          with C.phase(f"T2_{l}"):
            io = dict(base, h_in=h_dram.ap(), xnT_in=xn2_dram.ap(), w_up=L[l]['w_up'], w_dn=L[l]['w_dn'], conv=L[l]['conv'], gpre=L[l]['gpre2'],
                      gpost=L[l]['gpost2'])
            if l == 0:
                io.update(h_out=h_dram.ap(), xnT_out=xn_loc.ap())
            else:
                io.update(out_final=out)
            emit_tok(C, 'T2', io)
        if l == 0 and 'g' in parts:
            gather(xn_loc, xn_all, 'xn_all', 128)
    S.wait_collectives()
    S.finish()
    C.es.close()
    print("fused build: inst", S.n_inst, "waits", S.n_wait, flush=True)
    return C.nc


def cat_perm():
    loc = []
    for g in range(4):
        loc.append(list(range(64 * g, 64 * g + 64)) + list(range(256 + 128 * g, 256 + 128 * g + 128)) + list(range(768 + 64 * g, 768 + 64 * g + 64)))
    p = []
    for k in range(8):
        for r in range(4):
            p += loc[r][k * 32:(k + 1) * 32]
    return np.array(p)


def kernel(**I):
    I = {k: np.asarray(v) for k, v in I.items()}
    ident, cm = consts()
    perm = cat_perm()
    in_maps = []
    for core in range(8):
        b, q = core // 4, core % 4
        x = I['x'][b]
        h0 = np.zeros((TOK, D), np.float32)
        if q == 0:
            h0[112:128] = I['meta_tokens']
        else:
            h0[0:128] = x[(16 * q - 1) * 128:16 * q * 128]
        h0[128:] = x[16 * q * 128:(16 * q + 16) * 128]
        rm = np.ones((128, 1), np.float32)
        if q == 0:
            rm[:112] = 0.0
        d = {"h0": h0, "rowmask": rm, "ident": ident, "cmask": cm}
        for l in range(2):
            W, P, M = mix_inputs(l, q, I)
            vals = {'w': W, 'gpre': I['norm_mix_pre'][l], 'pcols': P, 'pmats': M, 'w_out': np.ascontiguousarray(I['w_out'][l][perm]),
                    'gpost1': I['norm_mix_post'][l], 'w_up': I['ffn_up'][l], 'w_dn': I['ffn_down'][l], 'conv': I['ffn_conv'][l],
                    'gpre2': I['norm_ffn_pre'][l], 'gpost2': I['norm_ffn_post'][l]}
            for k, v in vals.items():
                d[f"{k}_{l}"] = np.ascontiguousarray(v, dtype=np.float32)
        in_maps.append(d)
    nc = build_fused()
    res = run_bass_kernel_spmd(nc, in_maps, core_ids=list(range(8))).results
    out = np.zeros((2, 8192, D), np.float32)
    for core in range(8):
        b, q = core // 4, core % 4
        out[b, 2048 * q:2048 * q + 2048] = res[core]["out"]
    return out
```
